# Optimizing a Trainium2 kernel written in Bass

```python
import math
import jax, jax.numpy as jnp
from jax import lax
import numpy as np

D_MODEL = 2048
BATCH = 1
SEQ = 8192
DEPTH = 4

HEAD_DIM = 64
A_HEADS = 12
A_KV_HEADS = 4
WINDOW = 128
A_BLOCK = 128
B_HEADS = 4
B_DK = 128
B_DV = 128
B_CHUNK = 16
C_HEADS = 12
C_KV_HEADS = 4
C_BLOCK = 128
ROPE_THETA = 10000.0
GRID_W = 64
REL_BUCKETS = 32
REL_MAX_DIST = 128
D_FF = 5632
EPS = 1e-6
NEG = -1e30

A_Q = A_HEADS * HEAD_DIM
A_KV = A_KV_HEADS * HEAD_DIM
B_W = B_HEADS * B_DK
B_VW = B_HEADS * B_DV
C_Q = C_HEADS * HEAD_DIM
C_KV = C_KV_HEADS * HEAD_DIM
IN_SPLITS = [A_Q, A_KV, A_KV, B_W, B_W, B_W, B_VW, B_VW, C_Q, C_KV, C_KV]
D_IN = sum(IN_SPLITS)
D_MIX = A_Q + B_VW + C_Q

kernel_name = "hymba_style_bidir_hybrid_encoder"


def rms_norm(x, w):
    xf = x.astype(jnp.float32)
    y = xf * lax.rsqrt(jnp.mean(xf * xf, axis=-1, keepdims=True) + EPS)
    return (y * w.astype(jnp.float32)).astype(x.dtype)


def swiglu(x, wg, wu, wd):
    return (jax.nn.silu(x @ wg) * (x @ wu)) @ wd


def t5_bucket(rel):
    nb = REL_BUCKETS // 2
    max_exact = nb // 2
    n = jnp.abs(rel)
    nf = jnp.maximum(n, 1).astype(jnp.float32)
    large = max_exact + (jnp.log(nf / max_exact) / math.log(REL_MAX_DIST / max_exact)
                         * (nb - max_exact)).astype(jnp.int32)
    large = jnp.minimum(large, nb - 1)
    return jnp.where(rel > 0, nb, 0) + jnp.where(n < max_exact, n, large)


def axial_rope_tables(L):
    rows = L // GRID_W
    half = HEAD_DIM // 2
    inv = 1.0 / (ROPE_THETA ** (jnp.arange(0, half, 2, dtype=jnp.float32) / half))
    nf = inv.shape[0]
    row_ang = jnp.arange(rows, dtype=jnp.float32)[:, None] * inv
    col_ang = jnp.arange(GRID_W, dtype=jnp.float32)[:, None] * inv
    ang = jnp.concatenate([jnp.broadcast_to(row_ang[:, None, :], (rows, GRID_W, nf)),
                           jnp.broadcast_to(col_ang[None, :, :], (rows, GRID_W, nf))], axis=-1)
    ang = ang.reshape(L, 2 * nf)
    return jnp.cos(ang), jnp.sin(ang)


def apply_rope(x, cos, sin):
    xf = x.astype(jnp.float32).reshape(*x.shape[:-1], HEAD_DIM // 2, 2)
    x0, x1 = xf[..., 0], xf[..., 1]
    c = cos[None, :, None, :]
    s = sin[None, :, None, :]
    out = jnp.stack([x0 * c - x1 * s, x0 * s + x1 * c], axis=-1).reshape(x.shape)
    return out.astype(x.dtype)


def band_windows(t):
    B_, L = t.shape[0], t.shape[1]
    nb = L // A_BLOCK
    tp = jnp.pad(t, ((0, 0), (A_BLOCK, A_BLOCK), (0, 0), (0, 0)))
    tp = tp.reshape(B_, nb + 2, A_BLOCK, *t.shape[2:])
    return jnp.concatenate([tp[:, :-2], tp[:, 1:-1], tp[:, 2:]], axis=2)


def windowed_attention(q, k, v, sink, bias, valid):
    B_, L, _ = q.shape
    nb = L // A_BLOCK
    G = A_HEADS // A_KV_HEADS
    qb = q.reshape(B_, nb, A_BLOCK, A_KV_HEADS, G, HEAD_DIM)
    kw = band_windows(k.reshape(B_, L, A_KV_HEADS, HEAD_DIM))
    vw = band_windows(v.reshape(B_, L, A_KV_HEADS, HEAD_DIM))
    s = jnp.einsum('bnqkgd,bnskd->bnkgqs', qb, kw,
                   preferred_element_type=jnp.float32) * (HEAD_DIM ** -0.5)
    s = jnp.where(valid[None, :, None, None], s + bias, NEG)
    sink_col = jnp.broadcast_to(sink.astype(jnp.float32).reshape(A_KV_HEADS, G, 1, 1),
                                s.shape[:-1] + (1,))
    p = jax.nn.softmax(jnp.concatenate([s, sink_col], axis=-1), axis=-1)[..., :-1]
    o = jnp.einsum('bnkgqs,bnskd->bnqkgd', p.astype(v.dtype), vw)
    return o.reshape(B_, L, A_Q)


def chunked_linear_recurrence(q, k, v, log_f):
    B_, L, H, _ = q.shape
    n = L // B_CHUNK

    def chunks(a):
        return a.astype(jnp.float32).reshape(B_, n, B_CHUNK, H, a.shape[-1]).transpose(1, 0, 3, 2, 4)

    qc, kc, vc, gc = chunks(q), chunks(k), chunks(v), chunks(log_f)
    b = jnp.cumsum(gc, axis=-2)
    tri = jnp.tril(jnp.ones((B_CHUNK, B_CHUNK), dtype=bool))
    diff = jnp.where(tri[:, :, None], b[..., :, None, :] - b[..., None, :, :], -jnp.inf)
    att = jnp.einsum('nbhtc,nbhsc,nbhtsc->nbhts', qc, kc, jnp.exp(diff))
    intra = jnp.einsum('nbhts,nbhsv->nbhtv', att, vc)
    b_last = b[..., -1, :]
    chunk_kv = jnp.einsum('nbhsc,nbhsv->nbhcv', kc * jnp.exp(b_last[..., None, :] - b), vc)

    def step(S, inp):
        decay, kv = inp
        return decay[..., None] * S + kv, S

    S0 = jnp.zeros((B_, H, q.shape[-1], v.shape[-1]), jnp.float32)
    _, S_prev = lax.scan(step, S0, (jnp.exp(b_last), chunk_kv))
    inter = jnp.einsum('nbhtc,nbhcv->nbhtv', qc * jnp.exp(b), S_prev)
    return (intra + inter).transpose(1, 0, 3, 2, 4).reshape(B_, L, H, v.shape[-1])


def hgrn2_bidir(q_raw, zf, zb, i_raw, g_raw, lb, gnorm_w):
    B_, L, _ = q_raw.shape
    q = jax.nn.silu(q_raw).reshape(B_, L, B_HEADS, B_DK)
    v = i_raw.reshape(B_, L, B_HEADS, B_DV)

    def gates(z, lbd):
        z = z.astype(jnp.float32)
        lbd = lbd.astype(jnp.float32)
        log_f = jnp.logaddexp(jnp.log(lbd), jnp.log1p(-lbd) + jax.nn.log_sigmoid(z))
        kk = (1.0 - lbd) * jax.nn.sigmoid(-z)
        return (kk.reshape(B_, L, B_HEADS, B_DK), log_f.reshape(B_, L, B_HEADS, B_DK))

    kf, gf = gates(zf, lb[0])
    kb, gb = gates(zb, lb[1])
    o_fwd = chunked_linear_recurrence(q, kf, v, gf)
    flip = lambda a: jnp.flip(a, axis=1)
    o_bwd = flip(chunked_linear_recurrence(flip(q), flip(kb), flip(v), flip(gb)))
    o = rms_norm(o_fwd + o_bwd, gnorm_w) * jax.nn.silu(
        g_raw.astype(jnp.float32).reshape(B_, L, B_HEADS, B_DV))
    return o.reshape(B_, L, B_VW).astype(q_raw.dtype)


def axial_attention(q, k, v, qk_w, cos, sin):
    B_, L, _ = q.shape
    nb = L // C_BLOCK
    G = C_HEADS // C_KV_HEADS
    q = apply_rope(rms_norm(q.reshape(B_, L, C_HEADS, HEAD_DIM), qk_w[0]), cos, sin)
    k = apply_rope(rms_norm(k.reshape(B_, L, C_KV_HEADS, HEAD_DIM), qk_w[1]), cos, sin)
    v = v.reshape(B_, L, C_KV_HEADS, HEAD_DIM)
    qb = q.reshape(B_, nb, C_BLOCK, C_KV_HEADS, G, HEAD_DIM).transpose(1, 0, 2, 3, 4, 5)

    def block(qblk):
        s = jnp.einsum('bqkgd,bskd->bkgqs', qblk, k,
                       preferred_element_type=jnp.float32) * (HEAD_DIM ** -0.5)
        p = jax.nn.softmax(s, axis=-1)
        return jnp.einsum('bkgqs,bskd->bqkgd', p.astype(v.dtype), v)

    o = lax.map(block, qb)
    return o.transpose(1, 0, 2, 3, 4, 5).reshape(B_, L, C_Q)


def token_mix(h, w_in, w_out, sink, qk_w, lb, gnorm_w, bias, valid, cos, sin):
    proj = h @ w_in
    cuts = [int(c) for c in np.cumsum(IN_SPLITS)[:-1]]
    aq, ak, av, bq, bzf, bzb, bi, bg, cq, ck, cv = jnp.split(proj, cuts, axis=-1)
    ya = windowed_attention(aq, ak, av, sink, bias, valid)
    yb = hgrn2_bidir(bq, bzf, bzb, bi, bg, lb, gnorm_w)
    yc = axial_attention(cq, ck, cv, qk_w, cos, sin)
    return jnp.concatenate([ya, yb, yc], axis=-1) @ w_out


def setup_inputs(seed: int = 0) -> dict:
    key = jax.random.key(seed)
    ks = jax.random.split(key, 16)
    f32 = jnp.float32

    def nrm(k, shape, scale):
        return jax.random.normal(k, shape, f32) * scale

    return {
        "x": nrm(ks[0], (BATCH, SEQ, D_MODEL), 1.0),
        "w_in": nrm(ks[1], (DEPTH, D_MODEL, D_IN), D_MODEL ** -0.5),
        "w_out": nrm(ks[2], (DEPTH, D_MIX, D_MODEL), D_MIX ** -0.5),
        "ffn1_gate": nrm(ks[3], (DEPTH, D_MODEL, D_FF), D_MODEL ** -0.5),
        "ffn1_up": nrm(ks[4], (DEPTH, D_MODEL, D_FF), D_MODEL ** -0.5),
        "ffn1_down": nrm(ks[5], (DEPTH, D_FF, D_MODEL), D_FF ** -0.5),
        "ffn2_gate": nrm(ks[6], (DEPTH, D_MODEL, D_FF), D_MODEL ** -0.5),
        "ffn2_up": nrm(ks[7], (DEPTH, D_MODEL, D_FF), D_MODEL ** -0.5),
        "ffn2_down": nrm(ks[8], (DEPTH, D_FF, D_MODEL), D_FF ** -0.5),
        "norm_w": 1.0 + nrm(ks[9], (DEPTH, 6, D_MODEL), 0.02),
        "sink_logits": nrm(ks[10], (DEPTH, A_HEADS), 0.5),
        "qk_norm_w": 1.0 + nrm(ks[11], (DEPTH, 2, HEAD_DIM), 0.02),
        "hgrn_lb": nrm(ks[12], (DEPTH, 2, B_W), 0.1),
        "hgrn_norm_w": 1.0 + nrm(ks[13], (DEPTH, B_DV), 0.02),
        "rel_bias": nrm(ks[14], (REL_BUCKETS, A_HEADS), 0.1),
    }


def reference(x, w_in, w_out, ffn1_gate, ffn1_up, ffn1_down, ffn2_gate, ffn2_up, ffn2_down,
              norm_w, sink_logits, qk_norm_w, hgrn_lb, hgrn_norm_w, rel_bias):
    B_, L, _ = x.shape
    nb = L // A_BLOCK
    G = A_HEADS // A_KV_HEADS
    qi = jnp.arange(A_BLOCK)[:, None]
    sj = jnp.arange(3 * A_BLOCK)[None, :]
    rel = sj - A_BLOCK - qi
    band = jnp.abs(rel) <= WINDOW
    key_abs = (jnp.arange(nb)[:, None] - 1) * A_BLOCK + jnp.arange(3 * A_BLOCK)[None, :]
    in_range = (key_abs >= 0) & (key_abs < L)
    valid = band[None] & in_range[:, None, :]
    bias = rel_bias.astype(jnp.float32)[t5_bucket(rel)]
    bias = bias.transpose(2, 0, 1).reshape(A_KV_HEADS, G, A_BLOCK, 3 * A_BLOCK)
    cos, sin = axial_rope_tables(L)
    lb_c = jnp.cumsum(jax.nn.softmax(hgrn_lb.astype(jnp.float32), axis=0), axis=0)
    lbs = lb_c - lb_c[0:1]
    for l in range(DEPTH):
        h = rms_norm(x, norm_w[l, 0])
        x = x + 0.5 * rms_norm(swiglu(h, ffn1_gate[l], ffn1_up[l], ffn1_down[l]), norm_w[l, 1])
        h = rms_norm(x, norm_w[l, 2])
        y = token_mix(h, w_in[l], w_out[l], sink_logits[l], qk_norm_w[l], lbs[l],
                      hgrn_norm_w[l], bias, valid, cos, sin)
        x = x + rms_norm(y, norm_w[l, 3])
        h = rms_norm(x, norm_w[l, 4])
        x = x + 0.5 * rms_norm(swiglu(h, ffn2_gate[l], ffn2_up[l], ffn2_down[l]), norm_w[l, 5])
    return x
```

```python
import contextlib
import os
import numpy as np
import concourse.bass as bass
import concourse.mybir as mybir
from concourse.bass_utils import run_bass_kernel_spmd

F32 = mybir.dt.float32
BF16 = mybir.dt.bfloat16
AF = mybir.ActivationFunctionType
ALU = mybir.AluOpType

NCORES = 8
D = 2048
SEQ = 8192
NT = SEQ // NCORES
DC = D // 128
DFF = 5632
FC = DFF // 128
EPS = 1e-6
DEPTH = 4


class Prog:
    COMPUTE = ("pe", "act", "dve", "pool")
    DMAQ = ("sp", "pool", "act")
    RING = 6

    def __init__(self, nc):
        self.nc = nc
        self.ops = []

    def op(self, eng, fn, reads=(), writes=()):
        self.ops.append(dict(kind="c", eng=eng, fn=fn, reads=tuple(reads), writes=tuple(writes)))

    def dma(self, q, fn, reads=(), writes=()):
        self.ops.append(dict(kind="d", eng=q, fn=fn, reads=tuple(reads), writes=tuple(writes)))

    def barrier(self):
        self.ops.append(dict(kind="b"))

    def emit(self, stack):
        nc = self.nc
        ops = self.ops
        last_w = {}
        readers = {}
        seq = {e: 0 for e in self.COMPUTE}
        dseq = {q: 0 for q in self.DMAQ}
        last_c = {}
        last_d = {q: [] for q in self.DMAQ}
        pending = {}
        for i, o in enumerate(ops):
            if o["kind"] == "b":
                bd = set(last_c.values())
                for q in self.DMAQ:
                    bd.update(last_d[q][-self.RING:])
                for st in ("pe", "act", "dve", "pool", "sp"):
                    pending[st] = set(bd)
                last_w, readers = {}, {}
                o["deps"] = set()
                continue
            deps = set()
            if o["eng"] in pending:
                deps.update(pending.pop(o["eng"]))
            if o["kind"] == "c":
                last_c[o["eng"]] = i
            else:
                last_d[o["eng"]].append(i)
            for r in o["reads"]:
                if r in last_w:
                    deps.add(last_w[r])
            for w in o["writes"]:
                if w in last_w:
                    deps.add(last_w[w])
                for rd in readers.get(w, ()):
                    deps.add(rd)
            deps.discard(i)
            best = {}
            red = set()
            for d_ in deps:
                od = ops[d_]
                if od["kind"] == "c":
                    if od["eng"] not in best or best[od["eng"]] < d_:
                        best[od["eng"]] = d_
                else:
                    red.add(d_)
            red.update(best.values())
            o["deps"] = red
            for w in o["writes"]:
                last_w[w] = i
                readers[w] = []
            for r in o["reads"]:
                if r not in o["writes"]:
                    readers.setdefault(r, []).append(i)
            if o["kind"] == "c":
                o["seq"] = seq[o["eng"]]
                seq[o["eng"]] += 1
            else:
                o["dseq"] = dseq[o["eng"]]
                dseq[o["eng"]] += 1
            o["signal"] = False
        ops_all = ops
        for i, o in enumerate(ops):
            if o["kind"] == "b":
                continue
            for d in o["deps"]:
                od = ops[d]
                if od["kind"] == "c":
                    if od["eng"] == o["eng"] and o["kind"] == "c" and od["eng"] == "pe":
                        continue
                    od["signal"] = True
        cnt = {e: 0 for e in self.COMPUTE}
        for o in ops:
            if o["kind"] == "b":
                continue
            if o["kind"] == "c":
                if o["signal"]:
                    cnt[o["eng"]] += 1
                    o["cnt"] = cnt[o["eng"]]
        csem = {e: stack.enter_context(nc.semaphore("s_" + e)) for e in self.COMPUTE}
        dsem = {q: [stack.enter_context(nc.semaphore("d_%s%d" % (q, k))) for k in range(self.RING)]
                for q in self.DMAQ if dseq[q] > 0}
        streams = {e: [] for e in ("pe", "act", "dve", "pool", "sp")}
        known = {e: {} for e in streams}

        def need(stream, sem, val, lst):
            k = known[stream]
            if k.get(sem.name if hasattr(sem, "name") else id(sem), 0) >= val:
                return
            k[sem.name if hasattr(sem, "name") else id(sem)] = val
            lst.append((sem, val))

        for i, o in enumerate(ops):
            if o["kind"] == "b":
                continue
            st = o["eng"]
            waits = []
            for d in sorted(o["deps"]):
                od = ops[d]
                if od["kind"] == "c":
                    if not od["signal"]:
                        continue
                    if od["eng"] == st and o["kind"] == "c" and st == "pe":
                        continue
                    need(st, csem[od["eng"]], od["cnt"], waits)
                else:
                    q = od["eng"]
                    j = od["dseq"]
                    need(st, dsem[q][j % self.RING], 16 * (j // self.RING + 1), waits)
            inc = None
            if o["kind"] == "d":
                j = o["dseq"]
                if j >= self.RING:
                    need(st, dsem[st][j % self.RING], 16 * (j // self.RING), waits)
                inc = (dsem[st][j % self.RING], 16)
            elif o["signal"]:
                inc = (csem[st], 1)
            streams[st].append((waits, o["fn"], inc))
        fin = []
        for q in dsem:
            n = dseq[q]
            for k in range(self.RING):
                m = (n - k + self.RING - 1) // self.RING if n > k else 0
                if m > 0:
                    need("sp", dsem[q][k], 16 * m, fin)
        streams["sp"].append((fin, None, None))

        self.stats = {e: len(v) for e, v in streams.items()}
        self.stats["signals"] = dict(cnt)

        def run_stream(eng, lst):
            for waits, fn, inc in lst:
                for sem, val in waits:
                    eng.wait_ge(sem, val)
                if fn is None:
                    continue
                ins = fn(eng)
                if inc is not None:
                    ins.then_inc(inc[0], inc[1])

        with nc.Block() as block:
            @block.tensor
            def _(e):
                run_stream(e, streams["pe"])

            @block.scalar
            def _(e):
                run_stream(e, streams["act"])

            @block.vector
            def _(e):
                run_stream(e, streams["dve"])

            @block.gpsimd
            def _(e):
                run_stream(e, streams["pool"])

            @block.sync
            def _(e):
                run_stream(e, streams["sp"])


class Ctx:
    pass


def alloc_common(nc, stack, P):
    C = Ctx()
    C.nc, C.P = nc, P
    sb = lambda name, shape, dt: stack.enter_context(nc.sbuf_tensor(name, shape, dt))
    C.sb = sb
    C.xT = sb("xT", [128, DC, NT], F32)
    C.arH = sb("arH", [128, DC * NT // 2], F32)
    C.arB = sb("arB", [128, FC * NT // 2], F32)
    C.arW = sb("arW", [128, 4096], F32)
    C.hT = C.arH[:, :].bitcast(BF16).rearrange("p (c t) -> p c t", c=DC)
    C.ones = sb("ones", [128, 128], F32)
    C.psum = [stack.enter_context(nc.psum_tensor("ps%d" % i, [128, 512], F32)) for i in range(8)]
    C.sq = [sb("sq%d" % i, [128, 512], F32) for i in range(2)]
    C.rstd = sb("rstd", [128, 512], F32)
    P.op("pool", lambda e: e.memset(C.ones[:, :], 1.0), writes=["ones"])
    C.ctr = {}
    return C


def rr(C, name, n):
    v = C.ctr.get(name, 0)
    C.ctr[name] = v + 1
    return v % n


def emit_rms_stats(C, src_fn, src_keys, half, pbank, post_scale=None):
    P = C.P
    ps = C.psum[pbank]
    for c in range(DC):
        s = rr(C, "sq", 2)
        sq = C.sq[s]
        P.op("act", lambda e, sq=sq, c=c: e.activation(out=sq[:, :], in_=src_fn(c), func=AF.Square),
             reads=list(src_keys(c)), writes=["sq%d" % s])
        P.op("pe", lambda e, sq=sq, c=c: e.matmul(ps[:, :], lhsT=C.ones[:, :], rhs=sq[:, :],
                                                  start=(c == 0), stop=(c == DC - 1)),
             reads=["sq%d" % s, "ones"], writes=["ps%d" % pbank])
    P.op("dve", lambda e: e.tensor_scalar(out=C.rstd[:, :], in0=ps[:, :], scalar1=1.0 / D, scalar2=EPS,
                                          op0=ALU.mult, op1=ALU.add),
         reads=["ps%d" % pbank], writes=["rstd"])
    P.op("act", lambda e: e.activation(out=C.rstd[:, :], in_=C.rstd[:, :], func=AF.Sqrt),
         reads=["rstd"], writes=["rstd"])
    P.op("dve", lambda e: e.reciprocal(out=C.rstd[:, :], in_=C.rstd[:, :]),
         reads=["rstd"], writes=["rstd"])
    if post_scale is not None:
        P.op("dve", lambda e: e.tensor_scalar(out=C.rstd[:, :], in0=C.rstd[:, :], scalar1=post_scale, scalar2=None,
                                              op0=ALU.mult),
             reads=["rstd"], writes=["rstd"])


def emit_prenorm(C, nw, nwkey):
    P = C.P
    for half in range(2):
        ts = slice(half * 512, (half + 1) * 512)
        emit_rms_stats(C, lambda c, ts=ts: C.xT[:, c, ts], lambda c, half=half: [("xT", c, half)], half, 7)
        for c in range(DC):
            P.op("dve", lambda e, c=c, ts=ts: e.scalar_tensor_tensor(
                out=C.hT[:, c, ts], in0=C.xT[:, c, ts], scalar=nw[:, c:c + 1], in1=C.rstd[:, :],
                op0=ALU.mult, op1=ALU.mult),
                reads=[("xT", c, half), "rstd", nwkey], writes=[("H", c, half)])


def emit_ffn(C, wgu, wd, nw_pre, nw_post, nwkey, F):
    P, nc = C.P, C.nc
    emit_prenorm(C, nw_pre, nwkey)
    for j in range(FC):
        s = rr(C, "wgu", 2)
        wt = F.wgu[s]
        P.dma("pool", lambda e, wt=wt, j=j: e.dma_start(out=wt[:, :, :, :], in_=wgu[j]),
              writes=[("wgu", s)] + ([("wd", i) for i in range(5)] if j < 2 else []))
        pb = (j % 2) * 4
        for c in range(DC):
            for g in range(2):
                for half in range(2):
                    ts = slice(half * 512, (half + 1) * 512)
                    b = pb + g * 2 + half
                    P.op("pe", lambda e, wt=wt, c=c, g=g, ts=ts, b=b: e.matmul(
                        C.psum[b][:, :], lhsT=wt[:, g, c, :], rhs=C.hT[:, c, ts],
                        start=(c == 0), stop=(c == DC - 1)),
                        reads=[("wgu", s), ("H", c, half)], writes=["ps%d" % b])
        for half in range(2):
            ts = slice(half * 512, (half + 1) * 512)
            bg, bu = pb + half, pb + 2 + half
            k = rr(C, "sq", 2)
            sil = F.sil[k]
            P.op("act", lambda e, sil=sil, bg=bg: e.activation(out=sil[:, :], in_=C.psum[bg][:, :], func=AF.Silu),
                 reads=["ps%d" % bg], writes=["sq%d" % k])
            P.op("dve", lambda e, sil=sil, bu=bu, j=j, ts=ts: e.tensor_tensor(
                out=F.hid[:, j, ts], in0=C.psum[bu][:, :], in1=sil[:, :], op=ALU.mult),
                reads=["ps%d" % bu, "sq%d" % k], writes=[("hid", j, half)])
    KH = FC // 4
    for half in range(2):
        ts = slice(half * 512, (half + 1) * 512)
        for m in range(DC):
            b = m % 2
            for kh in range(4):
                s = rr(C, "wd", 5)
                wt = F.wd[s]
                P.dma("pool", lambda e, wt=wt, m=m, kh=kh: e.dma_start(
                    out=wt[:, :, :], in_=wd[m, :, kh * KH:(kh + 1) * KH, :]),
                    writes=[("wd", s)] + ([("wgu", 0), ("wgu", 1)] if (half == 0 and m < 2) else []))
                for kk in range(KH):
                    k = kh * KH + kk
                    P.op("pe", lambda e, wt=wt, kk=kk, k=k, b=b, ts=ts: e.matmul(
                        C.psum[b][:, :], lhsT=wt[:, kk, :], rhs=F.hid[:, k, ts],
                        start=(k == 0), stop=(k == FC - 1)),
                        reads=[("wd", s), ("hid", k, half)], writes=["ps%d" % b])
            P.op("act", lambda e, m=m, b=b: e.activation(out=F.yT[:, m, :], in_=C.psum[b][:, :], func=AF.Copy),
                 reads=["ps%d" % b], writes=[("H", m, 0), ("H", m, 1)])
        emit_rms_stats(C, lambda c: F.yT[:, c, :], lambda c: [("H", c, 0), ("H", c, 1)], half, 7, post_scale=0.5)
        for c in range(DC):
            P.op("dve", lambda e, c=c: e.scalar_tensor_tensor(
                out=F.yT[:, c, :], in0=F.yT[:, c, :], scalar=nw_post[:, c:c + 1], in1=C.rstd[:, :],
                op0=ALU.mult, op1=ALU.mult),
                reads=[("H", c, 0), ("H", c, 1), "rstd", nwkey], writes=[("H", c, 0), ("H", c, 1)])
            P.op("pool", lambda e, c=c, ts=ts: e.tensor_tensor(
                out=C.xT[:, c, ts], in0=F.yT[:, c, :], in1=C.xT[:, c, ts], op=ALU.add),
                reads=[("H", c, 0), ("H", c, 1), ("xT", c, half)], writes=[("xT", c, half)])


def alloc_ffn(C, stack):
    F = Ctx()
    F.hid = C.arB[:, :].bitcast(BF16).rearrange("p (k t) -> p k t", k=FC)
    wb = C.arW[:, :].bitcast(BF16)
    F.wgu = [wb[:, i * 4096:(i + 1) * 4096].rearrange("p (g c n) -> p g c n", g=2, c=DC) for i in range(2)]
    n = (FC // 4) * 128
    F.wd = [wb[:, i * n:(i + 1) * n].rearrange("p (k n) -> p k n", n=128) for i in range(5)]
    F.sil = C.sq
    F.yT = C.arH[:, :].rearrange("p (c t) -> p c t", c=DC)
    return F


def build_ffn_prog():
    nc = bass.Bass("TRN2", target_bir_lowering=False)
    xin = nc.dram_tensor("xin", [D, NT], F32, kind="ExternalInput").ap()
    wgu = nc.dram_tensor("wgu", [FC, 128, 2, DC, 128], F32, kind="ExternalInput").ap()
    wd = nc.dram_tensor("wd", [DC, 128, FC, 128], F32, kind="ExternalInput").ap()
    nwd = nc.dram_tensor("nw", [128, 2, DC], F32, kind="ExternalInput").ap()
    xout = nc.dram_tensor("xout", [D, NT], F32, kind="ExternalOutput").ap()
    with contextlib.ExitStack() as stack:
        P = Prog(nc)
        C = alloc_common(nc, stack, P)
        F = alloc_ffn(C, stack)
        nw = C.sb("nw_sb", [128, 2, DC], F32)
        P.dma("sp", lambda e: e.dma_start(out=nw[:, :, :], in_=nwd), writes=["nw"])
        for c in range(DC):
            P.dma("sp", lambda e, c=c: e.dma_start(out=C.xT[:, c, :], in_=xin[c * 128:(c + 1) * 128, :]),
                  writes=[("xT", c, 0), ("xT", c, 1)])
        emit_ffn(C, wgu, wd, nw[:, 0, :], nw[:, 1, :], "nw", F)
        for c in range(DC):
            P.dma("sp", lambda e, c=c: e.dma_start(out=xout[c * 128:(c + 1) * 128, :], in_=C.xT[:, c, :]),
                  reads=[("xT", c, 0), ("xT", c, 1)])
        P.emit(stack)
        print("prog stats", P.stats)
    return nc


def lay_gu(wg, wu):
    a = np.stack([wg, wu], axis=0).reshape(2, DC, 128, FC, 128)
    return np.ascontiguousarray(a.transpose(3, 2, 0, 1, 4))


def lay_d(wd):
    a = wd.reshape(FC, 128, DC, 128)
    return np.ascontiguousarray(a.transpose(2, 1, 0, 3))


def lay_nw(w):
    return np.ascontiguousarray(w.reshape(DC, 128).T)


_PROGS = {}
CMQ = None


def run_ffn(xT_shards, wg, wu, wd, nw_pre, nw_post):
    if "ffn" not in _PROGS:
        _PROGS["ffn"] = build_ffn_prog()
    nc = _PROGS["ffn"]
    gu = lay_gu(wg, wu)
    dd = lay_d(wd)
    nw = np.ascontiguousarray(np.stack([lay_nw(nw_pre), lay_nw(nw_post)], axis=1))
    in_maps = [{"xin": xT_shards[c], "wgu": gu, "wd": dd, "nw": nw} for c in range(NCORES)]
    import os
    res = run_bass_kernel_spmd(nc, in_maps, core_ids=list(range(NCORES)), trace=bool(os.environ.get("KTRACE")))
    if os.environ.get("KTRACE"):
        print("exec_time_ns", res.exec_time_ns)
    return [r["xout"] for r in res.results]


AX = mybir.AxisListType
NCH_IN = 40
O_NW, O_LBR, O_LMASK, O_QKW, O_GNW, O_SINK, O_CMASK, O_FLAGS = 0, 96, 128, 132, 134, 135, 147, 179
NCST = 181
NCB = 392
NCC = 256


def carve(ar, off, nbytes, dt, pattern=None, **kw):
    assert off % 4 == 0 and nbytes % 4 == 0
    v = ar[:, off // 4:(off + nbytes) // 4]
    if dt == BF16:
        v = v.bitcast(BF16)
    if pattern:
        v = v.rearrange(pattern, **kw)
    return v


def emit_consts(C, X):
    P = C.P
    C.cst = C.sb("cst_sb", [128, NCST], F32)
    C.der = C.sb("der_sb", [128, 96], F32)
    P.dma("sp", lambda e: e.dma_start(out=C.cst[:, :], in_=X["cst"]), writes=["cst"])
    cst, der = C.cst, C.der
    P.op("act", lambda e: e.activation(out=der[:, 0:32], in_=cst[:, O_LBR:O_LBR + 32], func=AF.Exp),
         reads=["cst"], writes=["der"])
    ev = der[:, 0:32].rearrange("p (l k) -> p l k", l=4)
    tot, part = der[:, 32:40], der[:, 40:48]
    P.op("dve", lambda e: e.tensor_tensor(out=tot, in0=ev[:, 0, :], in1=ev[:, 1, :], op=ALU.add), reads=["der"], writes=["der"])
    P.op("dve", lambda e: e.tensor_tensor(out=tot, in0=tot, in1=ev[:, 2, :], op=ALU.add), reads=["der"], writes=["der"])
    P.op("dve", lambda e: e.tensor_tensor(out=tot, in0=tot, in1=ev[:, 3, :], op=ALU.add), reads=["der"], writes=["der"])
    P.op("dve", lambda e: e.tensor_scalar(out=part, in0=ev[:, 0, :], scalar1=cst[:, O_LMASK:O_LMASK + 1], scalar2=None,
                                          op0=ALU.mult), reads=["der", "cst"], writes=["der"])
    for l in range(1, 4):
        P.op("dve", lambda e, l=l: e.scalar_tensor_tensor(out=part, in0=ev[:, l, :], scalar=cst[:, O_LMASK + l:O_LMASK + l + 1],
                                                          in1=part, op0=ALU.mult, op1=ALU.add),
             reads=["der", "cst"], writes=["der"])
    P.op("dve", lambda e: e.reciprocal(out=tot, in_=tot), reads=["der"], writes=["der"])
    C.lb, C.oml, C.noml = der[:, 48:56], der[:, 56:64], der[:, 64:72]
    P.op("dve", lambda e: e.tensor_tensor(out=C.lb, in0=part, in1=tot, op=ALU.mult), reads=["der"], writes=["der"])
    P.op("dve", lambda e: e.tensor_scalar(out=C.oml, in0=C.lb, scalar1=-1.0, scalar2=1.0, op0=ALU.mult, op1=ALU.add),
         reads=["der"], writes=["der"])
    P.op("dve", lambda e: e.tensor_scalar(out=C.noml, in0=C.oml, scalar1=-1.0, scalar2=None, op0=ALU.mult),
         reads=["der"], writes=["der"])
    C.esink = der[:, 72:84]
    P.op("act", lambda e: e.activation(out=C.esink, in_=cst[:, O_SINK:O_SINK + 12], func=AF.Exp),
         reads=["cst", "der"], writes=["der"])
    C.nw = cst[:, O_NW:O_NW + 96].rearrange("p (i c) -> p i c", i=6)


class WStream:
    def __init__(self, C, nslot=4):
        self.C = C
        wb = C.arW[:, :].bitcast(BF16)
        self.slots = [wb[:, i * 2048:(i + 1) * 2048].rearrange("p (c n) -> p c n", c=DC) for i in range(nslot)]
        self.n = nslot
        self.i = 0

    def load(self, src):
        s = self.i % self.n
        self.i += 1
        t = self.slots[s]
        self.C.P.dma("pool", lambda e: e.dma_start(out=t[:, :, :], in_=src), writes=[("ws", s)])
        return t, ("ws", s)


def proj_fm(C, W, src, b0=0):
    P = C.P
    t, key = W.load(src)
    for c in range(DC):
        for half in range(2):
            ts = slice(half * 512, (half + 1) * 512)
            P.op("pe", lambda e, c=c, ts=ts, half=half: e.matmul(
                C.psum[b0 + half][:, :], lhsT=t[:, c, :], rhs=C.hT[:, c, ts], start=(c == 0), stop=(c == DC - 1)),
                reads=[key, ("H", c, half)], writes=["ps%d" % (b0 + half)])


def proj_tm(C, W, src, b0=2):
    P = C.P
    t, key = W.load(src)
    for tile in range(8):
        b = b0 + tile // 4
        cs = slice((tile % 4) * 128, (tile % 4 + 1) * 128)
        half = tile // 4
        for c in range(DC):
            P.op("pe", lambda e, c=c, b=b, cs=cs, tile=tile: e.matmul(
                C.psum[b][:, cs], lhsT=C.hT[:, c, tile * 128:(tile + 1) * 128], rhs=t[:, c, :],
                start=(c == 0), stop=(c == DC - 1)),
                reads=[key, ("H", c, half)], writes=["ps%d" % b])


def emit_B(C, X, W, final):
    P, nc = C.P, C.nc
    arB = C.arB
    base = 8192
    off = [base]

    def take(nbytes, dt, pattern=None, **kw):
        v = carve(arB, off[0], nbytes, dt, pattern, **kw)
        off[0] += nbytes
        return v

    C.yB = carve(arB, 0, 8192, BF16, "p (h t) -> p h t", h=4)
    cB = take(NCB * 4, F32)
    qf = take(4096, F32)
    gs = take(4096, F32)
    Vt = take(2048, BF16, "p (a n) -> p a n", a=8)
    Vbd = [take(2048, BF16, "p (j n) -> p j n", j=8) for _ in range(2)]
    osb = take(4096, F32)
    sig = take(4096, F32)
    logf = take(4096, F32)
    kk = take(4096, F32)
    pa = take(4096, F32)
    pb = take(4096, F32)
    ex = [take(4096, F32) for _ in range(2)]
    Qd = take(2048, BF16)
    Kd = take(2048, BF16)
    Ke = take(2048, BF16)
    KeT = take(2048, BF16, "p (a n) -> p a n", a=8)
    attm = [take(256, BF16) for _ in range(2)]
    S = [take(512, F32) for _ in range(2)]
    Sbf = [take(2048, BF16, "p (j n) -> p j n", j=8) for _ in range(2)]
    dch = take(256, F32)
    tsum = take(32, F32)
    identb = take(256, BF16)
    bst_sb = take(8 * 129 * 4, F32, "p (k n) -> p k n", k=8) if not final else None
    bsin = take(8 * 2 * 129 * 4, F32, "p (c d n) -> p c d n", c=8, d=2) if final else None
    stmp = take(512, F32)
    deff = take(32, F32)
    ytmp = take(2048, F32)
    cmq = take(2048, BF16, "p (j t) -> p j t", j=8)
    Qdm = [take(2048, BF16, "p (j t) -> p j t", j=8) for _ in range(2)]
    assert off[0] <= 90112, off[0]
    P.dma("sp", lambda e: e.dma_start(out=cmq[:, :, :], in_=X["cmq"]), writes=["cmq"])

    P.dma("sp", lambda e: e.dma_start(out=cB[:, :], in_=X["cB"]), writes=["cB"])
    P.op("pool", lambda e: e.tensor_copy(out=identb[:, :], in_=cB[:, 256:384]), reads=["cB"], writes=["identb"])
    maskT = [cB[:, 0:128], cB[:, 128:256]]
    cmk = cB[:, 384:392]
    v3 = lambda a: a[:, :].rearrange("p (j s) -> p j s", s=16)
    cst = C.cst

    for h in range(4):
        wbase = 5 * h
        proj_tm(C, W, X["wfm"][wbase + 0], b0=2)
        for bb in range(2):
            P.op("act", lambda e, bb=bb: e.activation(
                out=Vt[:, 4 * bb:4 * bb + 4, :], in_=C.psum[2 + bb][:, :].rearrange("p (a n) -> p a n", a=4), func=AF.Copy),
                reads=["ps%d" % (2 + bb)], writes=["Vt"])
        if final:
            proj_fm(C, W, X["wfm"][wbase + 1], b0=0)
            for half in range(2):
                ts = slice(half * 512, (half + 1) * 512)
                P.op("act", lambda e, half=half, ts=ts: e.activation(out=qf[:, ts], in_=C.psum[half][:, :], func=AF.Silu),
                     reads=["ps%d" % half], writes=[("qf", half)])
            P.dma("sp", lambda e, h=h: e.dma_start(out=bsin[:, :, :, :], in_=X["bst_all"][h]), writes=["bsin"])
        for dirn in range(2):
            hd = 2 * h + dirn
            lbi = dirn * 4 + h
            proj_fm(C, W, X["wfm"][wbase + 2 + dirn], b0=0)
            for half in range(2):
                ts = slice(half * 512, (half + 1) * 512)
                P.op("act", lambda e, half=half, ts=ts: e.activation(out=sig[:, ts], in_=C.psum[half][:, :], func=AF.Sigmoid),
                     reads=["ps%d" % half], writes=[("sig", half)])
                P.op("act", lambda e, ts=ts, lbi=lbi: e.activation(out=logf[:, ts], in_=sig[:, ts], func=AF.Ln,
                                                                   bias=C.lb[:, lbi:lbi + 1], scale=C.oml[:, lbi:lbi + 1]),
                     reads=[("sig", half), "der"], writes=[("logf", half)])
                P.op("dve", lambda e, ts=ts, lbi=lbi: e.tensor_scalar(out=kk[:, ts], in0=sig[:, ts], scalar1=C.noml[:, lbi:lbi + 1],
                                                                      scalar2=C.oml[:, lbi:lbi + 1], op0=ALU.mult, op1=ALU.add),
                     reads=[("sig", half), "der"], writes=[("kk", half)])
            src, skey = logf, [("logf", 0), ("logf", 1)]
            dsts = [(pa, "pa"), (pb, "pb"), (pa, "pa"), (pb, "pb")]
            for si, sft in enumerate((1, 2, 4, 8)):
                dst, dkey = dsts[si]
                P.op("dve", lambda e, src=src, dst=dst, sft=sft: e.tensor_tensor(
                    out=v3(dst)[:, :, sft:16], in0=v3(src)[:, :, sft:16], in1=v3(src)[:, :, 0:16 - sft], op=ALU.add),
                    reads=skey, writes=[dkey])
                P.op("pool", lambda e, src=src, dst=dst, sft=sft: e.tensor_copy(
                    out=v3(dst)[:, :, 0:sft], in_=v3(src)[:, :, 0:sft]),
                    reads=skey, writes=[dkey + "c"])
                src, skey = dst, [dkey, dkey + "c"]
            PIN = ["pb", "pbc"]
            Tb = v3(pb)[:, :, 15:16].to_broadcast([128, 64, 16])
            P.op("act", lambda e: e.activation(out=dch[:, :].rearrange("p (j o) -> p j o", o=1), in_=v3(pb)[:, :, 15:16], func=AF.Exp),
                 reads=PIN, writes=["dch"])
            if dirn == 0:
                bsrc, bkey = pb, PIN
            else:
                P.op("dve", lambda e: e.tensor_tensor(out=v3(pa), in0=Tb, in1=v3(pb), op=ALU.subtract),
                     reads=PIN + ["pa", "pac"], writes=["pa", "pac"])
                P.op("dve", lambda e: e.tensor_tensor(out=pa[:, :], in0=pa[:, :], in1=logf[:, :], op=ALU.add),
                     reads=["pa", "pac", ("logf", 0), ("logf", 1)], writes=["pa", "pac"])
                bsrc, bkey = pa, ["pa", "pac"]
            if final:
                P.op("act", lambda e, bsrc=bsrc: e.activation(out=ex[0][:, :], in_=bsrc[:, :], func=AF.Exp),
                     reads=bkey, writes=["ex0"])
                P.op("dve", lambda e: e.tensor_tensor(out=Qd[:, :], in0=qf[:, :], in1=ex[0][:, :], op=ALU.mult),
                     reads=["ex0", ("qf", 0), ("qf", 1)], writes=["Qd"])
                P.op("act", lambda e, bsrc=bsrc: e.activation(out=ex[1][:, :], in_=bsrc[:, :], func=AF.Exp, scale=-1.0),
                     reads=bkey, writes=["ex1"])
                P.op("dve", lambda e: e.tensor_tensor(out=Kd[:, :], in0=kk[:, :], in1=ex[1][:, :], op=ALU.mult),
                     reads=["ex1", ("kk", 0), ("kk", 1)], writes=["Kd"])
            if dirn == 0:
                P.op("dve", lambda e: e.tensor_tensor(out=v3(pa), in0=Tb, in1=v3(pb), op=ALU.subtract),
                     reads=PIN + ["pa", "pac"], writes=["pa", "pac"])
                esrc, ekey = pa, ["pa", "pac"]
            else:
                P.op("dve", lambda e: e.tensor_tensor(out=pb[:, :], in0=pb[:, :], in1=logf[:, :], op=ALU.subtract),
                     reads=PIN + [("logf", 0), ("logf", 1), "dch"] + bkey, writes=PIN)
                esrc, ekey = pb, PIN
            P.op("act", lambda e, esrc=esrc: e.activation(out=ex[0][:, :], in_=esrc[:, :], func=AF.Exp),
                 reads=ekey + ["Qd"], writes=["ex0"])
            P.op("dve", lambda e: e.tensor_tensor(out=Ke[:, :], in0=kk[:, :], in1=ex[0][:, :], op=ALU.mult),
                 reads=["ex0", ("kk", 0), ("kk", 1)], writes=["Ke"])
            for tile in range(8):
                reg = tile % 2
                pst = C.psum[7][:, :].bitcast(BF16)[:, reg * 128:(reg + 1) * 128]
                P.op("pe", lambda e, tile=tile, pst=pst: e.transpose(pst, Ke[:, tile * 128:(tile + 1) * 128], identb[:, :]),
                     reads=["Ke", "identb"], writes=[("ps7", reg)])
                P.op("act", lambda e, tile=tile, pst=pst: e.activation(out=KeT[:, tile, :], in_=pst, func=AF.Copy),
                     reads=[("ps7", reg)], writes=[("KeT", tile)])
            cur = 0
            if final and not os.environ.get("KB_NOS0"):
                P.op("pool", lambda e: e.memset(S[0][:, :], 0.0), writes=["S0"])
                order = range(8) if dirn == 0 else range(7, -1, -1)
                mo = O_CMASK + (0 if dirn == 0 else 16)
                for cc in order:
                    P.op("dve", lambda e, cc=cc, mo=mo, dirn=dirn: e.tensor_scalar(
                        out=stmp[:, :], in0=bsin[:, cc, dirn, 0:128], scalar1=cst[:, mo + cc:mo + cc + 1], scalar2=None, op0=ALU.mult),
                        reads=["bsin", "cst"], writes=["stmp"])
                    P.op("dve", lambda e, cc=cc, mo=mo, dirn=dirn: e.tensor_scalar(
                        out=deff[:, 0:1], in0=bsin[:, cc, dirn, 128:129], scalar1=cst[:, mo + cc:mo + cc + 1],
                        scalar2=cst[:, mo + 8 + cc:mo + 8 + cc + 1], op0=ALU.mult, op1=ALU.add),
                        reads=["bsin", "cst"], writes=["deff"])
                    P.op("dve", lambda e: e.scalar_tensor_tensor(out=S[0][:, :], in0=S[0][:, :], scalar=deff[:, 0:1], in1=stmp[:, :],
                                                                 op0=ALU.mult, op1=ALU.add),
                         reads=["S0", "deff", "stmp"], writes=["S0"])
            else:
                P.op("pool", lambda e: e.memset(S[0][:, :], 0.0), writes=["S0"])
            tiles = range(8) if dirn == 0 else range(7, -1, -1)
            for ti, tile in enumerate(tiles):
                vb = Vbd[ti % 2]
                for j in range(8):
                    P.op("pool", lambda e, vb=vb, j=j, tile=tile: e.tensor_scalar(
                        out=vb[:, j, :], in0=Vt[:, tile, :], scalar1=cmk[:, j:j + 1], scalar2=None, op0=ALU.mult),
                        reads=["Vt", "cB"], writes=[("Vbd", ti % 2, j)])
                for hb in range(2):
                    P.op("pe", lambda e, vb=vb, hb=hb, tile=tile: e.matmul(
                        C.psum[4 + hb][:, :], lhsT=KeT[:, tile, :],
                        rhs=vb[:, 4 * hb:4 * hb + 4, :], start=True, stop=True),
                        reads=[("KeT", tile)] + [("Vbd", ti % 2, j) for j in range(4 * hb, 4 * hb + 4)], writes=["ps%d" % (4 + hb)])
                sb_ = Sbf[ti % 2]
                js = range(8) if dirn == 0 else range(7, -1, -1)
                for j in js:
                    gj = tile * 8 + j
                    if final:
                        P.op("pool", lambda e, sb_=sb_, j=j, cur=cur: e.tensor_copy(out=sb_[:, j, :], in_=S[cur][:, :]),
                             reads=["S%d" % cur], writes=[("Sbf", ti % 2, j)])
                    kvp = C.psum[4 + j // 4][:, (j % 4) * 128:(j % 4 + 1) * 128]
                    P.op("dve", lambda e, cur=cur, gj=gj, kvp=kvp: e.scalar_tensor_tensor(
                        out=S[1 - cur][:, :], in0=S[cur][:, :], scalar=dch[:, gj:gj + 1], in1=kvp, op0=ALU.mult, op1=ALU.add),
                        reads=["S%d" % cur, "dch", "ps%d" % (4 + j // 4)], writes=["S%d" % (1 - cur)])
                    cur = 1 - cur
                if final and not os.environ.get("KB_NOINTRA"):
                    tsl = slice(tile * 128, (tile + 1) * 128)
                    pa_ = C.psum[6][:, 0:128]
                    po_ = C.psum[6][:, 256:384]
                    am = attm[ti % 2]
                    P.op("pe", lambda e, tsl=tsl, pa_=pa_: e.matmul(pa_, lhsT=Kd[:, tsl], rhs=Qd[:, tsl], start=True, stop=True),
                         reads=["Kd", "Qd"], writes=[("ps6", 0)])
                    P.op("dve", lambda e, pa_=pa_, am=am, dirn=dirn: e.tensor_tensor(out=am[:, :], in0=pa_, in1=maskT[dirn], op=ALU.mult),
                         reads=[("ps6", 0), "cB"], writes=[("attm", ti % 2)])
                    qm = Qdm[ti % 2]
                    P.op("dve", lambda e, qm=qm, tsl=tsl: e.tensor_tensor(
                        out=qm[:, :, :], in0=Qd[:, tsl].unsqueeze(1).to_broadcast([128, 8, 128]), in1=cmq[:, :, :], op=ALU.mult),
                        reads=["Qd", "cmq"], writes=[("Qdm", ti % 2)])
                    P.op("pe", lambda e, po_=po_, am=am, tile=tile: e.matmul(po_, lhsT=Vt[:, tile, :], rhs=am[:, :], start=True, stop=False),
                         reads=["Vt", ("attm", ti % 2)], writes=[("ps6", 1)])
                    for jj, j in enumerate(js):
                        P.op("pe", lambda e, po_=po_, sb_=sb_, j=j, qm=qm, jj=jj: e.matmul(
                            po_, lhsT=sb_[:, j, :], rhs=qm[:, j, :], start=False, stop=(jj == 7)),
                            reads=[("Sbf", ti % 2, j), ("Qdm", ti % 2)], writes=[("ps6", 1)])
                    if dirn == 0:
                        P.op("act", lambda e, tsl=tsl, po_=po_: e.activation(out=osb[:, tsl], in_=po_, func=AF.Copy),
                             reads=[("ps6", 1)], writes=[("osb", tile // 4)])
                    else:
                        P.op("dve", lambda e, tsl=tsl, po_=po_: e.tensor_tensor(out=osb[:, tsl], in0=osb[:, tsl], in1=po_, op=ALU.add),
                             reads=[("ps6", 1), ("osb", tile // 4)], writes=[("osb", tile // 4)])
            if not final:
                P.op("pool", lambda e, cur=cur, hd=hd: e.tensor_copy(out=bst_sb[:, hd, 0:128], in_=S[cur][:, :]),
                     reads=["S%d" % cur], writes=[("bst", hd)])
                P.op("dve", lambda e, hd=hd: e.tensor_reduce(out=bst_sb[:, hd, 128:129], in_=dch[:, :], axis=AX.X, op=ALU.mult),
                     reads=["dch"], writes=[("bstd", hd)])
        if final:
            proj_fm(C, W, X["wfm"][wbase + 4], b0=0)
            for half in range(2):
                ts = slice(half * 512, (half + 1) * 512)
                P.op("act", lambda e, half=half, ts=ts: e.activation(out=gs[:, ts], in_=C.psum[half][:, :], func=AF.Silu),
                     reads=["ps%d" % half], writes=[("gs", half)])
                s = rr(C, "sq", 2)
                sq = C.sq[s]
                P.op("act", lambda e, sq=sq, ts=ts: e.activation(out=sq[:, :], in_=osb[:, ts], func=AF.Square),
                     reads=[("osb", half)], writes=["sq%d" % s])
                P.op("pe", lambda e, sq=sq: e.matmul(C.psum[7][:, :], lhsT=C.ones[:, :], rhs=sq[:, :], start=True, stop=True),
                     reads=["sq%d" % s, "ones"], writes=[("ps7", 0), ("ps7", 1)])
                P.op("dve", lambda e: e.tensor_scalar(out=C.rstd[:, :], in0=C.psum[7][:, :], scalar1=1.0 / 128, scalar2=EPS,
                                                      op0=ALU.mult, op1=ALU.add),
                     reads=[("ps7", 0), ("ps7", 1)], writes=["rstd"])
                P.op("act", lambda e: e.activation(out=C.rstd[:, :], in_=C.rstd[:, :], func=AF.Sqrt), reads=["rstd"], writes=["rstd"])
                P.op("dve", lambda e: e.reciprocal(out=C.rstd[:, :], in_=C.rstd[:, :]), reads=["rstd"], writes=["rstd"])
                P.op("dve", lambda e, ts=ts: e.scalar_tensor_tensor(
                    out=ytmp[:, :], in0=osb[:, ts], scalar=cst[:, O_GNW:O_GNW + 1], in1=C.rstd[:, :], op0=ALU.mult, op1=ALU.mult),
                    reads=[("osb", half), "rstd", "cst"], writes=["ytmp"])
                P.op("dve", lambda e, ts=ts, h=h: e.tensor_tensor(out=C.yB[:, h, ts], in0=ytmp[:, :], in1=gs[:, ts], op=ALU.mult),
                     reads=["ytmp", ("gs", half)], writes=[("yB", h, half)])
    return bst_sb


OFF_QC, OFF_KCL, OFF_VCL, OFF_OV = 8192, 20480, 24576, 28672
OFF_QA, OFF_KA, OFF_VA = 28672, 40960, 46080
OFF_SCR = 69760
VXW = 192


def vx_cols(g):
    base = (g // 2) * VXW
    if g % 2 == 0:
        return base, 128, 64, 0
    return base + 64, 128, 0, 64


def emit_norm_out(C, A, pso, g, dest, esink_col, okeys, dkey):
    P = C.P
    _, M, r, r0 = vx_cols(g)
    rows = slice(r0, r0 + 64)
    rr_ = slice(r, r + 1)
    if esink_col is None:
        P.op("dve", lambda e: e.tensor_copy(out=A.den[rr_, :], in_=pso[rr_, :]), reads=okeys, writes=["den"])
    else:
        P.op("dve", lambda e: e.tensor_scalar(out=A.den[rr_, :], in0=pso[rr_, :], scalar1=C.esink[rr_, esink_col:esink_col + 1],
                                              scalar2=None, op0=ALU.add), reads=okeys + ["der"], writes=["den"])
    P.op("dve", lambda e: e.reciprocal(out=A.rden[rr_, :], in_=A.den[rr_, :]), reads=["den"], writes=["rden"])
    P.op("pe", lambda e: e.matmul(C.psum[6][:, :], lhsT=C.ones[rr_, :], rhs=A.rden[rr_, :], start=True, stop=True),
         reads=["rden", "ones"], writes=["ps6"])
    P.op("act", lambda e: e.activation(out=A.bcs[rows, :], in_=C.psum[6][rows, :], func=AF.Copy), reads=["ps6"], writes=["bcs"])
    P.op("dve", lambda e: e.tensor_tensor(out=dest, in0=pso[rows, :], in1=A.bcs[rows, :], op=ALU.mult),
         reads=okeys + ["bcs"], writes=[dkey])


def emit_proj_attn(C, X, W, stage):
    P = C.P
    arB = C.arB
    A = Ctx()
    C.A = A
    A.qC = carve(arB, OFF_QC, 12288, BF16, "p (c t) -> p c t", c=6)
    A.kCl = carve(arB, OFF_KCL, 4096, BF16, "p (c t) -> p c t", c=2)
    A.vCl = carve(arB, OFF_VCL, 4096, BF16, "p (a n) -> p a n", a=8)
    if stage == "S2":
        A.qA = carve(arB, OFF_QA, 12288, BF16, "p (c t) -> p c t", c=6)
        A.kA = carve(arB, OFF_KA, 5120, BF16, "p (c t) -> p c t", c=2)
        A.vA = carve(arB, OFF_VA, 7744, BF16)[:, 0:10 * 2 * VXW].rearrange("p (a n) -> p a n", a=10)
    else:
        A.kAl = carve(arB, OFF_QA, 4096, BF16, "p (c t) -> p c t", c=2)
        A.vAl = carve(arB, OFF_QA + 4096, 4096, BF16, "p (a n) -> p a n", a=8)
    cs = carve(arB, OFF_SCR, 8192, F32, "p (k t) -> p k t", k=2)
    cC = carve(arB, OFF_SCR + 8192, 1024, F32)
    qn = [carve(arB, OFF_SCR + 9216 + i * 2048, 2048, F32) for i in range(2)]
    P.dma("sp", lambda e: e.dma_start(out=cs[:, :, :], in_=X["cs"]), writes=["cs"])
    P.dma("sp", lambda e: e.dma_start(out=cC[:, :], in_=X["cC"]), writes=["cC"])
    Rm, bones = cC[:, 0:128], cC[:, 128:256]
    cst = C.cst
    if stage == "S2":
        P.op("pool", lambda e: e.memset(A.vA[:, :, :], 0.0), writes=["vA"])
        P.op("pool", lambda e: e.memset(A.vA[:, :, 64:66], 1.0), reads=["vA"], writes=["vA"])
        P.op("pool", lambda e: e.memset(A.vA[:, :, VXW + 64:VXW + 66], 1.0), reads=["vA"], writes=["vA"])
        for ci in range(6):
            proj_fm(C, W, X["wfm"][20 + ci], b0=(ci % 2) * 2)
            for half in range(2):
                b = (ci % 2) * 2 + half
                P.op("act", lambda e, ci=ci, half=half, b=b: e.activation(
                    out=A.qA[:, ci, half * 512:(half + 1) * 512], in_=C.psum[b][:, :], func=AF.Copy),
                    reads=["ps%d" % b], writes=[("qA", ci)])
    for pair in range(2):
        proj_fm(C, W, X["wfm"][26 + pair], b0=(pair % 2) * 2)
        for half in range(2):
            b = (pair % 2) * 2 + half
            dst = (A.kA[:, pair, 128 + half * 512:128 + (half + 1) * 512] if stage == "S2"
                   else A.kAl[:, pair, half * 512:(half + 1) * 512])
            P.op("act", lambda e, dst=dst, b=b: e.activation(out=dst, in_=C.psum[b][:, :], func=AF.Copy),
                 reads=["ps%d" % b], writes=[("kA", pair)])
    for pair in range(2):
        proj_tm(C, W, X["wfm"][28 + pair], b0=4)
        for bb in range(2):
            pv = C.psum[4 + bb][:, :].rearrange("p (a n) -> p a n", a=4)
            if stage == "S2":
                o0 = pair * VXW
                P.op("act", lambda e, bb=bb, pv=pv, o0=o0: e.activation(
                    out=A.vA[:, 1 + 4 * bb:5 + 4 * bb, o0:o0 + 64], in_=pv[:, :, 0:64], func=AF.Copy),
                    reads=["ps%d" % (4 + bb), "vA"], writes=[("vAx", pair, bb, 0)])
                P.op("act", lambda e, bb=bb, pv=pv, o0=o0: e.activation(
                    out=A.vA[:, 1 + 4 * bb:5 + 4 * bb, o0 + 128:o0 + 192], in_=pv[:, :, 64:128], func=AF.Copy),
                    reads=["ps%d" % (4 + bb), "vA"], writes=[("vAx", pair, bb, 1)])
            else:
                P.op("act", lambda e, bb=bb, pv=pv, pair=pair: e.activation(
                    out=A.vAl[:, 4 * bb:4 * bb + 4, pair * 128:(pair + 1) * 128], in_=pv, func=AF.Copy),
                    reads=["ps%d" % (4 + bb)], writes=[("vAl", pair, bb)])
    chunks = ([("q", ci) for ci in range(6)] if stage == "S2" else []) + \
             ([("k", pr) for pr in range(2)] if stage == "S1" else [])
    for n_, (kind, idx) in enumerate(chunks):
        b0 = (n_ % 2) * 2
        proj_fm(C, W, X["wfm"][(30 if kind == "q" else 36) + idx], b0=b0)
        wcol = O_QKW + (0 if kind == "q" else 1)
        for half in range(2):
            b = b0 + half
            ts = slice(half * 512, (half + 1) * 512)
            s = rr(C, "sq", 2)
            sq = C.sq[s]
            q_ = qn[half]
            P.op("act", lambda e, sq=sq, b=b: e.activation(out=sq[:, :], in_=C.psum[b][:, :], func=AF.Square),
                 reads=["ps%d" % b], writes=["sq%d" % s])
            P.op("pe", lambda e, sq=sq, half=half: e.matmul(C.psum[4 + half][:, :], lhsT=bones, rhs=sq[:, :], start=True, stop=True),
                 reads=["sq%d" % s, "cC"], writes=["ps%d" % (4 + half)])
            P.op("dve", lambda e, half=half: e.tensor_scalar(out=C.rstd[:, :], in0=C.psum[4 + half][:, :], scalar1=1.0 / 64, scalar2=EPS,
                                                             op0=ALU.mult, op1=ALU.add),
                 reads=["ps%d" % (4 + half)], writes=["rstd"])
            P.op("act", lambda e: e.activation(out=C.rstd[:, :], in_=C.rstd[:, :], func=AF.Sqrt), reads=["rstd"], writes=["rstd"])
            P.op("dve", lambda e: e.reciprocal(out=C.rstd[:, :], in_=C.rstd[:, :]), reads=["rstd"], writes=["rstd"])
            P.op("dve", lambda e, q_=q_, b=b, wcol=wcol: e.scalar_tensor_tensor(
                out=q_[:, :], in0=C.psum[b][:, :], scalar=cst[:, wcol:wcol + 1], in1=C.rstd[:, :], op0=ALU.mult, op1=ALU.mult),
                reads=["ps%d" % b, "rstd", "cst"], writes=[("qn", half)])
            P.op("pe", lambda e, q_=q_, half=half: e.matmul(C.psum[6 + half][:, :], lhsT=Rm, rhs=q_[:, :], start=True, stop=True),
                 reads=[("qn", half), "cC"], writes=["ps%d" % (6 + half)])
            s2 = rr(C, "sq", 2)
            t2 = C.sq[s2]
            P.op("dve", lambda e, t2=t2, half=half, ts=ts: e.tensor_tensor(out=t2[:, :], in0=C.psum[6 + half][:, :], in1=cs[:, 1, ts], op=ALU.mult),
                 reads=["ps%d" % (6 + half), "cs"], writes=["sq%d" % s2])
            P.op("dve", lambda e, q_=q_, ts=ts: e.tensor_tensor(out=q_[:, :], in0=q_[:, :], in1=cs[:, 0, ts], op=ALU.mult),
                 reads=[("qn", half), "cs"], writes=[("qn", half)])
            dst = A.qC[:, idx, ts] if kind == "q" else A.kCl[:, idx, ts]
            P.op("dve", lambda e, q_=q_, t2=t2, dst=dst: e.tensor_tensor(out=dst, in0=q_[:, :], in1=t2[:, :], op=ALU.add),
                 reads=[("qn", half), "sq%d" % s2], writes=[("qkC", kind, idx, half)])
    if stage == "S1":
        for pair in range(2):
            proj_tm(C, W, X["wfm"][38 + pair], b0=4)
            for bb in range(2):
                pv = C.psum[4 + bb][:, :].rearrange("p (a n) -> p a n", a=4)
                P.op("act", lambda e, bb=bb, pv=pv, pair=pair: e.activation(
                    out=A.vCl[:, 4 * bb:4 * bb + 4, pair * 128:(pair + 1) * 128], in_=pv, func=AF.Copy),
                    reads=["ps%d" % (4 + bb)], writes=[("vCl", pair, bb)])


def emit_attn_scratch(C):
    A = C.A
    arB = C.arB
    o = OFF_SCR
    A.bias = [carve(arB, o + i * 1536, 1536, F32) for i in range(2)]
    o += 3072
    A.tA = [carve(arB, o + i * 1536, 1536, F32) for i in range(2)]
    o += 3072
    A.den = carve(arB, o, 2048, F32)
    A.rden = carve(arB, o + 2048, 2048, F32)
    A.bcs = carve(arB, o + 4096, 2048, F32)
    o += 6144
    A.PT = [carve(arB, o + i * 1024, 1024, BF16) for i in range(4)]
    A.APT = [carve(arB, o + i * 768, 768, BF16) for i in range(10)]
    assert o + 7680 <= 90112
    A.yA = carve(C.arH, 0, 12288, BF16, "p (c t) -> p c t", c=6)
    A.yC = carve(C.arH, 12288, 12288, BF16, "p (c t) -> p c t", c=6)


def emit_attn_A(C, X):
    P, A = C.P, C.A
    cst = C.cst
    for side in range(2):
        col = 0 if side == 0 else 1152
        P.dma("sp", lambda e, side=side, col=col: e.dma_start(out=A.kA[:, :, col:col + 128], in_=X["kA_halo"][:, :, side, :]),
              writes=[("kAh", side)])
        t = 0 if side == 0 else 9
        P.dma("sp", lambda e, side=side, t=t: e.dma_start(out=A.vA[:, t, :], in_=X["vA_halo"][:, side, :]),
              reads=["vA"], writes=[("vAh", side)])
        P.op("pool", lambda e, side=side, t=t: e.tensor_scalar(
            out=A.vA[:, t, :], in0=A.vA[:, t, :], scalar1=cst[:, O_FLAGS + side:O_FLAGS + side + 1], scalar2=None, op0=ALU.mult),
            reads=[("vAh", side), "vA", "cst"], writes=[("vAt", side)])
    vkeys = ["vA", ("vAt", 0), ("vAt", 1)] + [("vAx", p_, b_, x_) for p_ in range(2) for b_ in range(2) for x_ in range(2)]
    for g in range(4):
        pair = g // 2
        o0, M, r, r0 = vx_cols(g)
        rows = slice(r0, r0 + 64)
        for r3 in range(3):
            h = 3 * g + r3
            ci = pair * 3 + r3
            bs = rr(C, "biasA", 2)
            P.dma("sp", lambda e, bs=bs, h=h: e.dma_start(out=A.bias[bs][:, :], in_=X["biasT"][h]), writes=[("biasA", bs)])
            for kt in range(10):
                n_lo, n_hi = max(kt - 2, 0), min(kt, 7)
                lo = (n_lo - (kt - 2)) * 128
                ncol = (n_hi - n_lo + 1) * 128
                sb_ = rr(C, "psA", 3)
                pss = C.psum[sb_][:, 0:ncol]
                P.op("pe", lambda e, pss=pss, kt=kt, n_lo=n_lo, ncol=ncol, ci=ci, pair=pair, rows=rows: e.matmul(
                    pss, lhsT=A.kA[rows, pair, kt * 128:(kt + 1) * 128], rhs=A.qA[rows, ci, n_lo * 128:n_lo * 128 + ncol],
                    start=True, stop=True),
                    reads=[("kA", pair), ("kAh", 0), ("kAh", 1), ("qA", ci)], writes=["ps%d" % sb_])
                tb = rr(C, "tA", 2)
                tA = A.tA[tb][:, 0:ncol]
                P.op("dve", lambda e, tA=tA, pss=pss, bs=bs, lo=lo, ncol=ncol: e.scalar_tensor_tensor(
                    out=tA, in0=pss, scalar=0.125, in1=A.bias[bs][:, lo:lo + ncol], op0=ALU.mult, op1=ALU.add),
                    reads=["ps%d" % sb_, ("biasA", bs)], writes=[("tA", tb)])
                PT = A.APT[kt][:, 0:ncol]
                P.op("act", lambda e, PT=PT, tA=tA: e.activation(out=PT, in_=tA, func=AF.Exp), reads=[("tA", tb)], writes=[("APT", kt)])
            for n in range(8):
                ob = 4 + n // 4
                for kt in (n, n + 1, n + 2):
                    n_lo = max(kt - 2, 0)
                    P.op("pe", lambda e, ob=ob, n=n, kt=kt, n_lo=n_lo, o0=o0, M=M: e.matmul(
                        C.psum[ob][0:M, (n % 4) * 128:(n % 4 + 1) * 128], lhsT=A.vA[:, kt, o0:o0 + M],
                        rhs=A.APT[kt][:, (n - n_lo) * 128:(n - n_lo + 1) * 128], start=(kt == n), stop=(kt == n + 2)),
                        reads=vkeys + [("APT", kt)], writes=["ps%d" % ob])
            for half in range(2):
                emit_norm_out(C, A, C.psum[4 + half], g, A.yA[rows, ci, half * 512:(half + 1) * 512], h,
                              ["ps%d" % (4 + half)], ("yA", ci, g % 2, half))


def emit_attn_C(C, X):
    P, A = C.P, C.A
    arB = C.arB
    Kall = carve(arB, OFF_OV, 16384, BF16)
    Vx = carve(arB, OFF_OV + 16384, 64 * VXW * 2, BF16, "p (a n) -> p a n", a=64)
    LA = 2
    for pair in range(2):
        for q4 in range(4):
            P.dma("sp", lambda e, pair=pair, q4=q4: e.dma_start(
                out=Kall[:, q4 * 2048:(q4 + 1) * 2048], in_=X["kC_all"][:, pair, q4 * 2048:(q4 + 1) * 2048]),
                writes=[("Kall", q4)])
        for q4 in range(4):
            P.dma("sp", lambda e, pair=pair, q4=q4: e.dma_start(
                out=Vx[:, q4 * 16:(q4 + 1) * 16, :], in_=X["vC_all"][:, pair, q4 * 16:(q4 + 1) * 16, :]),
                writes=[("Vxd", 0, q4), ("Vxd", 1, q4)])
        kkeys = [("Kall", q4) for q4 in range(4)]
        vkeys = [("Vxd", gh, q4) for gh in range(2) for q4 in range(4)]
        for r3 in range(3):
            ci = pair * 3 + r3
            for gh in range(2):
                g = 2 * pair + gh
                o0, M, r, r0 = vx_cols(g)
                o0 -= pair * VXW
                rows = slice(r0, r0 + 64)
                for qh in range(2):
                    ob = 4 + rr(C, "obC", 2)
                    qs = slice(qh * 512, (qh + 1) * 512)
                    pend = []

                    def qk(kt):
                        sb_ = rr(C, "psC", 4)
                        P.op("pe", lambda e, sb_=sb_, kt=kt, rows=rows, ci=ci, qs=qs: e.matmul(
                            C.psum[sb_][:, :], lhsT=Kall[rows, kt * 128:(kt + 1) * 128], rhs=A.qC[rows, ci, qs], start=True, stop=True),
                            reads=kkeys + [("qkC", "q", ci, qh)], writes=["ps%d" % sb_])
                        pb_ = rr(C, "PT", 4)
                        P.op("act", lambda e, sb_=sb_, pb_=pb_: e.activation(out=A.PT[pb_][:, :], in_=C.psum[sb_][:, :], func=AF.Exp, scale=0.125),
                             reads=["ps%d" % sb_], writes=[("PT", pb_)])
                        pend.append((kt, pb_))

                    def pv():
                        kt, pb_ = pend.pop(0)
                        P.op("pe", lambda e, kt=kt, pb_=pb_, ob=ob, M=M, o0=o0: e.matmul(
                            C.psum[ob][0:M, :], lhsT=Vx[:, kt, o0:o0 + M], rhs=A.PT[pb_][:, :], start=(kt == 0), stop=(kt == 63)),
                            reads=vkeys + [("PT", pb_)], writes=["ps%d" % ob])

                    for kt in range(64):
                        qk(kt)
                        if kt >= LA:
                            pv()
                    while pend:
                        pv()
                    emit_norm_out(C, A, C.psum[ob], g, A.yC[rows, ci, qs], None, ["ps%d" % ob], ("yC", ci, gh, qh))


def emit_wout(C, X, W):
    P, A = C.P, C.A
    yT = carve(C.arB, 16384, 32768, F32, "p (c t) -> p c t", c=DC)
    ych = [(A.yA[:, k, :], None) for k in range(6)] + [(C.yB[:, k, :], None) for k in range(4)] + \
          [(A.yC[:, k, :], None) for k in range(6)]
    nw = C.nw[:, 3, :]
    for half in range(2):
        ts = slice(half * 512, (half + 1) * 512)
        for m in range(DC):
            b = m % 2
            t, key = W.load(X["wout"][m])
            for k in range(DC):
                P.op("pe", lambda e, t=t, k=k, b=b, ts=ts: e.matmul(
                    C.psum[b][:, :], lhsT=t[:, k, :], rhs=ych[k][0][:, ts], start=(k == 0), stop=(k == DC - 1)),
                    reads=[key], writes=["ps%d" % b])
            P.op("act", lambda e, m=m, b=b: e.activation(out=yT[:, m, :], in_=C.psum[b][:, :], func=AF.Copy),
                 reads=["ps%d" % b], writes=[("yT", m)])
        emit_rms_stats(C, lambda c: yT[:, c, :], lambda c: [("yT", c)], half, 7)
        for c in range(DC):
            P.op("dve", lambda e, c=c: e.scalar_tensor_tensor(
                out=yT[:, c, :], in0=yT[:, c, :], scalar=nw[:, c:c + 1], in1=C.rstd[:, :], op0=ALU.mult, op1=ALU.mult),
                reads=[("yT", c), "rstd", "cst"], writes=[("yT", c)])
            P.op("pool", lambda e, c=c, ts=ts: e.tensor_tensor(out=C.xT[:, c, ts], in0=yT[:, c, :], in1=C.xT[:, c, ts], op=ALU.add),
                 reads=[("yT", c), ("xT", c, half)], writes=[("xT", c, half)])


def _din(nc, name, shape, dt=F32):
    return nc.dram_tensor(name, list(shape), dt, kind="ExternalInput").ap()


def _dout(nc, name, shape, dt=F32):
    return nc.dram_tensor(name, list(shape), dt, kind="ExternalOutput").ap()


def build_stage(stage):
    nc = bass.Bass("TRN2", target_bir_lowering=False)
    X = {}
    X["xin"] = _din(nc, "xin", [D, NT])
    X["wgu"] = _din(nc, "wgu", [FC, 128, 2, DC, 128])
    X["wd"] = _din(nc, "wd", [DC, 128, FC, 128])
    X["wfm"] = _din(nc, "wfm", [NCH_IN, 128, DC, 128])
    X["cst"] = _din(nc, "cst", [128, NCST])
    X["cB"] = _din(nc, "cB", [128, NCB])
    X["cC"] = _din(nc, "cC", [128, NCC])
    X["cs"] = _din(nc, "cs", [128, 2, NT])
    X["cmq"] = _din(nc, "cmq", [128, 8, 128], BF16)
    X["xout"] = _dout(nc, "xout", [D, NT])
    if stage == "S1":
        X["bst_out"] = _dout(nc, "bst_out", [128, 8, 129])
        X["kA_out"] = _dout(nc, "kA_out", [128, 2, NT], BF16)
        X["vA_out"] = _dout(nc, "vA_out", [128, 8, 256], BF16)
        X["kC_out"] = _dout(nc, "kC_out", [128, 2, NT], BF16)
        X["vC_out"] = _dout(nc, "vC_out", [128, 8, 256], BF16)
    else:
        X["wout"] = _din(nc, "wout", [DC, 128, DC, 128])
        X["bst_all"] = _din(nc, "bst_all", [4, 128, NCORES, 2, 129])
        X["kC_all"] = _din(nc, "kC_all", [128, 2, SEQ], BF16)
        X["vC_all"] = _din(nc, "vC_all", [128, 2, SEQ // 128, VXW], BF16)
        X["kA_halo"] = _din(nc, "kA_halo", [128, 2, 2, 128], BF16)
        X["vA_halo"] = _din(nc, "vA_halo", [128, 2, 2 * VXW], BF16)
        X["biasT"] = _din(nc, "biasT", [12, 128, 384])
    with contextlib.ExitStack() as stack:
        P = Prog(nc)
        C = alloc_common(nc, stack, P)
        F = alloc_ffn(C, stack)
        emit_consts(C, X)
        for c in range(DC):
            P.dma("sp", lambda e, c=c: e.dma_start(out=C.xT[:, c, :], in_=X["xin"][c * 128:(c + 1) * 128, :]),
                  writes=[("xT", c, 0), ("xT", c, 1)])
        W = WStream(C)
        if stage == "S1":
            emit_ffn(C, X["wgu"], X["wd"], C.nw[:, 0, :], C.nw[:, 1, :], "cst", F)
            P.barrier()
            emit_prenorm(C, C.nw[:, 2, :], "cst")
            bst_sb = emit_B(C, X, W, final=False)
            P.dma("sp", lambda e: e.dma_start(out=X["bst_out"], in_=bst_sb[:, :, :]),
                  reads=[("bst", k) for k in range(8)] + [("bstd", k) for k in range(8)])
            P.barrier()
            emit_proj_attn(C, X, W, "S1")
            A = C.A
            P.dma("sp", lambda e: e.dma_start(out=X["kA_out"], in_=A.kAl[:, :, :]), reads=[("kA", 0), ("kA", 1)])
            P.dma("sp", lambda e: e.dma_start(out=X["vA_out"], in_=A.vAl[:, :, :]),
                  reads=[("vAl", p_, b_) for p_ in range(2) for b_ in range(2)])
            P.dma("sp", lambda e: e.dma_start(out=X["kC_out"], in_=A.kCl[:, :, :]),
                  reads=[("qkC", "k", i_, h_) for i_ in range(2) for h_ in range(2)])
            P.dma("sp", lambda e: e.dma_start(out=X["vC_out"], in_=A.vCl[:, :, :]),
                  reads=[("vCl", p_, b_) for p_ in range(2) for b_ in range(2)])
        else:
            import os
            stop = os.environ.get("KSTOP", "")
            phases = ["B", "proj", "A", "C", "dbg", "wout", "ffn"]
            upto = phases.index(stop) if stop in phases else len(phases) - 1
            skip = os.environ.get("KSKIP", "").split(",")
            emit_prenorm(C, C.nw[:, 2, :], "cst")
            if "B" not in skip:
                emit_B(C, X, W, final=True)
            P.barrier()
            if upto >= 1:
                emit_proj_attn(C, X, W, "S2")
                P.barrier()
                emit_attn_scratch(C)
            if upto >= 2 and "A" not in skip:
                emit_attn_A(C, X)
                P.barrier()
            if upto >= 3 and "C" not in skip:
                emit_attn_C(C, X)
                P.barrier()
            if upto >= 4 and os.environ.get("KDEBUG"):
                X["dbgH"] = _dout(nc, "dbgH", [128, 8192])
                X["dbgB"] = _dout(nc, "dbgB", [128, 2048])
                P.dma("sp", lambda e: e.dma_start(out=X["dbgH"], in_=C.arH[:, :]))
                P.dma("sp", lambda e: e.dma_start(out=X["dbgB"], in_=C.arB[:, 0:2048]))
                P.barrier()
            if upto >= 5:
                emit_wout(C, X, W)
                P.barrier()
            if upto >= 6:
                emit_ffn(C, X["wgu"], X["wd"], C.nw[:, 4, :], C.nw[:, 5, :], "cst", F)
        for c in range(DC):
            P.dma("sp", lambda e, c=c: e.dma_start(out=X["xout"][c * 128:(c + 1) * 128, :], in_=C.xT[:, c, :]),
                  reads=[("xT", c, 0), ("xT", c, 1)])
        P.emit(stack)
        print("stage", stage, "stats", P.stats)
    return nc


import ml_dtypes

OFFS = dict(aq=0, ak=768, av=1024, bq=1280, bzf=1792, bzb=2304, bi=2816, bg=3328, cq=3840, ck=4608, cv=4864)


def _qcols(base, ci):
    pair, r3 = ci // 3, ci % 3
    lo = base + (3 * (2 * pair) + r3) * 64
    hi = base + (3 * (2 * pair + 1) + r3) * 64
    return list(range(lo, lo + 64)) + list(range(hi, hi + 64))


def in_col_index():
    cols = []
    for h in range(4):
        for nm in ("bi", "bq", "bzf", "bzb", "bg"):
            cols += list(range(OFFS[nm] + h * 128, OFFS[nm] + (h + 1) * 128))
    for pre in ("a", "c"):
        for ci in range(6):
            cols += _qcols(OFFS[pre + "q"], ci)
        for pair in range(2):
            cols += list(range(OFFS[pre + "k"] + pair * 128, OFFS[pre + "k"] + (pair + 1) * 128))
        for pair in range(2):
            cols += list(range(OFFS[pre + "v"] + pair * 128, OFFS[pre + "v"] + (pair + 1) * 128))
    return np.asarray(cols)


def out_row_index():
    rows = []
    for ci in range(6):
        rows += _qcols(0, ci)
    rows += list(range(768, 1280))
    for ci in range(6):
        rows += _qcols(1280, ci)
    return np.asarray(rows)


def lay_wfm(w_in):
    w = w_in[:, in_col_index()]
    a = w.reshape(DC, 128, NCH_IN, 128)
    return np.ascontiguousarray(a.transpose(2, 1, 0, 3))


def lay_wout(w_out):
    w = w_out[out_row_index(), :]
    a = w.reshape(DC, 128, DC, 128)
    return np.ascontiguousarray(a.transpose(2, 1, 0, 3))


def t5_bucket_np(rel):
    nb, max_exact = 16, 8
    n = np.abs(rel)
    nf = np.maximum(n, 1).astype(np.float32)
    val = np.log(nf / np.float32(max_exact)) / np.float32(np.log(128 / 8)) * np.float32(nb - max_exact)
    large = max_exact + val.astype(np.int32)
    large = np.minimum(large, nb - 1)
    return np.where(rel > 0, nb, 0) + np.where(n < max_exact, n, large)


def make_biasT(rel_bias):
    kj = np.arange(128)[:, None, None]
    dd = np.arange(3)[None, :, None]
    qi = np.arange(128)[None, None, :]
    rel = (1 - dd) * 128 + kj - qi
    valid = np.abs(rel) <= 128
    bk = t5_bucket_np(rel)
    tab = rel_bias.astype(np.float32)[bk]
    tab = np.where(valid[..., None], tab, np.float32(-30000.0))
    return np.ascontiguousarray(tab.transpose(3, 0, 1, 2).reshape(12, 128, 384)).astype(np.float32)


def make_consts():
    s = np.arange(128)[:, None]
    t = np.arange(128)[None, :]
    same = (s // 16) == (t // 16)
    cB = np.zeros((128, NCB), np.float32)
    cB[:, 0:128] = (same & (s <= t))
    cB[:, 128:256] = (same & (s >= t))
    cB[:, 256:384] = np.eye(128)
    cB[:, 384:392] = (np.arange(128)[:, None] // 16) == np.arange(8)[None, :]
    global CMQ
    CMQ = np.ascontiguousarray(np.broadcast_to(((np.arange(128)[None, :] // 16) == np.arange(8)[:, None]).astype(np.float32)[None], (128, 8, 128))).astype(ml_dtypes.bfloat16)
    cC = np.zeros((128, NCC), np.float32)
    for i in range(64):
        cC[2 * i + 1, 2 * i] = -1.0
        cC[2 * i, 2 * i + 1] = 1.0
    cC[:, 128:256] = (s // 64) == (t // 64)
    half = 32
    inv = (1.0 / (np.float32(10000.0) ** (np.arange(0, half, 2, dtype=np.float32) / np.float32(half)))).astype(np.float32)
    pos = np.arange(SEQ)
    row_ang = (pos // 64).astype(np.float32)[:, None] * inv[None, :]
    col_ang = (pos % 64).astype(np.float32)[:, None] * inv[None, :]
    ang = np.concatenate([row_ang, col_ang], axis=1).astype(np.float32)
    idx = (np.arange(128) % 64) // 2
    cs = np.stack([np.cos(ang)[:, idx].T, np.sin(ang)[:, idx].T], axis=1).astype(np.float32)
    cs_cores = [np.ascontiguousarray(cs[:, :, c * NT:(c + 1) * NT]) for c in range(NCORES)]
    return cB, cC, cs_cores


def make_cst(l, core, norm_w, hgrn_lb, qk_norm_w, hgrn_norm_w, sink_logits):
    cst = np.zeros((128, NCST), np.float32)
    cst[:, O_NW:O_NW + 96] = norm_w[l].reshape(6, DC, 128).transpose(2, 0, 1).reshape(128, 96)
    cst[:, O_LBR:O_LBR + 32] = hgrn_lb.reshape(4, 2, 4, 128).transpose(3, 0, 1, 2).reshape(128, 32)
    for l2 in range(4):
        cst[:, O_LMASK + l2] = 1.0 if (1 <= l2 <= l) else 0.0
    cst[:, O_QKW] = np.tile(qk_norm_w[l, 0], 2)
    cst[:, O_QKW + 1] = np.tile(qk_norm_w[l, 1], 2)
    cst[:, O_GNW] = hgrn_norm_w[l]
    cst[:, O_SINK:O_SINK + 12] = sink_logits[l][None, :]
    for c2 in range(NCORES):
        mf = 1.0 if c2 < core else 0.0
        mb = 1.0 if c2 > core else 0.0
        cst[:, O_CMASK + c2] = mf
        cst[:, O_CMASK + 8 + c2] = 1.0 - mf
        cst[:, O_CMASK + 16 + c2] = mb
        cst[:, O_CMASK + 24 + c2] = 1.0 - mb
    cst[:, O_FLAGS] = 1.0 if core > 0 else 0.0
    cst[:, O_FLAGS + 1] = 1.0 if core < NCORES - 1 else 0.0
    return cst


def v_ext(v):
    out = np.zeros(v.shape[:-1] + (2 * VXW,), v.dtype)
    for pair in range(2):
        b = pair * VXW
        out[..., b:b + 64] = v[..., (2 * pair) * 64:(2 * pair + 1) * 64]
        out[..., b + 64] = 1
        out[..., b + 65] = 1
        out[..., b + 128:b + 192] = v[..., (2 * pair + 1) * 64:(2 * pair + 2) * 64]
    return out


def exchange_layout(r1):
    b = np.stack([r["bst_out"] for r in r1], axis=0)
    b = b.reshape(NCORES, 128, 4, 2, 129).transpose(2, 1, 0, 3, 4)
    kC_all = np.ascontiguousarray(np.concatenate([r["kC_out"] for r in r1], axis=2))
    v = np.concatenate([r["vC_out"] for r in r1], axis=1)
    ve = v_ext(v).reshape(128, SEQ // 128, 2, VXW).transpose(0, 2, 1, 3)
    return np.ascontiguousarray(b), kC_all, np.ascontiguousarray(ve)


def _run(name, in_maps):
    import os
    if name not in _PROGS:
        _PROGS[name] = build_stage(name)
    res = run_bass_kernel_spmd(_PROGS[name], in_maps, core_ids=list(range(NCORES)), trace=bool(os.environ.get("KTRACE")))
    if os.environ.get("KTRACE"):
        print("exec_time_ns", name, res.exec_time_ns)
    return res.results


def run_layer(l, xT, inp, consts, debug=None):
    cB, cC, cs_cores = consts
    wfm = lay_wfm(inp["w_in"][l])
    wout = lay_wout(inp["w_out"][l])
    gu1, d1 = lay_gu(inp["ffn1_gate"][l], inp["ffn1_up"][l]), lay_d(inp["ffn1_down"][l])
    csts = [make_cst(l, c, inp["norm_w"], inp["hgrn_lb"], inp["qk_norm_w"], inp["hgrn_norm_w"], inp["sink_logits"])
            for c in range(NCORES)]
    in1 = [dict(xin=xT[c], wgu=gu1, wd=d1, wfm=wfm, cst=csts[c], cB=cB, cC=cC, cs=cs_cores[c], cmq=CMQ) for c in range(NCORES)]
    r1 = _run("S1", in1)
    del gu1, d1, in1
    bst_all, kC_all, vC_all = exchange_layout(r1)
    zk = np.zeros((128, 2, 128), r1[0]["kA_out"].dtype)
    zv = np.zeros((128, 256), r1[0]["vA_out"].dtype)
    gu2, d2 = lay_gu(inp["ffn2_gate"][l], inp["ffn2_up"][l]), lay_d(inp["ffn2_down"][l])
    biasT = make_biasT(inp["rel_bias"])
    in2 = []
    for c in range(NCORES):
        kp = r1[c - 1]["kA_out"][:, :, NT - 128:] if c > 0 else zk
        kn = r1[c + 1]["kA_out"][:, :, :128] if c < NCORES - 1 else zk
        vp = r1[c - 1]["vA_out"][:, 7, :] if c > 0 else zv
        vn = r1[c + 1]["vA_out"][:, 0, :] if c < NCORES - 1 else zv
        in2.append(dict(xin=r1[c]["xout"], wgu=gu2, wd=d2, wfm=wfm, wout=wout, cst=csts[c], cB=cB, cC=cC, cs=cs_cores[c], cmq=CMQ,
                        bst_all=bst_all, kC_all=kC_all, vC_all=vC_all,
                        kA_halo=np.ascontiguousarray(np.stack([kp, kn], axis=2)),
                        vA_halo=np.ascontiguousarray(np.stack([v_ext(vp), v_ext(vn)], axis=1)), biasT=biasT))
    if debug is not None:
        debug["r1"] = r1
    r2 = _run("S2", in2)
    if debug is not None:
        debug["r2"] = r2
    return [r["xout"] for r in r2]


def kernel(**inputs):
    inp = {k: np.asarray(v) for k, v in inputs.items()}
    x = inp["x"][0]
    xT = [np.ascontiguousarray(x[c * NT:(c + 1) * NT].T) for c in range(NCORES)]
    consts = make_consts()
    for l in range(DEPTH):
        xT = run_layer(l, xT, inp, consts)
    y = np.concatenate([o.T for o in xT], axis=0)
    return np.ascontiguousarray(y[None]).astype(np.float32)
```

```python
import contextlib
import os
import numpy as np
import concourse.bass as bass
import concourse.mybir as mybir
from concourse.bass_utils import run_bass_kernel_spmd

F32 = mybir.dt.float32
BF16 = mybir.dt.bfloat16
AF = mybir.ActivationFunctionType
ALU = mybir.AluOpType

NCORES = 8
D = 2048
SEQ = 8192
NT = SEQ // NCORES
DC = D // 128
DFF = 5632
FC = DFF // 128
EPS = 1e-6
DEPTH = 4


class Prog:
    COMPUTE = ("pe", "act", "dve", "pool")
    DMAQ = ("sp", "pool", "act")
    RING = 6

    def __init__(self, nc):
        self.nc = nc
        self.ops = []

    def op(self, eng, fn, reads=(), writes=()):
        self.ops.append(dict(kind="c", eng=eng, fn=fn, reads=tuple(reads), writes=tuple(writes)))

    def dma(self, q, fn, reads=(), writes=()):
        self.ops.append(dict(kind="d", eng=q, fn=fn, reads=tuple(reads), writes=tuple(writes)))

    def barrier(self):
        self.ops.append(dict(kind="b"))

    def emit(self, stack):
        nc = self.nc
        ops = self.ops
        last_w = {}
        readers = {}
        seq = {e: 0 for e in self.COMPUTE}
        dseq = {q: 0 for q in self.DMAQ}
        last_c = {}
        last_d = {q: [] for q in self.DMAQ}
        pending = {}
        for i, o in enumerate(ops):
            if o["kind"] == "b":
                bd = set(last_c.values())
                for q in self.DMAQ:
                    bd.update(last_d[q][-self.RING:])
                for st in ("pe", "act", "dve", "pool", "sp"):
                    pending[st] = set(bd)
                last_w, readers = {}, {}
                o["deps"] = set()
                continue
            deps = set()
            if o["eng"] in pending:
                deps.update(pending.pop(o["eng"]))
            if o["kind"] == "c":
                last_c[o["eng"]] = i
            else:
                last_d[o["eng"]].append(i)
            for r in o["reads"]:
                if r in last_w:
                    deps.add(last_w[r])
            for w in o["writes"]:
                if w in last_w:
                    deps.add(last_w[w])
                for rd in readers.get(w, ()):
                    deps.add(rd)
            deps.discard(i)
            best = {}
            red = set()
            for d_ in deps:
                od = ops[d_]
                if od["kind"] == "c":
                    if od["eng"] not in best or best[od["eng"]] < d_:
                        best[od["eng"]] = d_
                else:
                    red.add(d_)
            red.update(best.values())
            o["deps"] = red
            for w in o["writes"]:
                last_w[w] = i
                readers[w] = []
            for r in o["reads"]:
                if r not in o["writes"]:
                    readers.setdefault(r, []).append(i)
            if o["kind"] == "c":
                o["seq"] = seq[o["eng"]]
                seq[o["eng"]] += 1
            else:
                o["dseq"] = dseq[o["eng"]]
                dseq[o["eng"]] += 1
            o["signal"] = False
        ops_all = ops
        for i, o in enumerate(ops):
            if o["kind"] == "b":
                continue
            for d in o["deps"]:
                od = ops[d]
                if od["kind"] == "c":
                    if od["eng"] == o["eng"] and o["kind"] == "c" and od["eng"] == "pe":
                        continue
                    od["signal"] = True
        cnt = {e: 0 for e in self.COMPUTE}
        for o in ops:
            if o["kind"] == "b":
                continue
            if o["kind"] == "c":
                if o["signal"]:
                    cnt[o["eng"]] += 1
                    o["cnt"] = cnt[o["eng"]]
        csem = {e: stack.enter_context(nc.semaphore("s_" + e)) for e in self.COMPUTE}
        dsem = {q: [stack.enter_context(nc.semaphore("d_%s%d" % (q, k))) for k in range(self.RING)]
                for q in self.DMAQ if dseq[q] > 0}
        streams = {e: [] for e in ("pe", "act", "dve", "pool", "sp")}
        known = {e: {} for e in streams}

        def need(stream, sem, val, lst):
            k = known[stream]
            if k.get(sem.name if hasattr(sem, "name") else id(sem), 0) >= val:
                return
            k[sem.name if hasattr(sem, "name") else id(sem)] = val
            lst.append((sem, val))

        for i, o in enumerate(ops):
            if o["kind"] == "b":
                continue
            st = o["eng"]
            waits = []
            for d in sorted(o["deps"]):
                od = ops[d]
                if od["kind"] == "c":
                    if not od["signal"]:
                        continue
                    if od["eng"] == st and o["kind"] == "c" and st == "pe":
                        continue
                    need(st, csem[od["eng"]], od["cnt"], waits)
                else:
                    q = od["eng"]
                    j = od["dseq"]
                    need(st, dsem[q][j % self.RING], 16 * (j // self.RING + 1), waits)
            inc = None
            if o["kind"] == "d":
                j = o["dseq"]
                if j >= self.RING:
                    need(st, dsem[st][j % self.RING], 16 * (j // self.RING), waits)
                inc = (dsem[st][j % self.RING], 16)
            elif o["signal"]:
                inc = (csem[st], 1)
            streams[st].append((waits, o["fn"], inc))
        fin = []
        for q in dsem:
            n = dseq[q]
            for k in range(self.RING):
                m = (n - k + self.RING - 1) // self.RING if n > k else 0
                if m > 0:
                    need("sp", dsem[q][k], 16 * m, fin)
        streams["sp"].append((fin, None, None))

        self.stats = {e: len(v) for e, v in streams.items()}
        self.stats["signals"] = dict(cnt)

        def run_stream(eng, lst):
            for waits, fn, inc in lst:
                for sem, val in waits:
                    eng.wait_ge(sem, val)
                if fn is None:
                    continue
                ins = fn(eng)
                if inc is not None:
                    ins.then_inc(inc[0], inc[1])

        with nc.Block() as block:
            @block.tensor
            def _(e):
                run_stream(e, streams["pe"])

            @block.scalar
            def _(e):
                run_stream(e, streams["act"])

            @block.vector
            def _(e):
                run_stream(e, streams["dve"])

            @block.gpsimd
            def _(e):
                run_stream(e, streams["pool"])

            @block.sync
            def _(e):
                run_stream(e, streams["sp"])


class Ctx:
    pass


def alloc_common(nc, stack, P):
    C = Ctx()
    C.nc, C.P = nc, P
    sb = lambda name, shape, dt: stack.enter_context(nc.sbuf_tensor(name, shape, dt))
    C.sb = sb
    C.xT = sb("xT", [128, DC, NT], F32)
    C.arH = sb("arH", [128, DC * NT // 2], F32)
    C.arB = sb("arB", [128, FC * NT // 2], F32)
    C.arW = sb("arW", [128, 4096], F32)
    C.hT = C.arH[:, :].bitcast(BF16).rearrange("p (c t) -> p c t", c=DC)
    C.ones = sb("ones", [128, 128], F32)
    C.psum = [stack.enter_context(nc.psum_tensor("ps%d" % i, [128, 512], F32)) for i in range(8)]
    C.sq = [sb("sq%d" % i, [128, 512], F32) for i in range(2)]
    C.rstd = sb("rstd", [128, 512], F32)
    P.op("pool", lambda e: e.memset(C.ones[:, :], 1.0), writes=["ones"])
    C.ctr = {}
    return C


def rr(C, name, n):
    v = C.ctr.get(name, 0)
    C.ctr[name] = v + 1
    return v % n


def emit_rms_stats(C, src_fn, src_keys, half, pbank, post_scale=None):
    P = C.P
    ps = C.psum[pbank]
    for c in range(DC):
        s = rr(C, "sq", 2)
        sq = C.sq[s]
        P.op("act", lambda e, sq=sq, c=c: e.activation(out=sq[:, :], in_=src_fn(c), func=AF.Square),
             reads=list(src_keys(c)), writes=["sq%d" % s])
        P.op("pe", lambda e, sq=sq, c=c: e.matmul(ps[:, :], lhsT=C.ones[:, :], rhs=sq[:, :],
                                                  start=(c == 0), stop=(c == DC - 1)),
             reads=["sq%d" % s, "ones"], writes=["ps%d" % pbank])
    P.op("dve", lambda e: e.tensor_scalar(out=C.rstd[:, :], in0=ps[:, :], scalar1=1.0 / D, scalar2=EPS,
                                          op0=ALU.mult, op1=ALU.add),
         reads=["ps%d" % pbank], writes=["rstd"])
    P.op("act", lambda e: e.activation(out=C.rstd[:, :], in_=C.rstd[:, :], func=AF.Sqrt),
         reads=["rstd"], writes=["rstd"])
    P.op("dve", lambda e: e.reciprocal(out=C.rstd[:, :], in_=C.rstd[:, :]),
         reads=["rstd"], writes=["rstd"])
    if post_scale is not None:
        P.op("dve", lambda e: e.tensor_scalar(out=C.rstd[:, :], in0=C.rstd[:, :], scalar1=post_scale, scalar2=None,
                                              op0=ALU.mult),
             reads=["rstd"], writes=["rstd"])


def emit_prenorm(C, nw, nwkey):
    P = C.P
    for half in range(2):
        ts = slice(half * 512, (half + 1) * 512)
        emit_rms_stats(C, lambda c, ts=ts: C.xT[:, c, ts], lambda c, half=half: [("xT", c, half)], half, 7)
        for c in range(DC):
            P.op("dve", lambda e, c=c, ts=ts: e.scalar_tensor_tensor(
                out=C.hT[:, c, ts], in0=C.xT[:, c, ts], scalar=nw[:, c:c + 1], in1=C.rstd[:, :],
                op0=ALU.mult, op1=ALU.mult),
                reads=[("xT", c, half), "rstd", nwkey], writes=[("H", c, half)])


def emit_ffn(C, wgu, wd, nw_pre, nw_post, nwkey, F):
    P, nc = C.P, C.nc
    emit_prenorm(C, nw_pre, nwkey)
    for j in range(FC):
        s = rr(C, "wgu", 2)
        wt = F.wgu[s]
        P.dma("pool", lambda e, wt=wt, j=j: e.dma_start(out=wt[:, :, :, :], in_=wgu[j]),
              writes=[("wgu", s)] + ([("wd", i) for i in range(5)] if j < 2 else []))
        pb = (j % 2) * 4
        for c in range(DC):
            for g in range(2):
                for half in range(2):
                    ts = slice(half * 512, (half + 1) * 512)
                    b = pb + g * 2 + half
                    P.op("pe", lambda e, wt=wt, c=c, g=g, ts=ts, b=b: e.matmul(
                        C.psum[b][:, :], lhsT=wt[:, g, c, :], rhs=C.hT[:, c, ts],
                        start=(c == 0), stop=(c == DC - 1)),
                        reads=[("wgu", s), ("H", c, half)], writes=["ps%d" % b])
        for half in range(2):
            ts = slice(half * 512, (half + 1) * 512)
            bg, bu = pb + half, pb + 2 + half
            k = rr(C, "sq", 2)
            sil = F.sil[k]
            P.op("act", lambda e, sil=sil, bg=bg: e.activation(out=sil[:, :], in_=C.psum[bg][:, :], func=AF.Silu),
                 reads=["ps%d" % bg], writes=["sq%d" % k])
            P.op("dve", lambda e, sil=sil, bu=bu, j=j, ts=ts: e.tensor_tensor(
                out=F.hid[:, j, ts], in0=C.psum[bu][:, :], in1=sil[:, :], op=ALU.mult),
                reads=["ps%d" % bu, "sq%d" % k], writes=[("hid", j, half)])
    KH = FC // 4
    for half in range(2):
        ts = slice(half * 512, (half + 1) * 512)
        for m in range(DC):
            b = m % 2
            for kh in range(4):
                s = rr(C, "wd", 5)
                wt = F.wd[s]
                P.dma("pool", lambda e, wt=wt, m=m, kh=kh: e.dma_start(
                    out=wt[:, :, :], in_=wd[m, :, kh * KH:(kh + 1) * KH, :]),
                    writes=[("wd", s)] + ([("wgu", 0), ("wgu", 1)] if (half == 0 and m < 2) else []))
                for kk in range(KH):
                    k = kh * KH + kk
                    P.op("pe", lambda e, wt=wt, kk=kk, k=k, b=b, ts=ts: e.matmul(
                        C.psum[b][:, :], lhsT=wt[:, kk, :], rhs=F.hid[:, k, ts],
                        start=(k == 0), stop=(k == FC - 1)),
                        reads=[("wd", s), ("hid", k, half)], writes=["ps%d" % b])
            P.op("act", lambda e, m=m, b=b: e.activation(out=F.yT[:, m, :], in_=C.psum[b][:, :], func=AF.Copy),
                 reads=["ps%d" % b], writes=[("H", m, 0), ("H", m, 1)])
        emit_rms_stats(C, lambda c: F.yT[:, c, :], lambda c: [("H", c, 0), ("H", c, 1)], half, 7, post_scale=0.5)
        for c in range(DC):
            P.op("dve", lambda e, c=c: e.scalar_tensor_tensor(
                out=F.yT[:, c, :], in0=F.yT[:, c, :], scalar=nw_post[:, c:c + 1], in1=C.rstd[:, :],
                op0=ALU.mult, op1=ALU.mult),
                reads=[("H", c, 0), ("H", c, 1), "rstd", nwkey], writes=[("H", c, 0), ("H", c, 1)])
            P.op("pool", lambda e, c=c, ts=ts: e.tensor_tensor(
                out=C.xT[:, c, ts], in0=F.yT[:, c, :], in1=C.xT[:, c, ts], op=ALU.add),
                reads=[("H", c, 0), ("H", c, 1), ("xT", c, half)], writes=[("xT", c, half)])


def alloc_ffn(C, stack):
    F = Ctx()
    F.hid = C.arB[:, :].bitcast(BF16).rearrange("p (k t) -> p k t", k=FC)
    wb = C.arW[:, :].bitcast(BF16)
    F.wgu = [wb[:, i * 4096:(i + 1) * 4096].rearrange("p (g c n) -> p g c n", g=2, c=DC) for i in range(2)]
    n = (FC // 4) * 128
    F.wd = [wb[:, i * n:(i + 1) * n].rearrange("p (k n) -> p k n", n=128) for i in range(5)]
    F.sil = C.sq
    F.yT = C.arH[:, :].rearrange("p (c t) -> p c t", c=DC)
    return F


def build_ffn_prog():
    nc = bass.Bass("TRN2", target_bir_lowering=False)
    xin = nc.dram_tensor("xin", [D, NT], F32, kind="ExternalInput").ap()
    wgu = nc.dram_tensor("wgu", [FC, 128, 2, DC, 128], F32, kind="ExternalInput").ap()
    wd = nc.dram_tensor("wd", [DC, 128, FC, 128], F32, kind="ExternalInput").ap()
    nwd = nc.dram_tensor("nw", [128, 2, DC], F32, kind="ExternalInput").ap()
    xout = nc.dram_tensor("xout", [D, NT], F32, kind="ExternalOutput").ap()
    with contextlib.ExitStack() as stack:
        P = Prog(nc)
        C = alloc_common(nc, stack, P)
        F = alloc_ffn(C, stack)
        nw = C.sb("nw_sb", [128, 2, DC], F32)
        P.dma("sp", lambda e: e.dma_start(out=nw[:, :, :], in_=nwd), writes=["nw"])
        for c in range(DC):
            P.dma("sp", lambda e, c=c: e.dma_start(out=C.xT[:, c, :], in_=xin[c * 128:(c + 1) * 128, :]),
                  writes=[("xT", c, 0), ("xT", c, 1)])
        emit_ffn(C, wgu, wd, nw[:, 0, :], nw[:, 1, :], "nw", F)
        for c in range(DC):
            P.dma("sp", lambda e, c=c: e.dma_start(out=xout[c * 128:(c + 1) * 128, :], in_=C.xT[:, c, :]),
                  reads=[("xT", c, 0), ("xT", c, 1)])
        P.emit(stack)
        print("prog stats", P.stats)
    return nc


def lay_gu(wg, wu):
    a = np.stack([wg, wu], axis=0).reshape(2, DC, 128, FC, 128)
    return np.ascontiguousarray(a.transpose(3, 2, 0, 1, 4))


def lay_d(wd):
    a = wd.reshape(FC, 128, DC, 128)
    return np.ascontiguousarray(a.transpose(2, 1, 0, 3))


def lay_nw(w):
    return np.ascontiguousarray(w.reshape(DC, 128).T)


_PROGS = {}
CMQ = None


def run_ffn(xT_shards, wg, wu, wd, nw_pre, nw_post):
    if "ffn" not in _PROGS:
        _PROGS["ffn"] = build_ffn_prog()
    nc = _PROGS["ffn"]
    gu = lay_gu(wg, wu)
    dd = lay_d(wd)
    nw = np.ascontiguousarray(np.stack([lay_nw(nw_pre), lay_nw(nw_post)], axis=1))
    in_maps = [{"xin": xT_shards[c], "wgu": gu, "wd": dd, "nw": nw} for c in range(NCORES)]
    import os
    res = run_bass_kernel_spmd(nc, in_maps, core_ids=list(range(NCORES)), trace=bool(os.environ.get("KTRACE")))
    if os.environ.get("KTRACE"):
        print("exec_time_ns", res.exec_time_ns)
    return [r["xout"] for r in res.results]


AX = mybir.AxisListType
NCH_IN = 40
O_NW, O_LBR, O_LMASK, O_QKW, O_GNW, O_SINK, O_CMASK, O_FLAGS = 0, 96, 128, 132, 134, 135, 147, 179
NCST = 181
NCB = 392
NCC = 256


def carve(ar, off, nbytes, dt, pattern=None, **kw):
    assert off % 4 == 0 and nbytes % 4 == 0
    v = ar[:, off // 4:(off + nbytes) // 4]
    if dt == BF16:
        v = v.bitcast(BF16)
    if pattern:
        v = v.rearrange(pattern, **kw)
    return v


def emit_consts(C, X):
    P = C.P
    C.cst = C.sb("cst_sb", [128, NCST], F32)
    C.der = C.sb("der_sb", [128, 96], F32)
    P.dma("sp", lambda e: e.dma_start(out=C.cst[:, :], in_=X["cst"]), writes=["cst"])
    cst, der = C.cst, C.der
    P.op("act", lambda e: e.activation(out=der[:, 0:32], in_=cst[:, O_LBR:O_LBR + 32], func=AF.Exp),
         reads=["cst"], writes=["der"])
    ev = der[:, 0:32].rearrange("p (l k) -> p l k", l=4)
    tot, part = der[:, 32:40], der[:, 40:48]
    P.op("dve", lambda e: e.tensor_tensor(out=tot, in0=ev[:, 0, :], in1=ev[:, 1, :], op=ALU.add), reads=["der"], writes=["der"])
    P.op("dve", lambda e: e.tensor_tensor(out=tot, in0=tot, in1=ev[:, 2, :], op=ALU.add), reads=["der"], writes=["der"])
    P.op("dve", lambda e: e.tensor_tensor(out=tot, in0=tot, in1=ev[:, 3, :], op=ALU.add), reads=["der"], writes=["der"])
    P.op("dve", lambda e: e.tensor_scalar(out=part, in0=ev[:, 0, :], scalar1=cst[:, O_LMASK:O_LMASK + 1], scalar2=None,
                                          op0=ALU.mult), reads=["der", "cst"], writes=["der"])
    for l in range(1, 4):
        P.op("dve", lambda e, l=l: e.scalar_tensor_tensor(out=part, in0=ev[:, l, :], scalar=cst[:, O_LMASK + l:O_LMASK + l + 1],
                                                          in1=part, op0=ALU.mult, op1=ALU.add),
             reads=["der", "cst"], writes=["der"])
    P.op("dve", lambda e: e.reciprocal(out=tot, in_=tot), reads=["der"], writes=["der"])
    C.lb, C.oml, C.noml = der[:, 48:56], der[:, 56:64], der[:, 64:72]
    P.op("dve", lambda e: e.tensor_tensor(out=C.lb, in0=part, in1=tot, op=ALU.mult), reads=["der"], writes=["der"])
    P.op("dve", lambda e: e.tensor_scalar(out=C.oml, in0=C.lb, scalar1=-1.0, scalar2=1.0, op0=ALU.mult, op1=ALU.add),
         reads=["der"], writes=["der"])
    P.op("dve", lambda e: e.tensor_scalar(out=C.noml, in0=C.oml, scalar1=-1.0, scalar2=None, op0=ALU.mult),
         reads=["der"], writes=["der"])
    C.esink = der[:, 72:84]
    P.op("act", lambda e: e.activation(out=C.esink, in_=cst[:, O_SINK:O_SINK + 12], func=AF.Exp),
         reads=["cst", "der"], writes=["der"])
    C.nw = cst[:, O_NW:O_NW + 96].rearrange("p (i c) -> p i c", i=6)


class WStream:
    def __init__(self, C, nslot=4):
        self.C = C
        wb = C.arW[:, :].bitcast(BF16)
        self.slots = [wb[:, i * 2048:(i + 1) * 2048].rearrange("p (c n) -> p c n", c=DC) for i in range(nslot)]
        self.n = nslot
        self.i = 0

    def load(self, src):
        s = self.i % self.n
        self.i += 1
        t = self.slots[s]
        self.C.P.dma("pool", lambda e: e.dma_start(out=t[:, :, :], in_=src), writes=[("ws", s)])
        return t, ("ws", s)


def proj_fm(C, W, src, b0=0):
    P = C.P
    t, key = W.load(src)
    for c in range(DC):
        for half in range(2):
            ts = slice(half * 512, (half + 1) * 512)
            P.op("pe", lambda e, c=c, ts=ts, half=half: e.matmul(
                C.psum[b0 + half][:, :], lhsT=t[:, c, :], rhs=C.hT[:, c, ts], start=(c == 0), stop=(c == DC - 1)),
                reads=[key, ("H", c, half)], writes=["ps%d" % (b0 + half)])


def proj_tm(C, W, src, b0=2):
    P = C.P
    t, key = W.load(src)
    for tile in range(8):
        b = b0 + tile // 4
        cs = slice((tile % 4) * 128, (tile % 4 + 1) * 128)
        half = tile // 4
        for c in range(DC):
            P.op("pe", lambda e, c=c, b=b, cs=cs, tile=tile: e.matmul(
                C.psum[b][:, cs], lhsT=C.hT[:, c, tile * 128:(tile + 1) * 128], rhs=t[:, c, :],
                start=(c == 0), stop=(c == DC - 1)),
                reads=[key, ("H", c, half)], writes=["ps%d" % b])


def emit_B(C, X, W, final):
    P, nc = C.P, C.nc
    arB = C.arB
    base = 8192
    off = [base]

    def take(nbytes, dt, pattern=None, **kw):
        v = carve(arB, off[0], nbytes, dt, pattern, **kw)
        off[0] += nbytes
        return v

    C.yB = carve(arB, 0, 8192, BF16, "p (h t) -> p h t", h=4)
    cB = take(NCB * 4, F32)
    qf = take(4096, F32)
    gs = take(4096, F32)
    Vt = take(2048, BF16, "p (a n) -> p a n", a=8)
    Vbd = [take(2048, BF16, "p (j n) -> p j n", j=8) for _ in range(2)]
    osb = take(4096, F32)
    sig = take(4096, F32)
    logf = take(4096, F32)
    kk = take(4096, F32)
    pa = take(4096, F32)
    pb = take(4096, F32)
    ex = [take(4096, F32) for _ in range(2)]
    Qd = take(2048, BF16)
    Kd = take(2048, BF16)
    Ke = take(2048, BF16)
    KeT = take(2048, BF16, "p (a n) -> p a n", a=8)
    attm = [take(256, BF16) for _ in range(2)]
    S = [take(512, F32) for _ in range(2)]
    Sbf = [take(2048, BF16, "p (j n) -> p j n", j=8) for _ in range(2)]
    dch = take(256, F32)
    tsum = take(32, F32)
    identb = take(256, BF16)
    bst_sb = take(8 * 129 * 4, F32, "p (k n) -> p k n", k=8) if not final else None
    bsin = take(8 * 2 * 129 * 4, F32, "p (c d n) -> p c d n", c=8, d=2) if final else None
    stmp = take(512, F32)
    deff = take(32, F32)
    ytmp = take(2048, F32)
    cmq = take(2048, BF16, "p (j t) -> p j t", j=8)
    Qdm = [take(2048, BF16, "p (j t) -> p j t", j=8) for _ in range(2)]
    assert off[0] <= 90112, off[0]
    P.dma("sp", lambda e: e.dma_start(out=cmq[:, :, :], in_=X["cmq"]), writes=["cmq"])

    P.dma("sp", lambda e: e.dma_start(out=cB[:, :], in_=X["cB"]), writes=["cB"])
    P.op("pool", lambda e: e.tensor_copy(out=identb[:, :], in_=cB[:, 256:384]), reads=["cB"], writes=["identb"])
    maskT = [cB[:, 0:128], cB[:, 128:256]]
    cmk = cB[:, 384:392]
    v3 = lambda a: a[:, :].rearrange("p (j s) -> p j s", s=16)
    cst = C.cst

    for h in range(4):
        wbase = 5 * h
        proj_tm(C, W, X["wfm"][wbase + 0], b0=2)
        for bb in range(2):
            P.op("act", lambda e, bb=bb: e.activation(
                out=Vt[:, 4 * bb:4 * bb + 4, :], in_=C.psum[2 + bb][:, :].rearrange("p (a n) -> p a n", a=4), func=AF.Copy),
                reads=["ps%d" % (2 + bb)], writes=["Vt"])
        if final:
            proj_fm(C, W, X["wfm"][wbase + 1], b0=0)
            for half in range(2):
                ts = slice(half * 512, (half + 1) * 512)
                P.op("act", lambda e, half=half, ts=ts: e.activation(out=qf[:, ts], in_=C.psum[half][:, :], func=AF.Silu),
                     reads=["ps%d" % half], writes=[("qf", half)])
            P.dma("sp", lambda e, h=h: e.dma_start(out=bsin[:, :, :, :], in_=X["bst_all"][h]), writes=["bsin"])
        for dirn in range(2):
            hd = 2 * h + dirn
            lbi = dirn * 4 + h
            proj_fm(C, W, X["wfm"][wbase + 2 + dirn], b0=0)
            for half in range(2):
                ts = slice(half * 512, (half + 1) * 512)
                P.op("act", lambda e, half=half, ts=ts: e.activation(out=sig[:, ts], in_=C.psum[half][:, :], func=AF.Sigmoid),
                     reads=["ps%d" % half], writes=[("sig", half)])
                P.op("act", lambda e, ts=ts, lbi=lbi: e.activation(out=logf[:, ts], in_=sig[:, ts], func=AF.Ln,
                                                                   bias=C.lb[:, lbi:lbi + 1], scale=C.oml[:, lbi:lbi + 1]),
                     reads=[("sig", half), "der"], writes=[("logf", half)])
                P.op("dve", lambda e, ts=ts, lbi=lbi: e.tensor_scalar(out=kk[:, ts], in0=sig[:, ts], scalar1=C.noml[:, lbi:lbi + 1],
                                                                      scalar2=C.oml[:, lbi:lbi + 1], op0=ALU.mult, op1=ALU.add),
                     reads=[("sig", half), "der"], writes=[("kk", half)])
            src, skey = logf, [("logf", 0), ("logf", 1)]
            dsts = [(pa, "pa"), (pb, "pb"), (pa, "pa"), (pb, "pb")]
            for si, sft in enumerate((1, 2, 4, 8)):
                dst, dkey = dsts[si]
                P.op("dve", lambda e, src=src, dst=dst, sft=sft: e.tensor_tensor(
                    out=v3(dst)[:, :, sft:16], in0=v3(src)[:, :, sft:16], in1=v3(src)[:, :, 0:16 - sft], op=ALU.add),
                    reads=skey, writes=[dkey])
                P.op("pool", lambda e, src=src, dst=dst, sft=sft: e.tensor_copy(
                    out=v3(dst)[:, :, 0:sft], in_=v3(src)[:, :, 0:sft]),
                    reads=skey, writes=[dkey + "c"])
                src, skey = dst, [dkey, dkey + "c"]
            PIN = ["pb", "pbc"]
            Tb = v3(pb)[:, :, 15:16].to_broadcast([128, 64, 16])
            P.op("act", lambda e: e.activation(out=dch[:, :].rearrange("p (j o) -> p j o", o=1), in_=v3(pb)[:, :, 15:16], func=AF.Exp),
                 reads=PIN, writes=["dch"])
            if dirn == 0:
                bsrc, bkey = pb, PIN
            else:
                P.op("dve", lambda e: e.tensor_tensor(out=v3(pa), in0=Tb, in1=v3(pb), op=ALU.subtract),
                     reads=PIN + ["pa", "pac"], writes=["pa", "pac"])
                P.op("dve", lambda e: e.tensor_tensor(out=pa[:, :], in0=pa[:, :], in1=logf[:, :], op=ALU.add),
                     reads=["pa", "pac", ("logf", 0), ("logf", 1)], writes=["pa", "pac"])
                bsrc, bkey = pa, ["pa", "pac"]
            if final:
                P.op("act", lambda e, bsrc=bsrc: e.activation(out=ex[0][:, :], in_=bsrc[:, :], func=AF.Exp),
                     reads=bkey, writes=["ex0"])
                P.op("dve", lambda e: e.tensor_tensor(out=Qd[:, :], in0=qf[:, :], in1=ex[0][:, :], op=ALU.mult),
                     reads=["ex0", ("qf", 0), ("qf", 1)], writes=["Qd"])
                P.op("act", lambda e, bsrc=bsrc: e.activation(out=ex[1][:, :], in_=bsrc[:, :], func=AF.Exp, scale=-1.0),
                     reads=bkey, writes=["ex1"])
                P.op("dve", lambda e: e.tensor_tensor(out=Kd[:, :], in0=kk[:, :], in1=ex[1][:, :], op=ALU.mult),
                     reads=["ex1", ("kk", 0), ("kk", 1)], writes=["Kd"])
            if dirn == 0:
                P.op("dve", lambda e: e.tensor_tensor(out=v3(pa), in0=Tb, in1=v3(pb), op=ALU.subtract),
                     reads=PIN + ["pa", "pac"], writes=["pa", "pac"])
                esrc, ekey = pa, ["pa", "pac"]
            else:
                P.op("dve", lambda e: e.tensor_tensor(out=pb[:, :], in0=pb[:, :], in1=logf[:, :], op=ALU.subtract),
                     reads=PIN + [("logf", 0), ("logf", 1), "dch"] + bkey, writes=PIN)
                esrc, ekey = pb, PIN
            P.op("act", lambda e, esrc=esrc: e.activation(out=ex[0][:, :], in_=esrc[:, :], func=AF.Exp),
                 reads=ekey + ["Qd"], writes=["ex0"])
            P.op("dve", lambda e: e.tensor_tensor(out=Ke[:, :], in0=kk[:, :], in1=ex[0][:, :], op=ALU.mult),
                 reads=["ex0", ("kk", 0), ("kk", 1)], writes=["Ke"])
            for tile in range(8):
                reg = tile % 2
                pst = C.psum[7][:, :].bitcast(BF16)[:, reg * 128:(reg + 1) * 128]
                P.op("pe", lambda e, tile=tile, pst=pst: e.transpose(pst, Ke[:, tile * 128:(tile + 1) * 128], identb[:, :]),
                     reads=["Ke", "identb"], writes=[("ps7", reg)])
                P.op("act", lambda e, tile=tile, pst=pst: e.activation(out=KeT[:, tile, :], in_=pst, func=AF.Copy),
                     reads=[("ps7", reg)], writes=[("KeT", tile)])
            cur = 0
            if final and not os.environ.get("KB_NOS0"):
                P.op("pool", lambda e: e.memset(S[0][:, :], 0.0), writes=["S0"])
                order = range(8) if dirn == 0 else range(7, -1, -1)
                mo = O_CMASK + (0 if dirn == 0 else 16)
                for cc in order:
                    P.op("dve", lambda e, cc=cc, mo=mo, dirn=dirn: e.tensor_scalar(
                        out=stmp[:, :], in0=bsin[:, cc, dirn, 0:128], scalar1=cst[:, mo + cc:mo + cc + 1], scalar2=None, op0=ALU.mult),
                        reads=["bsin", "cst"], writes=["stmp"])
                    P.op("dve", lambda e, cc=cc, mo=mo, dirn=dirn: e.tensor_scalar(
                        out=deff[:, 0:1], in0=bsin[:, cc, dirn, 128:129], scalar1=cst[:, mo + cc:mo + cc + 1],
                        scalar2=cst[:, mo + 8 + cc:mo + 8 + cc + 1], op0=ALU.mult, op1=ALU.add),
                        reads=["bsin", "cst"], writes=["deff"])
                    P.op("dve", lambda e: e.scalar_tensor_tensor(out=S[0][:, :], in0=S[0][:, :], scalar=deff[:, 0:1], in1=stmp[:, :],
                                                                 op0=ALU.mult, op1=ALU.add),
                         reads=["S0", "deff", "stmp"], writes=["S0"])
            else:
                P.op("pool", lambda e: e.memset(S[0][:, :], 0.0), writes=["S0"])
            tiles = range(8) if dirn == 0 else range(7, -1, -1)
            for ti, tile in enumerate(tiles):
                vb = Vbd[ti % 2]
                if final:
                    for j in range(8):
                        P.op("act", lambda e, vb=vb, j=j, tile=tile: e.activation(
                            out=vb[:, j, :], in_=Vt[:, tile, :], func=AF.Copy, scale=cmk[:, j:j + 1]),
                            reads=["Vt", "cB"], writes=[("Vbd", ti % 2, j)])
                else:
                    P.op("dve", lambda e, vb=vb, tile=tile: e.tensor_tensor(
                        out=vb[:, :, :], in0=Vt[:, tile, :].unsqueeze(1).to_broadcast([128, 8, 128]),
                        in1=cmk.unsqueeze(2).to_broadcast([128, 8, 128]), op=ALU.mult),
                        reads=["Vt", "cB"], writes=[("Vbd", ti % 2, j) for j in range(8)])
                for hb in range(2):
                    P.op("pe", lambda e, vb=vb, hb=hb, tile=tile: e.matmul(
                        C.psum[4 + hb][:, :], lhsT=KeT[:, tile, :],
                        rhs=vb[:, 4 * hb:4 * hb + 4, :], start=True, stop=True),
                        reads=[("KeT", tile)] + [("Vbd", ti % 2, j) for j in range(4 * hb, 4 * hb + 4)], writes=["ps%d" % (4 + hb)])
                sb_ = Sbf[ti % 2]
                js = range(8) if dirn == 0 else range(7, -1, -1)
                for j in js:
                    gj = tile * 8 + j
                    if final:
                        P.op("act", lambda e, sb_=sb_, j=j, cur=cur: e.activation(out=sb_[:, j, :], in_=S[cur][:, :], func=AF.Copy),
                             reads=["S%d" % cur], writes=[("Sbf", ti % 2, j)])
                    kvp = C.psum[4 + j // 4][:, (j % 4) * 128:(j % 4 + 1) * 128]
                    P.op("dve", lambda e, cur=cur, gj=gj, kvp=kvp: e.scalar_tensor_tensor(
                        out=S[1 - cur][:, :], in0=S[cur][:, :], scalar=dch[:, gj:gj + 1], in1=kvp, op0=ALU.mult, op1=ALU.add),
                        reads=["S%d" % cur, "dch", "ps%d" % (4 + j // 4)], writes=["S%d" % (1 - cur)])
                    cur = 1 - cur
                if final and not os.environ.get("KB_NOINTRA"):
                    tsl = slice(tile * 128, (tile + 1) * 128)
                    pa_ = C.psum[6][:, 0:128]
                    po_ = C.psum[6][:, 256:384]
                    am = attm[ti % 2]
                    P.op("pe", lambda e, tsl=tsl, pa_=pa_: e.matmul(pa_, lhsT=Kd[:, tsl], rhs=Qd[:, tsl], start=True, stop=True),
                         reads=["Kd", "Qd"], writes=[("ps6", 0)])
                    P.op("dve", lambda e, pa_=pa_, am=am, dirn=dirn: e.tensor_tensor(out=am[:, :], in0=pa_, in1=maskT[dirn], op=ALU.mult),
                         reads=[("ps6", 0), "cB"], writes=[("attm", ti % 2)])
                    qm = Qdm[ti % 2]
                    P.op("dve", lambda e, qm=qm, tsl=tsl: e.tensor_tensor(
                        out=qm[:, :, :], in0=Qd[:, tsl].unsqueeze(1).to_broadcast([128, 8, 128]), in1=cmq[:, :, :], op=ALU.mult),
                        reads=["Qd", "cmq"], writes=[("Qdm", ti % 2)])
                    P.op("pe", lambda e, po_=po_, am=am, tile=tile: e.matmul(po_, lhsT=Vt[:, tile, :], rhs=am[:, :], start=True, stop=False),
                         reads=["Vt", ("attm", ti % 2)], writes=[("ps6", 1)])
                    for jj, j in enumerate(js):
                        P.op("pe", lambda e, po_=po_, sb_=sb_, j=j, qm=qm, jj=jj: e.matmul(
                            po_, lhsT=sb_[:, j, :], rhs=qm[:, j, :], start=False, stop=(jj == 7)),
                            reads=[("Sbf", ti % 2, j), ("Qdm", ti % 2)], writes=[("ps6", 1)])
                    if dirn == 0:
                        P.op("act", lambda e, tsl=tsl, po_=po_: e.activation(out=osb[:, tsl], in_=po_, func=AF.Copy),
                             reads=[("ps6", 1)], writes=[("osb", tile // 4)])
                    else:
                        P.op("dve", lambda e, tsl=tsl, po_=po_: e.tensor_tensor(out=osb[:, tsl], in0=osb[:, tsl], in1=po_, op=ALU.add),
                             reads=[("ps6", 1), ("osb", tile // 4)], writes=[("osb", tile // 4)])
            if not final:
                P.op("pool", lambda e, cur=cur, hd=hd: e.tensor_copy(out=bst_sb[:, hd, 0:128], in_=S[cur][:, :]),
                     reads=["S%d" % cur], writes=[("bst", hd)])
                P.op("dve", lambda e, hd=hd: e.tensor_reduce(out=bst_sb[:, hd, 128:129], in_=dch[:, :], axis=AX.X, op=ALU.mult),
                     reads=["dch"], writes=[("bstd", hd)])
        if final:
            proj_fm(C, W, X["wfm"][wbase + 4], b0=0)
            for half in range(2):
                ts = slice(half * 512, (half + 1) * 512)
                P.op("act", lambda e, half=half, ts=ts: e.activation(out=gs[:, ts], in_=C.psum[half][:, :], func=AF.Silu),
                     reads=["ps%d" % half], writes=[("gs", half)])
                s = rr(C, "sq", 2)
                sq = C.sq[s]
                P.op("act", lambda e, sq=sq, ts=ts: e.activation(out=sq[:, :], in_=osb[:, ts], func=AF.Square),
                     reads=[("osb", half)], writes=["sq%d" % s])
                P.op("pe", lambda e, sq=sq: e.matmul(C.psum[7][:, :], lhsT=C.ones[:, :], rhs=sq[:, :], start=True, stop=True),
                     reads=["sq%d" % s, "ones"], writes=[("ps7", 0), ("ps7", 1)])
                P.op("dve", lambda e: e.tensor_scalar(out=C.rstd[:, :], in0=C.psum[7][:, :], scalar1=1.0 / 128, scalar2=EPS,
                                                      op0=ALU.mult, op1=ALU.add),
                     reads=[("ps7", 0), ("ps7", 1)], writes=["rstd"])
                P.op("act", lambda e: e.activation(out=C.rstd[:, :], in_=C.rstd[:, :], func=AF.Sqrt), reads=["rstd"], writes=["rstd"])
                P.op("dve", lambda e: e.reciprocal(out=C.rstd[:, :], in_=C.rstd[:, :]), reads=["rstd"], writes=["rstd"])
                P.op("dve", lambda e, ts=ts: e.scalar_tensor_tensor(
                    out=ytmp[:, :], in0=osb[:, ts], scalar=cst[:, O_GNW:O_GNW + 1], in1=C.rstd[:, :], op0=ALU.mult, op1=ALU.mult),
                    reads=[("osb", half), "rstd", "cst"], writes=["ytmp"])
                P.op("dve", lambda e, ts=ts, h=h: e.tensor_tensor(out=C.yB[:, h, ts], in0=ytmp[:, :], in1=gs[:, ts], op=ALU.mult),
                     reads=["ytmp", ("gs", half)], writes=[("yB", h, half)])
    return bst_sb


OFF_QC, OFF_KCL, OFF_VCL, OFF_OV = 8192, 20480, 24576, 28672
OFF_QA, OFF_KA, OFF_VA = 28672, 40960, 46080
OFF_SCR = 69760
VXW = 192


def vx_cols(g):
    base = (g // 2) * VXW
    if g % 2 == 0:
        return base, 128, 64, 0
    return base + 64, 128, 0, 64


def emit_norm_out(C, A, pso, g, dest, esink_col, okeys, dkey):
    P = C.P
    _, M, r, r0 = vx_cols(g)
    rows = slice(r0, r0 + 64)
    rr_ = slice(r, r + 1)
    if esink_col is None:
        P.op("dve", lambda e: e.tensor_copy(out=A.den[rr_, :], in_=pso[rr_, :]), reads=okeys, writes=["den"])
    else:
        P.op("dve", lambda e: e.tensor_scalar(out=A.den[rr_, :], in0=pso[rr_, :], scalar1=C.esink[rr_, esink_col:esink_col + 1],
                                              scalar2=None, op0=ALU.add), reads=okeys + ["der"], writes=["den"])
    P.op("dve", lambda e: e.reciprocal(out=A.rden[rr_, :], in_=A.den[rr_, :]), reads=["den"], writes=["rden"])
    P.op("pe", lambda e: e.matmul(C.psum[6][:, :], lhsT=C.ones[rr_, :], rhs=A.rden[rr_, :], start=True, stop=True),
         reads=["rden", "ones"], writes=["ps6"])
    P.op("act", lambda e: e.activation(out=A.bcs[rows, :], in_=C.psum[6][rows, :], func=AF.Copy), reads=["ps6"], writes=["bcs"])
    P.op("dve", lambda e: e.tensor_tensor(out=dest, in0=pso[rows, :], in1=A.bcs[rows, :], op=ALU.mult),
         reads=okeys + ["bcs"], writes=[dkey])


def emit_proj_attn(C, X, W, stage):
    P = C.P
    arB = C.arB
    A = Ctx()
    C.A = A
    A.qC = carve(arB, OFF_QC, 12288, BF16, "p (c t) -> p c t", c=6)
    A.kCl = carve(arB, OFF_KCL, 4096, BF16, "p (c t) -> p c t", c=2)
    A.vCl = carve(arB, OFF_VCL, 4096, BF16, "p (a n) -> p a n", a=8)
    if stage == "S2":
        A.qA = carve(arB, OFF_QA, 12288, BF16, "p (c t) -> p c t", c=6)
        A.kA = carve(arB, OFF_KA, 5120, BF16, "p (c t) -> p c t", c=2)
        A.vA = carve(arB, OFF_VA, 7744, BF16)[:, 0:10 * 2 * VXW].rearrange("p (a n) -> p a n", a=10)
    else:
        A.kAl = carve(arB, OFF_QA, 4096, BF16, "p (c t) -> p c t", c=2)
        A.vAl = carve(arB, OFF_QA + 4096, 4096, BF16, "p (a n) -> p a n", a=8)
    cs = carve(arB, OFF_SCR, 8192, F32, "p (k t) -> p k t", k=2)
    cC = carve(arB, OFF_SCR + 8192, 1024, F32)
    qn = [carve(arB, OFF_SCR + 9216 + i * 2048, 2048, F32) for i in range(2)]
    P.dma("sp", lambda e: e.dma_start(out=cs[:, :, :], in_=X["cs"]), writes=["cs"])
    P.dma("sp", lambda e: e.dma_start(out=cC[:, :], in_=X["cC"]), writes=["cC"])
    Rm, bones = cC[:, 0:128], cC[:, 128:256]
    cst = C.cst
    if stage == "S2":
        P.op("pool", lambda e: e.memset(A.vA[:, :, :], 0.0), writes=["vA"])
        P.op("pool", lambda e: e.memset(A.vA[:, :, 64:66], 1.0), reads=["vA"], writes=["vA"])
        P.op("pool", lambda e: e.memset(A.vA[:, :, VXW + 64:VXW + 66], 1.0), reads=["vA"], writes=["vA"])
        for ci in range(6):
            proj_fm(C, W, X["wfm"][20 + ci], b0=(ci % 2) * 2)
            for half in range(2):
                b = (ci % 2) * 2 + half
                P.op("act", lambda e, ci=ci, half=half, b=b: e.activation(
                    out=A.qA[:, ci, half * 512:(half + 1) * 512], in_=C.psum[b][:, :], func=AF.Copy),
                    reads=["ps%d" % b], writes=[("qA", ci)])
    for pair in range(2):
        proj_fm(C, W, X["wfm"][26 + pair], b0=(pair % 2) * 2)
        for half in range(2):
            b = (pair % 2) * 2 + half
            dst = (A.kA[:, pair, 128 + half * 512:128 + (half + 1) * 512] if stage == "S2"
                   else A.kAl[:, pair, half * 512:(half + 1) * 512])
            P.op("act", lambda e, dst=dst, b=b: e.activation(out=dst, in_=C.psum[b][:, :], func=AF.Copy),
                 reads=["ps%d" % b], writes=[("kA", pair)])
    for pair in range(2):
        proj_tm(C, W, X["wfm"][28 + pair], b0=4)
        for bb in range(2):
            pv = C.psum[4 + bb][:, :].rearrange("p (a n) -> p a n", a=4)
            if stage == "S2":
                o0 = pair * VXW
                P.op("act", lambda e, bb=bb, pv=pv, o0=o0: e.activation(
                    out=A.vA[:, 1 + 4 * bb:5 + 4 * bb, o0:o0 + 64], in_=pv[:, :, 0:64], func=AF.Copy),
                    reads=["ps%d" % (4 + bb), "vA"], writes=[("vAx", pair, bb, 0)])
                P.op("act", lambda e, bb=bb, pv=pv, o0=o0: e.activation(
                    out=A.vA[:, 1 + 4 * bb:5 + 4 * bb, o0 + 128:o0 + 192], in_=pv[:, :, 64:128], func=AF.Copy),
                    reads=["ps%d" % (4 + bb), "vA"], writes=[("vAx", pair, bb, 1)])
            else:
                P.op("act", lambda e, bb=bb, pv=pv, pair=pair: e.activation(
                    out=A.vAl[:, 4 * bb:4 * bb + 4, pair * 128:(pair + 1) * 128], in_=pv, func=AF.Copy),
                    reads=["ps%d" % (4 + bb)], writes=[("vAl", pair, bb)])
    chunks = ([("q", ci) for ci in range(6)] if stage == "S2" else []) + \
             ([("k", pr) for pr in range(2)] if stage == "S1" else [])
    for n_, (kind, idx) in enumerate(chunks):
        b0 = (n_ % 2) * 2
        proj_fm(C, W, X["wfm"][(30 if kind == "q" else 36) + idx], b0=b0)
        wcol = O_QKW + (0 if kind == "q" else 1)
        for half in range(2):
            b = b0 + half
            ts = slice(half * 512, (half + 1) * 512)
            s = rr(C, "sq", 2)
            sq = C.sq[s]
            q_ = qn[half]
            P.op("act", lambda e, sq=sq, b=b: e.activation(out=sq[:, :], in_=C.psum[b][:, :], func=AF.Square),
                 reads=["ps%d" % b], writes=["sq%d" % s])
            P.op("pe", lambda e, sq=sq, half=half: e.matmul(C.psum[4 + half][:, :], lhsT=bones, rhs=sq[:, :], start=True, stop=True),
                 reads=["sq%d" % s, "cC"], writes=["ps%d" % (4 + half)])
            P.op("dve", lambda e, half=half: e.tensor_scalar(out=C.rstd[:, :], in0=C.psum[4 + half][:, :], scalar1=1.0 / 64, scalar2=EPS,
                                                             op0=ALU.mult, op1=ALU.add),
                 reads=["ps%d" % (4 + half)], writes=["rstd"])
            P.op("act", lambda e: e.activation(out=C.rstd[:, :], in_=C.rstd[:, :], func=AF.Sqrt), reads=["rstd"], writes=["rstd"])
            P.op("dve", lambda e: e.reciprocal(out=C.rstd[:, :], in_=C.rstd[:, :]), reads=["rstd"], writes=["rstd"])
            P.op("dve", lambda e, q_=q_, b=b, wcol=wcol: e.scalar_tensor_tensor(
                out=q_[:, :], in0=C.psum[b][:, :], scalar=cst[:, wcol:wcol + 1], in1=C.rstd[:, :], op0=ALU.mult, op1=ALU.mult),
                reads=["ps%d" % b, "rstd", "cst"], writes=[("qn", half)])
            P.op("pe", lambda e, q_=q_, half=half: e.matmul(C.psum[6 + half][:, :], lhsT=Rm, rhs=q_[:, :], start=True, stop=True),
                 reads=[("qn", half), "cC"], writes=["ps%d" % (6 + half)])
            s2 = rr(C, "sq", 2)
            t2 = C.sq[s2]
            P.op("dve", lambda e, t2=t2, half=half, ts=ts: e.tensor_tensor(out=t2[:, :], in0=C.psum[6 + half][:, :], in1=cs[:, 1, ts], op=ALU.mult),
                 reads=["ps%d" % (6 + half), "cs"], writes=["sq%d" % s2])
            P.op("dve", lambda e, q_=q_, ts=ts: e.tensor_tensor(out=q_[:, :], in0=q_[:, :], in1=cs[:, 0, ts], op=ALU.mult),
                 reads=[("qn", half), "cs"], writes=[("qn", half)])
            dst = A.qC[:, idx, ts] if kind == "q" else A.kCl[:, idx, ts]
            P.op("dve", lambda e, q_=q_, t2=t2, dst=dst: e.tensor_tensor(out=dst, in0=q_[:, :], in1=t2[:, :], op=ALU.add),
                 reads=[("qn", half), "sq%d" % s2], writes=[("qkC", kind, idx, half)])
    if stage == "S1":
        for pair in range(2):
            proj_tm(C, W, X["wfm"][38 + pair], b0=4)
            for bb in range(2):
                pv = C.psum[4 + bb][:, :].rearrange("p (a n) -> p a n", a=4)
                P.op("act", lambda e, bb=bb, pv=pv, pair=pair: e.activation(
                    out=A.vCl[:, 4 * bb:4 * bb + 4, pair * 128:(pair + 1) * 128], in_=pv, func=AF.Copy),
                    reads=["ps%d" % (4 + bb)], writes=[("vCl", pair, bb)])


def emit_attn_scratch(C):
    A = C.A
    arB = C.arB
    o = OFF_SCR
    A.bias = [carve(arB, o + i * 1536, 1536, F32) for i in range(2)]
    o += 3072
    A.tA = [carve(arB, o + i * 1536, 1536, F32) for i in range(2)]
    o += 3072
    A.den = carve(arB, o, 2048, F32)
    A.rden = carve(arB, o + 2048, 2048, F32)
    A.bcs = carve(arB, o + 4096, 2048, F32)
    o += 6144
    A.PT = [carve(arB, o + i * 1024, 1024, BF16) for i in range(4)]
    A.APT = [carve(arB, o + i * 768, 768, BF16) for i in range(10)]
    assert o + 7680 <= 90112
    A.yA = carve(C.arH, 0, 12288, BF16, "p (c t) -> p c t", c=6)
    A.yC = carve(C.arH, 12288, 12288, BF16, "p (c t) -> p c t", c=6)


def emit_attn_A(C, X):
    P, A = C.P, C.A
    cst = C.cst
    for side in range(2):
        col = 0 if side == 0 else 1152
        P.dma("sp", lambda e, side=side, col=col: e.dma_start(out=A.kA[:, :, col:col + 128], in_=X["kA_halo"][:, :, side, :]),
              writes=[("kAh", side)])
        t = 0 if side == 0 else 9
        P.dma("sp", lambda e, side=side, t=t: e.dma_start(out=A.vA[:, t, :], in_=X["vA_halo"][:, side, :]),
              reads=["vA"], writes=[("vAh", side)])
        P.op("pool", lambda e, side=side, t=t: e.tensor_scalar(
            out=A.vA[:, t, :], in0=A.vA[:, t, :], scalar1=cst[:, O_FLAGS + side:O_FLAGS + side + 1], scalar2=None, op0=ALU.mult),
            reads=[("vAh", side), "vA", "cst"], writes=[("vAt", side)])
    vkeys = ["vA", ("vAt", 0), ("vAt", 1)] + [("vAx", p_, b_, x_) for p_ in range(2) for b_ in range(2) for x_ in range(2)]
    for g in range(4):
        pair = g // 2
        o0, M, r, r0 = vx_cols(g)
        rows = slice(r0, r0 + 64)
        for r3 in range(3):
            h = 3 * g + r3
            ci = pair * 3 + r3
            bs = rr(C, "biasA", 2)
            P.dma("sp", lambda e, bs=bs, h=h: e.dma_start(out=A.bias[bs][:, :], in_=X["biasT"][h]), writes=[("biasA", bs)])
            for kt in range(10):
                n_lo, n_hi = max(kt - 2, 0), min(kt, 7)
                lo = (n_lo - (kt - 2)) * 128
                ncol = (n_hi - n_lo + 1) * 128
                sb_ = rr(C, "psA", 3)
                pss = C.psum[sb_][:, 0:ncol]
                P.op("pe", lambda e, pss=pss, kt=kt, n_lo=n_lo, ncol=ncol, ci=ci, pair=pair, rows=rows: e.matmul(
                    pss, lhsT=A.kA[rows, pair, kt * 128:(kt + 1) * 128], rhs=A.qA[rows, ci, n_lo * 128:n_lo * 128 + ncol],
                    start=True, stop=True),
                    reads=[("kA", pair), ("kAh", 0), ("kAh", 1), ("qA", ci)], writes=["ps%d" % sb_])
                tb = rr(C, "tA", 2)
                tA = A.tA[tb][:, 0:ncol]
                P.op("dve", lambda e, tA=tA, pss=pss, bs=bs, lo=lo, ncol=ncol: e.scalar_tensor_tensor(
                    out=tA, in0=pss, scalar=0.125, in1=A.bias[bs][:, lo:lo + ncol], op0=ALU.mult, op1=ALU.add),
                    reads=["ps%d" % sb_, ("biasA", bs)], writes=[("tA", tb)])
                PT = A.APT[kt][:, 0:ncol]
                P.op("act", lambda e, PT=PT, tA=tA: e.activation(out=PT, in_=tA, func=AF.Exp), reads=[("tA", tb)], writes=[("APT", kt)])
            for n in range(8):
                ob = 4 + n // 4
                for kt in (n, n + 1, n + 2):
                    n_lo = max(kt - 2, 0)
                    P.op("pe", lambda e, ob=ob, n=n, kt=kt, n_lo=n_lo, o0=o0, M=M: e.matmul(
                        C.psum[ob][0:M, (n % 4) * 128:(n % 4 + 1) * 128], lhsT=A.vA[:, kt, o0:o0 + M],
                        rhs=A.APT[kt][:, (n - n_lo) * 128:(n - n_lo + 1) * 128], start=(kt == n), stop=(kt == n + 2)),
                        reads=vkeys + [("APT", kt)], writes=["ps%d" % ob])
            for half in range(2):
                emit_norm_out(C, A, C.psum[4 + half], g, A.yA[rows, ci, half * 512:(half + 1) * 512], h,
                              ["ps%d" % (4 + half)], ("yA", ci, g % 2, half))


def emit_attn_C(C, X):
    P, A = C.P, C.A
    arB = C.arB
    Kall = carve(arB, OFF_OV, 16384, BF16)
    Vx = carve(arB, OFF_OV + 16384, 64 * VXW * 2, BF16, "p (a n) -> p a n", a=64)
    LA = 2
    for pair in range(2):
        for q4 in range(4):
            P.dma("sp", lambda e, pair=pair, q4=q4: e.dma_start(
                out=Kall[:, q4 * 2048:(q4 + 1) * 2048], in_=X["kC_all"][:, pair, q4 * 2048:(q4 + 1) * 2048]),
                writes=[("Kall", q4)])
        for q4 in range(4):
            P.dma("sp", lambda e, pair=pair, q4=q4: e.dma_start(
                out=Vx[:, q4 * 16:(q4 + 1) * 16, :], in_=X["vC_all"][:, pair, q4 * 16:(q4 + 1) * 16, :]),
                writes=[("Vxd", 0, q4), ("Vxd", 1, q4)])
        kkeys = [("Kall", q4) for q4 in range(4)]
        vkeys = [("Vxd", gh, q4) for gh in range(2) for q4 in range(4)]
        for r3 in range(3):
            ci = pair * 3 + r3
            for gh in range(2):
                g = 2 * pair + gh
                o0, M, r, r0 = vx_cols(g)
                o0 -= pair * VXW
                rows = slice(r0, r0 + 64)
                for qh in range(2):
                    ob = 4 + rr(C, "obC", 2)
                    qs = slice(qh * 512, (qh + 1) * 512)
                    pend = []

                    def qk(kt):
                        sb_ = rr(C, "psC", 4)
                        P.op("pe", lambda e, sb_=sb_, kt=kt, rows=rows, ci=ci, qs=qs: e.matmul(
                            C.psum[sb_][:, :], lhsT=Kall[rows, kt * 128:(kt + 1) * 128], rhs=A.qC[rows, ci, qs], start=True, stop=True),
                            reads=kkeys + [("qkC", "q", ci, qh)], writes=["ps%d" % sb_])
                        pb_ = rr(C, "PT", 4)
                        P.op("act", lambda e, sb_=sb_, pb_=pb_: e.activation(out=A.PT[pb_][:, :], in_=C.psum[sb_][:, :], func=AF.Exp, scale=0.125),
                             reads=["ps%d" % sb_], writes=[("PT", pb_)])
                        pend.append((kt, pb_))

                    def pv():
                        kt, pb_ = pend.pop(0)
                        P.op("pe", lambda e, kt=kt, pb_=pb_, ob=ob, M=M, o0=o0: e.matmul(
                            C.psum[ob][0:M, :], lhsT=Vx[:, kt, o0:o0 + M], rhs=A.PT[pb_][:, :], start=(kt == 0), stop=(kt == 63)),
                            reads=vkeys + [("PT", pb_)], writes=["ps%d" % ob])

                    for kt in range(64):
                        qk(kt)
                        if kt >= LA:
                            pv()
                    while pend:
                        pv()
                    emit_norm_out(C, A, C.psum[ob], g, A.yC[rows, ci, qs], None, ["ps%d" % ob], ("yC", ci, gh, qh))


def emit_wout(C, X, W):
    P, A = C.P, C.A
    yT = carve(C.arB, 16384, 32768, F32, "p (c t) -> p c t", c=DC)
    ych = [(A.yA[:, k, :], None) for k in range(6)] + [(C.yB[:, k, :], None) for k in range(4)] + \
          [(A.yC[:, k, :], None) for k in range(6)]
    nw = C.nw[:, 3, :]
    for half in range(2):
        ts = slice(half * 512, (half + 1) * 512)
        for m in range(DC):
            b = m % 2
            t, key = W.load(X["wout"][m])
            for k in range(DC):
                P.op("pe", lambda e, t=t, k=k, b=b, ts=ts: e.matmul(
                    C.psum[b][:, :], lhsT=t[:, k, :], rhs=ych[k][0][:, ts], start=(k == 0), stop=(k == DC - 1)),
                    reads=[key], writes=["ps%d" % b])
            P.op("act", lambda e, m=m, b=b: e.activation(out=yT[:, m, :], in_=C.psum[b][:, :], func=AF.Copy),
                 reads=["ps%d" % b], writes=[("yT", m)])
        emit_rms_stats(C, lambda c: yT[:, c, :], lambda c: [("yT", c)], half, 7)
        for c in range(DC):
            P.op("dve", lambda e, c=c: e.scalar_tensor_tensor(
                out=yT[:, c, :], in0=yT[:, c, :], scalar=nw[:, c:c + 1], in1=C.rstd[:, :], op0=ALU.mult, op1=ALU.mult),
                reads=[("yT", c), "rstd", "cst"], writes=[("yT", c)])
            P.op("pool", lambda e, c=c, ts=ts: e.tensor_tensor(out=C.xT[:, c, ts], in0=yT[:, c, :], in1=C.xT[:, c, ts], op=ALU.add),
                 reads=[("yT", c), ("xT", c, half)], writes=[("xT", c, half)])


def _din(nc, name, shape, dt=F32):
    return nc.dram_tensor(name, list(shape), dt, kind="ExternalInput").ap()


def _dout(nc, name, shape, dt=F32):
    return nc.dram_tensor(name, list(shape), dt, kind="ExternalOutput").ap()


def build_stage(stage):
    nc = bass.Bass("TRN2", target_bir_lowering=False)
    X = {}
    X["xin"] = _din(nc, "xin", [D, NT])
    X["wgu"] = _din(nc, "wgu", [FC, 128, 2, DC, 128])
    X["wd"] = _din(nc, "wd", [DC, 128, FC, 128])
    X["wfm"] = _din(nc, "wfm", [NCH_IN, 128, DC, 128])
    X["cst"] = _din(nc, "cst", [128, NCST])
    X["cB"] = _din(nc, "cB", [128, NCB])
    X["cC"] = _din(nc, "cC", [128, NCC])
    X["cs"] = _din(nc, "cs", [128, 2, NT])
    X["cmq"] = _din(nc, "cmq", [128, 8, 128], BF16)
    X["xout"] = _dout(nc, "xout", [D, NT])
    if stage == "S1":
        X["bst_out"] = _dout(nc, "bst_out", [128, 8, 129])
        X["kA_out"] = _dout(nc, "kA_out", [128, 2, NT], BF16)
        X["vA_out"] = _dout(nc, "vA_out", [128, 8, 256], BF16)
        X["kC_out"] = _dout(nc, "kC_out", [128, 2, NT], BF16)
        X["vC_out"] = _dout(nc, "vC_out", [128, 8, 256], BF16)
    else:
        X["wout"] = _din(nc, "wout", [DC, 128, DC, 128])
        X["bst_all"] = _din(nc, "bst_all", [4, 128, NCORES, 2, 129])
        X["kC_all"] = _din(nc, "kC_all", [128, 2, SEQ], BF16)
        X["vC_all"] = _din(nc, "vC_all", [128, 2, SEQ // 128, VXW], BF16)
        X["kA_halo"] = _din(nc, "kA_halo", [128, 2, 2, 128], BF16)
        X["vA_halo"] = _din(nc, "vA_halo", [128, 2, 2 * VXW], BF16)
        X["biasT"] = _din(nc, "biasT", [12, 128, 384])
    with contextlib.ExitStack() as stack:
        P = Prog(nc)
        C = alloc_common(nc, stack, P)
        F = alloc_ffn(C, stack)
        emit_consts(C, X)
        for c in range(DC):
            P.dma("sp", lambda e, c=c: e.dma_start(out=C.xT[:, c, :], in_=X["xin"][c * 128:(c + 1) * 128, :]),
                  writes=[("xT", c, 0), ("xT", c, 1)])
        W = WStream(C)
        if stage == "S1":
            emit_ffn(C, X["wgu"], X["wd"], C.nw[:, 0, :], C.nw[:, 1, :], "cst", F)
            P.barrier()
            emit_prenorm(C, C.nw[:, 2, :], "cst")
            bst_sb = emit_B(C, X, W, final=False)
            P.dma("sp", lambda e: e.dma_start(out=X["bst_out"], in_=bst_sb[:, :, :]),
                  reads=[("bst", k) for k in range(8)] + [("bstd", k) for k in range(8)])
            P.barrier()
            emit_proj_attn(C, X, W, "S1")
            A = C.A
            P.dma("sp", lambda e: e.dma_start(out=X["kA_out"], in_=A.kAl[:, :, :]), reads=[("kA", 0), ("kA", 1)])
            P.dma("sp", lambda e: e.dma_start(out=X["vA_out"], in_=A.vAl[:, :, :]),
                  reads=[("vAl", p_, b_) for p_ in range(2) for b_ in range(2)])
            P.dma("sp", lambda e: e.dma_start(out=X["kC_out"], in_=A.kCl[:, :, :]),
                  reads=[("qkC", "k", i_, h_) for i_ in range(2) for h_ in range(2)])
            P.dma("sp", lambda e: e.dma_start(out=X["vC_out"], in_=A.vCl[:, :, :]),
                  reads=[("vCl", p_, b_) for p_ in range(2) for b_ in range(2)])
        else:
            import os
            stop = os.environ.get("KSTOP", "")
            phases = ["B", "proj", "A", "C", "dbg", "wout", "ffn"]
            upto = phases.index(stop) if stop in phases else len(phases) - 1
            skip = os.environ.get("KSKIP", "").split(",")
            emit_prenorm(C, C.nw[:, 2, :], "cst")
            if "B" not in skip:
                emit_B(C, X, W, final=True)
            P.barrier()
            if upto >= 1:
                emit_proj_attn(C, X, W, "S2")
                P.barrier()
                emit_attn_scratch(C)
            if upto >= 2 and "A" not in skip:
                emit_attn_A(C, X)
                P.barrier()
            if upto >= 3 and "C" not in skip:
                emit_attn_C(C, X)
                P.barrier()
            if upto >= 4 and os.environ.get("KDEBUG"):
                X["dbgH"] = _dout(nc, "dbgH", [128, 8192])
                X["dbgB"] = _dout(nc, "dbgB", [128, 2048])
                P.dma("sp", lambda e: e.dma_start(out=X["dbgH"], in_=C.arH[:, :]))
                P.dma("sp", lambda e: e.dma_start(out=X["dbgB"], in_=C.arB[:, 0:2048]))
                P.barrier()
            if upto >= 5:
                emit_wout(C, X, W)
                P.barrier()
            if upto >= 6:
                emit_ffn(C, X["wgu"], X["wd"], C.nw[:, 4, :], C.nw[:, 5, :], "cst", F)
        for c in range(DC):
            P.dma("sp", lambda e, c=c: e.dma_start(out=X["xout"][c * 128:(c + 1) * 128, :], in_=C.xT[:, c, :]),
                  reads=[("xT", c, 0), ("xT", c, 1)])
        P.emit(stack)
        print("stage", stage, "stats", P.stats)
    return nc


import ml_dtypes

OFFS = dict(aq=0, ak=768, av=1024, bq=1280, bzf=1792, bzb=2304, bi=2816, bg=3328, cq=3840, ck=4608, cv=4864)


def _qcols(base, ci):
    pair, r3 = ci // 3, ci % 3
    lo = base + (3 * (2 * pair) + r3) * 64
    hi = base + (3 * (2 * pair + 1) + r3) * 64
    return list(range(lo, lo + 64)) + list(range(hi, hi + 64))


def in_col_index():
    cols = []
    for h in range(4):
        for nm in ("bi", "bq", "bzf", "bzb", "bg"):
            cols += list(range(OFFS[nm] + h * 128, OFFS[nm] + (h + 1) * 128))
    for pre in ("a", "c"):
        for ci in range(6):
            cols += _qcols(OFFS[pre + "q"], ci)
        for pair in range(2):
            cols += list(range(OFFS[pre + "k"] + pair * 128, OFFS[pre + "k"] + (pair + 1) * 128))
        for pair in range(2):
            cols += list(range(OFFS[pre + "v"] + pair * 128, OFFS[pre + "v"] + (pair + 1) * 128))
    return np.asarray(cols)


def out_row_index():
    rows = []
    for ci in range(6):
        rows += _qcols(0, ci)
    rows += list(range(768, 1280))
    for ci in range(6):
        rows += _qcols(1280, ci)
    return np.asarray(rows)


def lay_wfm(w_in):
    w = w_in[:, in_col_index()]
    a = w.reshape(DC, 128, NCH_IN, 128)
    return np.ascontiguousarray(a.transpose(2, 1, 0, 3))


def lay_wout(w_out):
    w = w_out[out_row_index(), :]
    a = w.reshape(DC, 128, DC, 128)
    return np.ascontiguousarray(a.transpose(2, 1, 0, 3))


def t5_bucket_np(rel):
    nb, max_exact = 16, 8
    n = np.abs(rel)
    nf = np.maximum(n, 1).astype(np.float32)
    val = np.log(nf / np.float32(max_exact)) / np.float32(np.log(128 / 8)) * np.float32(nb - max_exact)
    large = max_exact + val.astype(np.int32)
    large = np.minimum(large, nb - 1)
    return np.where(rel > 0, nb, 0) + np.where(n < max_exact, n, large)


def make_biasT(rel_bias):
    kj = np.arange(128)[:, None, None]
    dd = np.arange(3)[None, :, None]
    qi = np.arange(128)[None, None, :]
    rel = (1 - dd) * 128 + kj - qi
    valid = np.abs(rel) <= 128
    bk = t5_bucket_np(rel)
    tab = rel_bias.astype(np.float32)[bk]
    tab = np.where(valid[..., None], tab, np.float32(-30000.0))
    return np.ascontiguousarray(tab.transpose(3, 0, 1, 2).reshape(12, 128, 384)).astype(np.float32)


def make_consts():
    s = np.arange(128)[:, None]
    t = np.arange(128)[None, :]
    same = (s // 16) == (t // 16)
    cB = np.zeros((128, NCB), np.float32)
    cB[:, 0:128] = (same & (s <= t))
    cB[:, 128:256] = (same & (s >= t))
    cB[:, 256:384] = np.eye(128)
    cB[:, 384:392] = (np.arange(128)[:, None] // 16) == np.arange(8)[None, :]
    global CMQ
    CMQ = np.ascontiguousarray(np.broadcast_to(((np.arange(128)[None, :] // 16) == np.arange(8)[:, None]).astype(np.float32)[None], (128, 8, 128))).astype(ml_dtypes.bfloat16)
    cC = np.zeros((128, NCC), np.float32)
    for i in range(64):
        cC[2 * i + 1, 2 * i] = -1.0
        cC[2 * i, 2 * i + 1] = 1.0
    cC[:, 128:256] = (s // 64) == (t // 64)
    half = 32
    inv = (1.0 / (np.float32(10000.0) ** (np.arange(0, half, 2, dtype=np.float32) / np.float32(half)))).astype(np.float32)
    pos = np.arange(SEQ)
    row_ang = (pos // 64).astype(np.float32)[:, None] * inv[None, :]
    col_ang = (pos % 64).astype(np.float32)[:, None] * inv[None, :]
    ang = np.concatenate([row_ang, col_ang], axis=1).astype(np.float32)
    idx = (np.arange(128) % 64) // 2
    cs = np.stack([np.cos(ang)[:, idx].T, np.sin(ang)[:, idx].T], axis=1).astype(np.float32)
    cs_cores = [np.ascontiguousarray(cs[:, :, c * NT:(c + 1) * NT]) for c in range(NCORES)]
    return cB, cC, cs_cores


def make_cst(l, core, norm_w, hgrn_lb, qk_norm_w, hgrn_norm_w, sink_logits):
    cst = np.zeros((128, NCST), np.float32)
    cst[:, O_NW:O_NW + 96] = norm_w[l].reshape(6, DC, 128).transpose(2, 0, 1).reshape(128, 96)
    cst[:, O_LBR:O_LBR + 32] = hgrn_lb.reshape(4, 2, 4, 128).transpose(3, 0, 1, 2).reshape(128, 32)
    for l2 in range(4):
        cst[:, O_LMASK + l2] = 1.0 if (1 <= l2 <= l) else 0.0
    cst[:, O_QKW] = np.tile(qk_norm_w[l, 0], 2)
    cst[:, O_QKW + 1] = np.tile(qk_norm_w[l, 1], 2)
    cst[:, O_GNW] = hgrn_norm_w[l]
    cst[:, O_SINK:O_SINK + 12] = sink_logits[l][None, :]
    for c2 in range(NCORES):
        mf = 1.0 if c2 < core else 0.0
        mb = 1.0 if c2 > core else 0.0
        cst[:, O_CMASK + c2] = mf
        cst[:, O_CMASK + 8 + c2] = 1.0 - mf
        cst[:, O_CMASK + 16 + c2] = mb
        cst[:, O_CMASK + 24 + c2] = 1.0 - mb
    cst[:, O_FLAGS] = 1.0 if core > 0 else 0.0
    cst[:, O_FLAGS + 1] = 1.0 if core < NCORES - 1 else 0.0
    return cst


def v_ext(v):
    out = np.zeros(v.shape[:-1] + (2 * VXW,), v.dtype)
    for pair in range(2):
        b = pair * VXW
        out[..., b:b + 64] = v[..., (2 * pair) * 64:(2 * pair + 1) * 64]
        out[..., b + 64] = 1
        out[..., b + 65] = 1
        out[..., b + 128:b + 192] = v[..., (2 * pair + 1) * 64:(2 * pair + 2) * 64]
    return out


def exchange_layout(r1):
    b = np.stack([r["bst_out"] for r in r1], axis=0)
    b = b.reshape(NCORES, 128, 4, 2, 129).transpose(2, 1, 0, 3, 4)
    kC_all = np.ascontiguousarray(np.concatenate([r["kC_out"] for r in r1], axis=2))
    v = np.concatenate([r["vC_out"] for r in r1], axis=1)
    ve = v_ext(v).reshape(128, SEQ // 128, 2, VXW).transpose(0, 2, 1, 3)
    return np.ascontiguousarray(b), kC_all, np.ascontiguousarray(ve)


def _run(name, in_maps):
    import os
    if name not in _PROGS:
        _PROGS[name] = build_stage(name)
    res = run_bass_kernel_spmd(_PROGS[name], in_maps, core_ids=list(range(NCORES)), trace=bool(os.environ.get("KTRACE")))
    if os.environ.get("KTRACE"):
        print("exec_time_ns", name, res.exec_time_ns)
    return res.results


def run_layer(l, xT, inp, consts, debug=None):
    cB, cC, cs_cores = consts
    wfm = lay_wfm(inp["w_in"][l])
    wout = lay_wout(inp["w_out"][l])
    gu1, d1 = lay_gu(inp["ffn1_gate"][l], inp["ffn1_up"][l]), lay_d(inp["ffn1_down"][l])
    csts = [make_cst(l, c, inp["norm_w"], inp["hgrn_lb"], inp["qk_norm_w"], inp["hgrn_norm_w"], inp["sink_logits"])
            for c in range(NCORES)]
    in1 = [dict(xin=xT[c], wgu=gu1, wd=d1, wfm=wfm, cst=csts[c], cB=cB, cC=cC, cs=cs_cores[c], cmq=CMQ) for c in range(NCORES)]
    r1 = _run("S1", in1)
    del gu1, d1, in1
    bst_all, kC_all, vC_all = exchange_layout(r1)
    zk = np.zeros((128, 2, 128), r1[0]["kA_out"].dtype)
    zv = np.zeros((128, 256), r1[0]["vA_out"].dtype)
    gu2, d2 = lay_gu(inp["ffn2_gate"][l], inp["ffn2_up"][l]), lay_d(inp["ffn2_down"][l])
    biasT = make_biasT(inp["rel_bias"])
    in2 = []
    for c in range(NCORES):
        kp = r1[c - 1]["kA_out"][:, :, NT - 128:] if c > 0 else zk
        kn = r1[c + 1]["kA_out"][:, :, :128] if c < NCORES - 1 else zk
        vp = r1[c - 1]["vA_out"][:, 7, :] if c > 0 else zv
        vn = r1[c + 1]["vA_out"][:, 0, :] if c < NCORES - 1 else zv
        in2.append(dict(xin=r1[c]["xout"], wgu=gu2, wd=d2, wfm=wfm, wout=wout, cst=csts[c], cB=cB, cC=cC, cs=cs_cores[c], cmq=CMQ,
                        bst_all=bst_all, kC_all=kC_all, vC_all=vC_all,
                        kA_halo=np.ascontiguousarray(np.stack([kp, kn], axis=2)),
                        vA_halo=np.ascontiguousarray(np.stack([v_ext(vp), v_ext(vn)], axis=1)), biasT=biasT))
    if debug is not None:
        debug["r1"] = r1
    r2 = _run("S2", in2)
    if debug is not None:
        debug["r2"] = r2
    return [r["xout"] for r in r2]


def kernel(**inputs):
    inp = {k: np.asarray(v) for k, v in inputs.items()}
    x = inp["x"][0]
    xT = [np.ascontiguousarray(x[c * NT:(c + 1) * NT].T) for c in range(NCORES)]
    consts = make_consts()
    for l in range(DEPTH):
        xT = run_layer(l, xT, inp, consts)
    y = np.concatenate([o.T for o in xT], axis=0)
    return np.ascontiguousarray(y[None]).astype(np.float32)
```

```python
import contextlib
import os
import numpy as np
import concourse.bass as bass
import concourse.mybir as mybir
from concourse.bass_utils import run_bass_kernel_spmd

F32 = mybir.dt.float32
BF16 = mybir.dt.bfloat16
AF = mybir.ActivationFunctionType
ALU = mybir.AluOpType

NCORES = 8
D = 2048
SEQ = 8192
NT = SEQ // NCORES
DC = D // 128
DFF = 5632
FC = DFF // 128
EPS = 1e-6
DEPTH = 4


class Prog:
    COMPUTE = ("pe", "act", "dve", "pool")
    DMAQ = ("sp", "pool", "act")
    RING = 6

    def __init__(self, nc):
        self.nc = nc
        self.ops = []

    def op(self, eng, fn, reads=(), writes=()):
        self.ops.append(dict(kind="c", eng=eng, fn=fn, reads=tuple(reads), writes=tuple(writes)))

    def dma(self, q, fn, reads=(), writes=()):
        self.ops.append(dict(kind="d", eng=q, fn=fn, reads=tuple(reads), writes=tuple(writes)))

    def barrier(self):
        self.ops.append(dict(kind="b"))

    def emit(self, stack):
        nc = self.nc
        ops = self.ops
        last_w = {}
        readers = {}
        seq = {e: 0 for e in self.COMPUTE}
        dseq = {q: 0 for q in self.DMAQ}
        last_c = {}
        last_d = {q: [] for q in self.DMAQ}
        pending = {}
        for i, o in enumerate(ops):
            if o["kind"] == "b":
                bd = set(last_c.values())
                for q in self.DMAQ:
                    bd.update(last_d[q][-self.RING:])
                for st in ("pe", "act", "dve", "pool", "sp"):
                    pending[st] = set(bd)
                last_w, readers = {}, {}
                o["deps"] = set()
                continue
            deps = set()
            if o["eng"] in pending:
                deps.update(pending.pop(o["eng"]))
            if o["kind"] == "c":
                last_c[o["eng"]] = i
            else:
                last_d[o["eng"]].append(i)
            for r in o["reads"]:
                if r in last_w:
                    deps.add(last_w[r])
            for w in o["writes"]:
                if w in last_w:
                    deps.add(last_w[w])
                for rd in readers.get(w, ()):
                    deps.add(rd)
            deps.discard(i)
            best = {}
            red = set()
            for d_ in deps:
                od = ops[d_]
                if od["kind"] == "c":
                    if od["eng"] not in best or best[od["eng"]] < d_:
                        best[od["eng"]] = d_
                else:
                    red.add(d_)
            red.update(best.values())
            o["deps"] = red
            for w in o["writes"]:
                last_w[w] = i
                readers[w] = []
            for r in o["reads"]:
                if r not in o["writes"]:
                    readers.setdefault(r, []).append(i)
            if o["kind"] == "c":
                o["seq"] = seq[o["eng"]]
                seq[o["eng"]] += 1
            else:
                o["dseq"] = dseq[o["eng"]]
                dseq[o["eng"]] += 1
            o["signal"] = False
        ops_all = ops
        for i, o in enumerate(ops):
            if o["kind"] == "b":
                continue
            for d in o["deps"]:
                od = ops[d]
                if od["kind"] == "c":
                    if od["eng"] == o["eng"] and o["kind"] == "c" and od["eng"] == "pe":
                        continue
                    od["signal"] = True
        cnt = {e: 0 for e in self.COMPUTE}
        for o in ops:
            if o["kind"] == "b":
                continue
            if o["kind"] == "c":
                if o["signal"]:
                    cnt[o["eng"]] += 1
                    o["cnt"] = cnt[o["eng"]]
        csem = {e: stack.enter_context(nc.semaphore("s_" + e)) for e in self.COMPUTE}
        dsem = {q: [stack.enter_context(nc.semaphore("d_%s%d" % (q, k))) for k in range(self.RING)]
                for q in self.DMAQ if dseq[q] > 0}
        streams = {e: [] for e in ("pe", "act", "dve", "pool", "sp")}
        known = {e: {} for e in streams}

        def need(stream, sem, val, lst):
            k = known[stream]
            if k.get(sem.name if hasattr(sem, "name") else id(sem), 0) >= val:
                return
            k[sem.name if hasattr(sem, "name") else id(sem)] = val
            lst.append((sem, val))

        for i, o in enumerate(ops):
            if o["kind"] == "b":
                continue
            st = o["eng"]
            waits = []
            for d in sorted(o["deps"]):
                od = ops[d]
                if od["kind"] == "c":
                    if not od["signal"]:
                        continue
                    if od["eng"] == st and o["kind"] == "c" and st == "pe":
                        continue
                    need(st, csem[od["eng"]], od["cnt"], waits)
                else:
                    q = od["eng"]
                    j = od["dseq"]
                    need(st, dsem[q][j % self.RING], 16 * (j // self.RING + 1), waits)
            inc = None
            if o["kind"] == "d":
                j = o["dseq"]
                if j >= self.RING:
                    need(st, dsem[st][j % self.RING], 16 * (j // self.RING), waits)
                inc = (dsem[st][j % self.RING], 16)
            elif o["signal"]:
                inc = (csem[st], 1)
            streams[st].append((waits, o["fn"], inc))
        fin = []
        for q in dsem:
            n = dseq[q]
            for k in range(self.RING):
                m = (n - k + self.RING - 1) // self.RING if n > k else 0
                if m > 0:
                    need("sp", dsem[q][k], 16 * m, fin)
        streams["sp"].append((fin, None, None))

        self.stats = {e: len(v) for e, v in streams.items()}
        self.stats["signals"] = dict(cnt)

        def run_stream(eng, lst):
            for waits, fn, inc in lst:
                for sem, val in waits:
                    eng.wait_ge(sem, val)
                if fn is None:
                    continue
                ins = fn(eng)
                if inc is not None:
                    ins.then_inc(inc[0], inc[1])

        with nc.Block() as block:
            @block.tensor
            def _(e):
                run_stream(e, streams["pe"])

            @block.scalar
            def _(e):
                run_stream(e, streams["act"])

            @block.vector
            def _(e):
                run_stream(e, streams["dve"])

            @block.gpsimd
            def _(e):
                run_stream(e, streams["pool"])

            @block.sync
            def _(e):
                run_stream(e, streams["sp"])


class Ctx:
    pass


def alloc_common(nc, stack, P):
    C = Ctx()
    C.nc, C.P = nc, P
    sb = lambda name, shape, dt: stack.enter_context(nc.sbuf_tensor(name, shape, dt))
    C.sb = sb
    C.xT = sb("xT", [128, DC, NT], F32)
    C.arH = sb("arH", [128, DC * NT // 2], F32)
    C.arB = sb("arB", [128, FC * NT // 2], F32)
    C.arW = sb("arW", [128, 4096], F32)
    C.hT = C.arH[:, :].bitcast(BF16).rearrange("p (c t) -> p c t", c=DC)
    C.ones = sb("ones", [128, 128], F32)
    C.psum = [stack.enter_context(nc.psum_tensor("ps%d" % i, [128, 512], F32)) for i in range(8)]
    C.sq = [sb("sq%d" % i, [128, 512], F32) for i in range(2)]
    C.rstd = sb("rstd", [128, 512], F32)
    P.op("pool", lambda e: e.memset(C.ones[:, :], 1.0), writes=["ones"])
    C.ctr = {}
    return C


def rr(C, name, n):
    v = C.ctr.get(name, 0)
    C.ctr[name] = v + 1
    return v % n


def emit_rms_stats(C, src_fn, src_keys, half, pbank, post_scale=None):
    P = C.P
    ps = C.psum[pbank]
    for c in range(DC):
        s = rr(C, "sq", 2)
        sq = C.sq[s]
        P.op("act", lambda e, sq=sq, c=c: e.activation(out=sq[:, :], in_=src_fn(c), func=AF.Square),
             reads=list(src_keys(c)), writes=["sq%d" % s])
        P.op("pe", lambda e, sq=sq, c=c: e.matmul(ps[:, :], lhsT=C.ones[:, :], rhs=sq[:, :],
                                                  start=(c == 0), stop=(c == DC - 1)),
             reads=["sq%d" % s, "ones"], writes=["ps%d" % pbank])
    P.op("dve", lambda e: e.tensor_scalar(out=C.rstd[:, :], in0=ps[:, :], scalar1=1.0 / D, scalar2=EPS,
                                          op0=ALU.mult, op1=ALU.add),
         reads=["ps%d" % pbank], writes=["rstd"])
    P.op("act", lambda e: e.activation(out=C.rstd[:, :], in_=C.rstd[:, :], func=AF.Sqrt),
         reads=["rstd"], writes=["rstd"])
    P.op("dve", lambda e: e.reciprocal(out=C.rstd[:, :], in_=C.rstd[:, :]),
         reads=["rstd"], writes=["rstd"])
    if post_scale is not None:
        P.op("dve", lambda e: e.tensor_scalar(out=C.rstd[:, :], in0=C.rstd[:, :], scalar1=post_scale, scalar2=None,
                                              op0=ALU.mult),
             reads=["rstd"], writes=["rstd"])


def emit_prenorm(C, nw, nwkey):
    P = C.P
    for half in range(2):
        ts = slice(half * 512, (half + 1) * 512)
        emit_rms_stats(C, lambda c, ts=ts: C.xT[:, c, ts], lambda c, half=half: [("xT", c, half)], half, 7)
        for c in range(DC):
            P.op("dve", lambda e, c=c, ts=ts: e.scalar_tensor_tensor(
                out=C.hT[:, c, ts], in0=C.xT[:, c, ts], scalar=nw[:, c:c + 1], in1=C.rstd[:, :],
                op0=ALU.mult, op1=ALU.mult),
                reads=[("xT", c, half), "rstd", nwkey], writes=[("H", c, half)])


def emit_ffn(C, wgu, wd, nw_pre, nw_post, nwkey, F):
    P, nc = C.P, C.nc
    emit_prenorm(C, nw_pre, nwkey)
    for j in range(FC):
        s = rr(C, "wgu", 2)
        wt = F.wgu[s]
        P.dma("pool", lambda e, wt=wt, j=j: e.dma_start(out=wt[:, :, :, :], in_=wgu[j]),
              writes=[("wgu", s)] + ([("wd", i) for i in range(5)] if j < 2 else []))
        pb = (j % 2) * 4
        for c in range(DC):
            for g in range(2):
                for half in range(2):
                    ts = slice(half * 512, (half + 1) * 512)
                    b = pb + g * 2 + half
                    P.op("pe", lambda e, wt=wt, c=c, g=g, ts=ts, b=b: e.matmul(
                        C.psum[b][:, :], lhsT=wt[:, g, c, :], rhs=C.hT[:, c, ts],
                        start=(c == 0), stop=(c == DC - 1)),
                        reads=[("wgu", s), ("H", c, half)], writes=["ps%d" % b])
        for half in range(2):
            ts = slice(half * 512, (half + 1) * 512)
            bg, bu = pb + half, pb + 2 + half
            k = rr(C, "sq", 2)
            sil = F.sil[k]
            P.op("act", lambda e, sil=sil, bg=bg: e.activation(out=sil[:, :], in_=C.psum[bg][:, :], func=AF.Silu),
                 reads=["ps%d" % bg], writes=["sq%d" % k])
            P.op("dve", lambda e, sil=sil, bu=bu, j=j, ts=ts: e.tensor_tensor(
                out=F.hid[:, j, ts], in0=C.psum[bu][:, :], in1=sil[:, :], op=ALU.mult),
                reads=["ps%d" % bu, "sq%d" % k], writes=[("hid", j, half)])
    KH = FC // 4
    for half in range(2):
        ts = slice(half * 512, (half + 1) * 512)
        for m in range(DC):
            b = m % 2
            for kh in range(4):
                s = rr(C, "wd", 5)
                wt = F.wd[s]
                P.dma("pool", lambda e, wt=wt, m=m, kh=kh: e.dma_start(
                    out=wt[:, :, :], in_=wd[m, :, kh * KH:(kh + 1) * KH, :]),
                    writes=[("wd", s)] + ([("wgu", 0), ("wgu", 1)] if (half == 0 and m < 2) else []))
                for kk in range(KH):
                    k = kh * KH + kk
                    P.op("pe", lambda e, wt=wt, kk=kk, k=k, b=b, ts=ts: e.matmul(
                        C.psum[b][:, :], lhsT=wt[:, kk, :], rhs=F.hid[:, k, ts],
                        start=(k == 0), stop=(k == FC - 1)),
                        reads=[("wd", s), ("hid", k, half)], writes=["ps%d" % b])
            P.op("act", lambda e, m=m, b=b: e.activation(out=F.yT[:, m, :], in_=C.psum[b][:, :], func=AF.Copy),
                 reads=["ps%d" % b], writes=[("H", m, 0), ("H", m, 1)])
        emit_rms_stats(C, lambda c: F.yT[:, c, :], lambda c: [("H", c, 0), ("H", c, 1)], half, 7, post_scale=0.5)
        for c in range(DC):
            P.op("dve", lambda e, c=c: e.scalar_tensor_tensor(
                out=F.yT[:, c, :], in0=F.yT[:, c, :], scalar=nw_post[:, c:c + 1], in1=C.rstd[:, :],
                op0=ALU.mult, op1=ALU.mult),
                reads=[("H", c, 0), ("H", c, 1), "rstd", nwkey], writes=[("H", c, 0), ("H", c, 1)])
            P.op("pool", lambda e, c=c, ts=ts: e.tensor_tensor(
                out=C.xT[:, c, ts], in0=F.yT[:, c, :], in1=C.xT[:, c, ts], op=ALU.add),
                reads=[("H", c, 0), ("H", c, 1), ("xT", c, half)], writes=[("xT", c, half)])


def alloc_ffn(C, stack):
    F = Ctx()
    F.hid = C.arB[:, :].bitcast(BF16).rearrange("p (k t) -> p k t", k=FC)
    wb = C.arW[:, :].bitcast(BF16)
    F.wgu = [wb[:, i * 4096:(i + 1) * 4096].rearrange("p (g c n) -> p g c n", g=2, c=DC) for i in range(2)]
    n = (FC // 4) * 128
    F.wd = [wb[:, i * n:(i + 1) * n].rearrange("p (k n) -> p k n", n=128) for i in range(5)]
    F.sil = C.sq
    F.yT = C.arH[:, :].rearrange("p (c t) -> p c t", c=DC)
    return F


def build_ffn_prog():
    nc = bass.Bass("TRN2", target_bir_lowering=False)
    xin = nc.dram_tensor("xin", [D, NT], F32, kind="ExternalInput").ap()
    wgu = nc.dram_tensor("wgu", [FC, 128, 2, DC, 128], F32, kind="ExternalInput").ap()
    wd = nc.dram_tensor("wd", [DC, 128, FC, 128], F32, kind="ExternalInput").ap()
    nwd = nc.dram_tensor("nw", [128, 2, DC], F32, kind="ExternalInput").ap()
    xout = nc.dram_tensor("xout", [D, NT], F32, kind="ExternalOutput").ap()
    with contextlib.ExitStack() as stack:
        P = Prog(nc)
        C = alloc_common(nc, stack, P)
        F = alloc_ffn(C, stack)
        nw = C.sb("nw_sb", [128, 2, DC], F32)
        P.dma("sp", lambda e: e.dma_start(out=nw[:, :, :], in_=nwd), writes=["nw"])
        for c in range(DC):
            P.dma("sp", lambda e, c=c: e.dma_start(out=C.xT[:, c, :], in_=xin[c * 128:(c + 1) * 128, :]),
                  writes=[("xT", c, 0), ("xT", c, 1)])
        emit_ffn(C, wgu, wd, nw[:, 0, :], nw[:, 1, :], "nw", F)
        for c in range(DC):
            P.dma("sp", lambda e, c=c: e.dma_start(out=xout[c * 128:(c + 1) * 128, :], in_=C.xT[:, c, :]),
                  reads=[("xT", c, 0), ("xT", c, 1)])
        P.emit(stack)
        print("prog stats", P.stats)
    return nc


def lay_gu(wg, wu):
    a = np.stack([wg, wu], axis=0).reshape(2, DC, 128, FC, 128)
    return np.ascontiguousarray(a.transpose(3, 2, 0, 1, 4))


def lay_d(wd):
    a = wd.reshape(FC, 128, DC, 128)
    return np.ascontiguousarray(a.transpose(2, 1, 0, 3))


def lay_nw(w):
    return np.ascontiguousarray(w.reshape(DC, 128).T)


_PROGS = {}
CMQ = None


def run_ffn(xT_shards, wg, wu, wd, nw_pre, nw_post):
    if "ffn" not in _PROGS:
        _PROGS["ffn"] = build_ffn_prog()
    nc = _PROGS["ffn"]
    gu = lay_gu(wg, wu)
    dd = lay_d(wd)
    nw = np.ascontiguousarray(np.stack([lay_nw(nw_pre), lay_nw(nw_post)], axis=1))
    in_maps = [{"xin": xT_shards[c], "wgu": gu, "wd": dd, "nw": nw} for c in range(NCORES)]
    import os
    res = run_bass_kernel_spmd(nc, in_maps, core_ids=list(range(NCORES)), trace=bool(os.environ.get("KTRACE")))
    if os.environ.get("KTRACE"):
        print("exec_time_ns", res.exec_time_ns)
    return [r["xout"] for r in res.results]


AX = mybir.AxisListType
NCH_IN = 40
O_NW, O_LBR, O_LMASK, O_QKW, O_GNW, O_SINK, O_CMASK, O_FLAGS = 0, 96, 128, 132, 134, 135, 147, 179
NCST = 181
NCB = 392
NCC = 256


def carve(ar, off, nbytes, dt, pattern=None, **kw):
    assert off % 4 == 0 and nbytes % 4 == 0
    v = ar[:, off // 4:(off + nbytes) // 4]
    if dt == BF16:
        v = v.bitcast(BF16)
    if pattern:
        v = v.rearrange(pattern, **kw)
    return v


def emit_consts(C, X):
    P = C.P
    C.cst = C.sb("cst_sb", [128, NCST], F32)
    C.der = C.sb("der_sb", [128, 96], F32)
    P.dma("sp", lambda e: e.dma_start(out=C.cst[:, :], in_=X["cst"]), writes=["cst"])
    cst, der = C.cst, C.der
    P.op("act", lambda e: e.activation(out=der[:, 0:32], in_=cst[:, O_LBR:O_LBR + 32], func=AF.Exp),
         reads=["cst"], writes=["der"])
    ev = der[:, 0:32].rearrange("p (l k) -> p l k", l=4)
    tot, part = der[:, 32:40], der[:, 40:48]
    P.op("dve", lambda e: e.tensor_tensor(out=tot, in0=ev[:, 0, :], in1=ev[:, 1, :], op=ALU.add), reads=["der"], writes=["der"])
    P.op("dve", lambda e: e.tensor_tensor(out=tot, in0=tot, in1=ev[:, 2, :], op=ALU.add), reads=["der"], writes=["der"])
    P.op("dve", lambda e: e.tensor_tensor(out=tot, in0=tot, in1=ev[:, 3, :], op=ALU.add), reads=["der"], writes=["der"])
    P.op("dve", lambda e: e.tensor_scalar(out=part, in0=ev[:, 0, :], scalar1=cst[:, O_LMASK:O_LMASK + 1], scalar2=None,
                                          op0=ALU.mult), reads=["der", "cst"], writes=["der"])
    for l in range(1, 4):
        P.op("dve", lambda e, l=l: e.scalar_tensor_tensor(out=part, in0=ev[:, l, :], scalar=cst[:, O_LMASK + l:O_LMASK + l + 1],
                                                          in1=part, op0=ALU.mult, op1=ALU.add),
             reads=["der", "cst"], writes=["der"])
    P.op("dve", lambda e: e.reciprocal(out=tot, in_=tot), reads=["der"], writes=["der"])
    C.lb, C.oml, C.noml = der[:, 48:56], der[:, 56:64], der[:, 64:72]
    P.op("dve", lambda e: e.tensor_tensor(out=C.lb, in0=part, in1=tot, op=ALU.mult), reads=["der"], writes=["der"])
    P.op("dve", lambda e: e.tensor_scalar(out=C.oml, in0=C.lb, scalar1=-1.0, scalar2=1.0, op0=ALU.mult, op1=ALU.add),
         reads=["der"], writes=["der"])
    P.op("dve", lambda e: e.tensor_scalar(out=C.noml, in0=C.oml, scalar1=-1.0, scalar2=None, op0=ALU.mult),
         reads=["der"], writes=["der"])
    C.esink = der[:, 72:84]
    P.op("act", lambda e: e.activation(out=C.esink, in_=cst[:, O_SINK:O_SINK + 12], func=AF.Exp),
         reads=["cst", "der"], writes=["der"])
    C.nw = cst[:, O_NW:O_NW + 96].rearrange("p (i c) -> p i c", i=6)


class WStream:
    def __init__(self, C, nslot=4):
        self.C = C
        wb = C.arW[:, :].bitcast(BF16)
        self.slots = [wb[:, i * 2048:(i + 1) * 2048].rearrange("p (c n) -> p c n", c=DC) for i in range(nslot)]
        self.n = nslot
        self.i = 0

    def load(self, src):
        s = self.i % self.n
        self.i += 1
        t = self.slots[s]
        self.C.P.dma("pool", lambda e: e.dma_start(out=t[:, :, :], in_=src), writes=[("ws", s)])
        return t, ("ws", s)


def proj_fm(C, W, src, b0=0):
    P = C.P
    t, key = W.load(src)
    for c in range(DC):
        for half in range(2):
            ts = slice(half * 512, (half + 1) * 512)
            P.op("pe", lambda e, c=c, ts=ts, half=half: e.matmul(
                C.psum[b0 + half][:, :], lhsT=t[:, c, :], rhs=C.hT[:, c, ts], start=(c == 0), stop=(c == DC - 1)),
                reads=[key, ("H", c, half)], writes=["ps%d" % (b0 + half)])


def proj_tm(C, W, src, b0=2):
    P = C.P
    t, key = W.load(src)
    for tile in range(8):
        b = b0 + tile // 4
        cs = slice((tile % 4) * 128, (tile % 4 + 1) * 128)
        half = tile // 4
        for c in range(DC):
            P.op("pe", lambda e, c=c, b=b, cs=cs, tile=tile: e.matmul(
                C.psum[b][:, cs], lhsT=C.hT[:, c, tile * 128:(tile + 1) * 128], rhs=t[:, c, :],
                start=(c == 0), stop=(c == DC - 1)),
                reads=[key, ("H", c, half)], writes=["ps%d" % b])


def emit_B(C, X, W, final):
    P, nc = C.P, C.nc
    arB = C.arB
    base = 8192
    off = [base]

    def take(nbytes, dt, pattern=None, **kw):
        v = carve(arB, off[0], nbytes, dt, pattern, **kw)
        off[0] += nbytes
        return v

    C.yB = carve(arB, 0, 8192, BF16, "p (h t) -> p h t", h=4)
    cB = take(NCB * 4, F32)
    qf = take(4096, F32)
    gs = take(4096, F32)
    Vt = take(2048, BF16, "p (a n) -> p a n", a=8)
    Vbd = [take(2048, BF16, "p (j n) -> p j n", j=8) for _ in range(2)]
    osb = take(4096, F32)
    sig = take(4096, F32)
    logf = take(4096, F32)
    kk = take(4096, F32)
    pa = take(4096, F32)
    pb = take(4096, F32)
    ex = [take(4096, F32) for _ in range(2)]
    Qd = take(2048, BF16)
    Kd = take(2048, BF16)
    Ke = take(2048, BF16)
    KeT = take(2048, BF16, "p (a n) -> p a n", a=8)
    attm = [take(256, BF16) for _ in range(2)]
    S = [take(512, F32) for _ in range(2)]
    Sbf = [take(2048, BF16, "p (j n) -> p j n", j=8) for _ in range(2)]
    dch = take(256, F32)
    tsum = take(32, F32)
    identb = take(256, BF16)
    bst_sb = take(8 * 129 * 4, F32, "p (k n) -> p k n", k=8) if not final else None
    bsin = take(8 * 2 * 129 * 4, F32, "p (c d n) -> p c d n", c=8, d=2) if final else None
    stmp = take(512, F32)
    deff = take(32, F32)
    ytmp = take(2048, F32)
    cmq = take(2048, BF16, "p (j t) -> p j t", j=8)
    Qdm = [take(2048, BF16, "p (j t) -> p j t", j=8) for _ in range(2)]
    assert off[0] <= 90112, off[0]
    P.dma("sp", lambda e: e.dma_start(out=cmq[:, :, :], in_=X["cmq"]), writes=["cmq"])

    P.dma("sp", lambda e: e.dma_start(out=cB[:, :], in_=X["cB"]), writes=["cB"])
    P.op("pool", lambda e: e.tensor_copy(out=identb[:, :], in_=cB[:, 256:384]), reads=["cB"], writes=["identb"])
    maskT = [cB[:, 0:128], cB[:, 128:256]]
    cmk = cB[:, 384:392]
    v3 = lambda a: a[:, :].rearrange("p (j s) -> p j s", s=16)
    cst = C.cst

    for h in range(4):
        wbase = 5 * h
        proj_tm(C, W, X["wfm"][wbase + 0], b0=2)
        for bb in range(2):
            P.op("act", lambda e, bb=bb: e.activation(
                out=Vt[:, 4 * bb:4 * bb + 4, :], in_=C.psum[2 + bb][:, :].rearrange("p (a n) -> p a n", a=4), func=AF.Copy),
                reads=["ps%d" % (2 + bb)], writes=["Vt"])
        if final:
            proj_fm(C, W, X["wfm"][wbase + 1], b0=0)
            for half in range(2):
                ts = slice(half * 512, (half + 1) * 512)
                P.op("act", lambda e, half=half, ts=ts: e.activation(out=qf[:, ts], in_=C.psum[half][:, :], func=AF.Silu),
                     reads=["ps%d" % half], writes=[("qf", half)])
            P.dma("sp", lambda e, h=h: e.dma_start(out=bsin[:, :, :, :], in_=X["bst_all"][h]), writes=["bsin"])
        for dirn in range(2):
            hd = 2 * h + dirn
            lbi = dirn * 4 + h
            proj_fm(C, W, X["wfm"][wbase + 2 + dirn], b0=0)
            for half in range(2):
                ts = slice(half * 512, (half + 1) * 512)
                P.op("act", lambda e, half=half, ts=ts: e.activation(out=sig[:, ts], in_=C.psum[half][:, :], func=AF.Sigmoid),
                     reads=["ps%d" % half], writes=[("sig", half)])
                P.op("act", lambda e, ts=ts, lbi=lbi: e.activation(out=logf[:, ts], in_=sig[:, ts], func=AF.Ln,
                                                                   bias=C.lb[:, lbi:lbi + 1], scale=C.oml[:, lbi:lbi + 1]),
                     reads=[("sig", half), "der"], writes=[("logf", half)])
                P.op("dve", lambda e, ts=ts, lbi=lbi: e.tensor_scalar(out=kk[:, ts], in0=sig[:, ts], scalar1=C.noml[:, lbi:lbi + 1],
                                                                      scalar2=C.oml[:, lbi:lbi + 1], op0=ALU.mult, op1=ALU.add),
                     reads=[("sig", half), "der"], writes=[("kk", half)])
            src, skey = logf, [("logf", 0), ("logf", 1)]
            dsts = [(pa, "pa"), (pb, "pb"), (pa, "pa"), (pb, "pb")]
            for si, sft in enumerate((1, 2, 4, 8)):
                dst, dkey = dsts[si]
                P.op("dve", lambda e, src=src, dst=dst, sft=sft: e.tensor_tensor(
                    out=v3(dst)[:, :, sft:16], in0=v3(src)[:, :, sft:16], in1=v3(src)[:, :, 0:16 - sft], op=ALU.add),
                    reads=skey, writes=[dkey])
                P.op("pool", lambda e, src=src, dst=dst, sft=sft: e.tensor_copy(
                    out=v3(dst)[:, :, 0:sft], in_=v3(src)[:, :, 0:sft]),
                    reads=skey, writes=[dkey + "c"])
                src, skey = dst, [dkey, dkey + "c"]
            PIN = ["pb", "pbc"]
            Tb = v3(pb)[:, :, 15:16].to_broadcast([128, 64, 16])
            P.op("act", lambda e: e.activation(out=dch[:, :].rearrange("p (j o) -> p j o", o=1), in_=v3(pb)[:, :, 15:16], func=AF.Exp),
                 reads=PIN, writes=["dch"])
            if dirn == 0:
                bsrc, bkey = pb, PIN
            else:
                P.op("dve", lambda e: e.tensor_tensor(out=v3(pa), in0=Tb, in1=v3(pb), op=ALU.subtract),
                     reads=PIN + ["pa", "pac"], writes=["pa", "pac"])
                P.op("dve", lambda e: e.tensor_tensor(out=pa[:, :], in0=pa[:, :], in1=logf[:, :], op=ALU.add),
                     reads=["pa", "pac", ("logf", 0), ("logf", 1)], writes=["pa", "pac"])
                bsrc, bkey = pa, ["pa", "pac"]
            if final:
                P.op("act", lambda e, bsrc=bsrc: e.activation(out=ex[0][:, :], in_=bsrc[:, :], func=AF.Exp),
                     reads=bkey, writes=["ex0"])
                P.op("dve", lambda e: e.tensor_tensor(out=Qd[:, :], in0=qf[:, :], in1=ex[0][:, :], op=ALU.mult),
                     reads=["ex0", ("qf", 0), ("qf", 1)], writes=["Qd"])
                P.op("act", lambda e, bsrc=bsrc: e.activation(out=ex[1][:, :], in_=bsrc[:, :], func=AF.Exp, scale=-1.0),
                     reads=bkey, writes=["ex1"])
                P.op("dve", lambda e: e.tensor_tensor(out=Kd[:, :], in0=kk[:, :], in1=ex[1][:, :], op=ALU.mult),
                     reads=["ex1", ("kk", 0), ("kk", 1)], writes=["Kd"])
            if dirn == 0:
                P.op("dve", lambda e: e.tensor_tensor(out=v3(pa), in0=Tb, in1=v3(pb), op=ALU.subtract),
                     reads=PIN + ["pa", "pac"], writes=["pa", "pac"])
                esrc, ekey = pa, ["pa", "pac"]
            else:
                P.op("dve", lambda e: e.tensor_tensor(out=pb[:, :], in0=pb[:, :], in1=logf[:, :], op=ALU.subtract),
                     reads=PIN + [("logf", 0), ("logf", 1), "dch"] + bkey, writes=PIN)
                esrc, ekey = pb, PIN
            P.op("act", lambda e, esrc=esrc: e.activation(out=ex[0][:, :], in_=esrc[:, :], func=AF.Exp),
                 reads=ekey + ["Qd"], writes=["ex0"])
            P.op("dve", lambda e: e.tensor_tensor(out=Ke[:, :], in0=kk[:, :], in1=ex[0][:, :], op=ALU.mult),
                 reads=["ex0", ("kk", 0), ("kk", 1)], writes=["Ke"])
            for tile in range(8):
                reg = tile % 2
                pst = C.psum[7][:, :].bitcast(BF16)[:, reg * 128:(reg + 1) * 128]
                P.op("pe", lambda e, tile=tile, pst=pst: e.transpose(pst, Ke[:, tile * 128:(tile + 1) * 128], identb[:, :]),
                     reads=["Ke", "identb"], writes=[("ps7", reg)])
                P.op("act", lambda e, tile=tile, pst=pst: e.activation(out=KeT[:, tile, :], in_=pst, func=AF.Copy),
                     reads=[("ps7", reg)], writes=[("KeT", tile)])
            cur = 0
            if final and not os.environ.get("KB_NOS0"):
                P.op("pool", lambda e: e.memset(S[0][:, :], 0.0), writes=["S0"])
                order = range(8) if dirn == 0 else range(7, -1, -1)
                mo = O_CMASK + (0 if dirn == 0 else 16)
                for cc in order:
                    P.op("dve", lambda e, cc=cc, mo=mo, dirn=dirn: e.tensor_scalar(
                        out=stmp[:, :], in0=bsin[:, cc, dirn, 0:128], scalar1=cst[:, mo + cc:mo + cc + 1], scalar2=None, op0=ALU.mult),
                        reads=["bsin", "cst"], writes=["stmp"])
                    P.op("dve", lambda e, cc=cc, mo=mo, dirn=dirn: e.tensor_scalar(
                        out=deff[:, 0:1], in0=bsin[:, cc, dirn, 128:129], scalar1=cst[:, mo + cc:mo + cc + 1],
                        scalar2=cst[:, mo + 8 + cc:mo + 8 + cc + 1], op0=ALU.mult, op1=ALU.add),
                        reads=["bsin", "cst"], writes=["deff"])
                    P.op("dve", lambda e: e.scalar_tensor_tensor(out=S[0][:, :], in0=S[0][:, :], scalar=deff[:, 0:1], in1=stmp[:, :],
                                                                 op0=ALU.mult, op1=ALU.add),
                         reads=["S0", "deff", "stmp"], writes=["S0"])
            else:
                P.op("pool", lambda e: e.memset(S[0][:, :], 0.0), writes=["S0"])
            tiles = range(8) if dirn == 0 else range(7, -1, -1)
            for ti, tile in enumerate(tiles):
                vb = Vbd[ti % 2]
                if final:
                    for j in range(8):
                        P.op("act", lambda e, vb=vb, j=j, tile=tile: e.activation(
                            out=vb[:, j, :], in_=Vt[:, tile, :], func=AF.Copy, scale=cmk[:, j:j + 1]),
                            reads=["Vt", "cB"], writes=[("Vbd", ti % 2, j)])
                else:
                    P.op("dve", lambda e, vb=vb, tile=tile: e.tensor_tensor(
                        out=vb[:, :, :], in0=Vt[:, tile, :].unsqueeze(1).to_broadcast([128, 8, 128]),
                        in1=cmk.unsqueeze(2).to_broadcast([128, 8, 128]), op=ALU.mult),
                        reads=["Vt", "cB"], writes=[("Vbd", ti % 2, j) for j in range(8)])
                for hb in range(2):
                    P.op("pe", lambda e, vb=vb, hb=hb, tile=tile: e.matmul(
                        C.psum[4 + hb][:, :], lhsT=KeT[:, tile, :],
                        rhs=vb[:, 4 * hb:4 * hb + 4, :], start=True, stop=True),
                        reads=[("KeT", tile)] + [("Vbd", ti % 2, j) for j in range(4 * hb, 4 * hb + 4)], writes=["ps%d" % (4 + hb)])
                sb_ = Sbf[ti % 2]
                js = range(8) if dirn == 0 else range(7, -1, -1)
                for j in js:
                    gj = tile * 8 + j
                    if final:
                        P.op("act", lambda e, sb_=sb_, j=j, cur=cur: e.activation(out=sb_[:, j, :], in_=S[cur][:, :], func=AF.Copy),
                             reads=["S%d" % cur], writes=[("Sbf", ti % 2, j)])
                    kvp = C.psum[4 + j // 4][:, (j % 4) * 128:(j % 4 + 1) * 128]
                    P.op("dve", lambda e, cur=cur, gj=gj, kvp=kvp: e.scalar_tensor_tensor(
                        out=S[1 - cur][:, :], in0=S[cur][:, :], scalar=dch[:, gj:gj + 1], in1=kvp, op0=ALU.mult, op1=ALU.add),
                        reads=["S%d" % cur, "dch", "ps%d" % (4 + j // 4)], writes=["S%d" % (1 - cur)])
                    cur = 1 - cur
                if final and not os.environ.get("KB_NOINTRA"):
                    tsl = slice(tile * 128, (tile + 1) * 128)
                    pa_ = C.psum[6][:, 0:128]
                    po_ = C.psum[6][:, 256:384]
                    am = attm[ti % 2]
                    P.op("pe", lambda e, tsl=tsl, pa_=pa_: e.matmul(pa_, lhsT=Kd[:, tsl], rhs=Qd[:, tsl], start=True, stop=True),
                         reads=["Kd", "Qd"], writes=[("ps6", 0)])
                    P.op("dve", lambda e, pa_=pa_, am=am, dirn=dirn: e.tensor_tensor(out=am[:, :], in0=pa_, in1=maskT[dirn], op=ALU.mult),
                         reads=[("ps6", 0), "cB"], writes=[("attm", ti % 2)])
                    qm = Qdm[ti % 2]
                    P.op("dve", lambda e, qm=qm, tsl=tsl: e.tensor_tensor(
                        out=qm[:, :, :], in0=Qd[:, tsl].unsqueeze(1).to_broadcast([128, 8, 128]), in1=cmq[:, :, :], op=ALU.mult),
                        reads=["Qd", "cmq"], writes=[("Qdm", ti % 2)])
                    P.op("pe", lambda e, po_=po_, am=am, tile=tile: e.matmul(po_, lhsT=Vt[:, tile, :], rhs=am[:, :], start=True, stop=False),
                         reads=["Vt", ("attm", ti % 2)], writes=[("ps6", 1)])
                    for jj, j in enumerate(js):
                        P.op("pe", lambda e, po_=po_, sb_=sb_, j=j, qm=qm, jj=jj: e.matmul(
                            po_, lhsT=sb_[:, j, :], rhs=qm[:, j, :], start=False, stop=(jj == 7)),
                            reads=[("Sbf", ti % 2, j), ("Qdm", ti % 2)], writes=[("ps6", 1)])
                    if dirn == 0:
                        P.op("act", lambda e, tsl=tsl, po_=po_: e.activation(out=osb[:, tsl], in_=po_, func=AF.Copy),
                             reads=[("ps6", 1)], writes=[("osb", tile // 4)])
                    else:
                        P.op("dve", lambda e, tsl=tsl, po_=po_: e.tensor_tensor(out=osb[:, tsl], in0=osb[:, tsl], in1=po_, op=ALU.add),
                             reads=[("ps6", 1), ("osb", tile // 4)], writes=[("osb", tile // 4)])
            if not final:
                P.op("pool", lambda e, cur=cur, hd=hd: e.tensor_copy(out=bst_sb[:, hd, 0:128], in_=S[cur][:, :]),
                     reads=["S%d" % cur], writes=[("bst", hd)])
                P.op("dve", lambda e, hd=hd: e.tensor_reduce(out=bst_sb[:, hd, 128:129], in_=dch[:, :], axis=AX.X, op=ALU.mult),
                     reads=["dch"], writes=[("bstd", hd)])
        if final:
            proj_fm(C, W, X["wfm"][wbase + 4], b0=0)
            for half in range(2):
                ts = slice(half * 512, (half + 1) * 512)
                P.op("act", lambda e, half=half, ts=ts: e.activation(out=gs[:, ts], in_=C.psum[half][:, :], func=AF.Silu),
                     reads=["ps%d" % half], writes=[("gs", half)])
                s = rr(C, "sq", 2)
                sq = C.sq[s]
                P.op("act", lambda e, sq=sq, ts=ts: e.activation(out=sq[:, :], in_=osb[:, ts], func=AF.Square),
                     reads=[("osb", half)], writes=["sq%d" % s])
                P.op("pe", lambda e, sq=sq: e.matmul(C.psum[7][:, :], lhsT=C.ones[:, :], rhs=sq[:, :], start=True, stop=True),
                     reads=["sq%d" % s, "ones"], writes=[("ps7", 0), ("ps7", 1)])
                P.op("dve", lambda e: e.tensor_scalar(out=C.rstd[:, :], in0=C.psum[7][:, :], scalar1=1.0 / 128, scalar2=EPS,
                                                      op0=ALU.mult, op1=ALU.add),
                     reads=[("ps7", 0), ("ps7", 1)], writes=["rstd"])
                P.op("act", lambda e: e.activation(out=C.rstd[:, :], in_=C.rstd[:, :], func=AF.Sqrt), reads=["rstd"], writes=["rstd"])
                P.op("dve", lambda e: e.reciprocal(out=C.rstd[:, :], in_=C.rstd[:, :]), reads=["rstd"], writes=["rstd"])
                P.op("dve", lambda e, ts=ts: e.scalar_tensor_tensor(
                    out=ytmp[:, :], in0=osb[:, ts], scalar=cst[:, O_GNW:O_GNW + 1], in1=C.rstd[:, :], op0=ALU.mult, op1=ALU.mult),
                    reads=[("osb", half), "rstd", "cst"], writes=["ytmp"])
                P.op("dve", lambda e, ts=ts, h=h: e.tensor_tensor(out=C.yB[:, h, ts], in0=ytmp[:, :], in1=gs[:, ts], op=ALU.mult),
                     reads=["ytmp", ("gs", half)], writes=[("yB", h, half)])
    return bst_sb


OFF_QC, OFF_KCL, OFF_VCL, OFF_OV = 8192, 20480, 24576, 28672
OFF_QA, OFF_KA, OFF_VA = 28672, 40960, 46080
OFF_SCR = 69760
VXW = 192


def vx_cols(g):
    base = (g // 2) * VXW
    if g % 2 == 0:
        return base, 128, 64, 0
    return base + 64, 128, 0, 64


def emit_norm_out(C, A, pso, g, dest, esink_col, okeys, dkey, bcb=6):
    P = C.P
    _, M, r, r0 = vx_cols(g)
    rows = slice(r0, r0 + 64)
    rr_ = slice(r, r + 1)
    if esink_col is None:
        P.op("dve", lambda e: e.tensor_copy(out=A.den[rr_, :], in_=pso[rr_, :]), reads=okeys, writes=["den"])
    else:
        P.op("dve", lambda e: e.tensor_scalar(out=A.den[rr_, :], in0=pso[rr_, :], scalar1=C.esink[rr_, esink_col:esink_col + 1],
                                              scalar2=None, op0=ALU.add), reads=okeys + ["der"], writes=["den"])
    P.op("dve", lambda e: e.reciprocal(out=A.rden[rr_, :], in_=A.den[rr_, :]), reads=["den"], writes=["rden"])
    P.op("pe", lambda e: e.matmul(C.psum[bcb][:, :], lhsT=C.ones[rr_, :], rhs=A.rden[rr_, :], start=True, stop=True),
         reads=["rden", "ones"], writes=["ps%d" % bcb])
    P.op("act", lambda e: e.activation(out=A.bcs[rows, :], in_=C.psum[bcb][rows, :], func=AF.Copy), reads=["ps%d" % bcb], writes=["bcs"])
    P.op("dve", lambda e: e.tensor_tensor(out=dest, in0=pso[rows, :], in1=A.bcs[rows, :], op=ALU.mult),
         reads=okeys + ["bcs"], writes=[dkey])


def emit_proj_attn(C, X, W, stage):
    P = C.P
    arB = C.arB
    A = Ctx()
    C.A = A
    A.qC = carve(arB, OFF_QC, 12288, BF16, "p (c t) -> p c t", c=6)
    A.kCl = carve(arB, OFF_KCL, 4096, BF16, "p (c t) -> p c t", c=2)
    A.vCl = carve(arB, OFF_VCL, 4096, BF16, "p (a n) -> p a n", a=8)
    if stage == "S2":
        A.qA = carve(arB, OFF_QA, 12288, BF16, "p (c t) -> p c t", c=6)
        A.kA = carve(arB, OFF_KA, 5120, BF16, "p (c t) -> p c t", c=2)
        A.vA = carve(arB, OFF_VA, 7744, BF16)[:, 0:10 * 2 * VXW].rearrange("p (a n) -> p a n", a=10)
    else:
        A.kAl = carve(arB, OFF_QA, 4096, BF16, "p (c t) -> p c t", c=2)
        A.vAl = carve(arB, OFF_QA + 4096, 4096, BF16, "p (a n) -> p a n", a=8)
    cs = carve(arB, OFF_SCR, 8192, F32, "p (k t) -> p k t", k=2)
    cC = carve(arB, OFF_SCR + 8192, 1024, F32)
    qn = [carve(arB, OFF_SCR + 9216 + i * 2048, 2048, F32) for i in range(2)]
    P.dma("sp", lambda e: e.dma_start(out=cs[:, :, :], in_=X["cs"]), writes=["cs"])
    P.dma("sp", lambda e: e.dma_start(out=cC[:, :], in_=X["cC"]), writes=["cC"])
    Rm, bones = cC[:, 0:128], cC[:, 128:256]
    cst = C.cst
    if stage == "S2":
        P.op("pool", lambda e: e.memset(A.vA[:, :, :], 0.0), writes=["vA"])
        P.op("pool", lambda e: e.memset(A.vA[:, :, 64:66], 1.0), reads=["vA"], writes=["vA"])
        P.op("pool", lambda e: e.memset(A.vA[:, :, VXW + 64:VXW + 66], 1.0), reads=["vA"], writes=["vA"])
        for ci in range(6):
            proj_fm(C, W, X["wfm"][20 + ci], b0=(ci % 2) * 2)
            for half in range(2):
                b = (ci % 2) * 2 + half
                P.op("act", lambda e, ci=ci, half=half, b=b: e.activation(
                    out=A.qA[:, ci, half * 512:(half + 1) * 512], in_=C.psum[b][:, :], func=AF.Copy),
                    reads=["ps%d" % b], writes=[("qA", ci)])
    for pair in range(2):
        proj_fm(C, W, X["wfm"][26 + pair], b0=(pair % 2) * 2)
        for half in range(2):
            b = (pair % 2) * 2 + half
            dst = (A.kA[:, pair, 128 + half * 512:128 + (half + 1) * 512] if stage == "S2"
                   else A.kAl[:, pair, half * 512:(half + 1) * 512])
            P.op("act", lambda e, dst=dst, b=b: e.activation(out=dst, in_=C.psum[b][:, :], func=AF.Copy),
                 reads=["ps%d" % b], writes=[("kA", pair)])
    for pair in range(2):
        proj_tm(C, W, X["wfm"][28 + pair], b0=4)
        for bb in range(2):
            pv = C.psum[4 + bb][:, :].rearrange("p (a n) -> p a n", a=4)
            if stage == "S2":
                o0 = pair * VXW
                P.op("act", lambda e, bb=bb, pv=pv, o0=o0: e.activation(
                    out=A.vA[:, 1 + 4 * bb:5 + 4 * bb, o0:o0 + 64], in_=pv[:, :, 0:64], func=AF.Copy),
                    reads=["ps%d" % (4 + bb), "vA"], writes=[("vAx", pair, bb, 0)])
                P.op("act", lambda e, bb=bb, pv=pv, o0=o0: e.activation(
                    out=A.vA[:, 1 + 4 * bb:5 + 4 * bb, o0 + 128:o0 + 192], in_=pv[:, :, 64:128], func=AF.Copy),
                    reads=["ps%d" % (4 + bb), "vA"], writes=[("vAx", pair, bb, 1)])
            else:
                P.op("act", lambda e, bb=bb, pv=pv, pair=pair: e.activation(
                    out=A.vAl[:, 4 * bb:4 * bb + 4, pair * 128:(pair + 1) * 128], in_=pv, func=AF.Copy),
                    reads=["ps%d" % (4 + bb)], writes=[("vAl", pair, bb)])
    chunks = ([("q", ci) for ci in range(6)] if stage == "S2" else []) + \
             ([("k", pr) for pr in range(2)] if stage == "S1" else [])
    for n_, (kind, idx) in enumerate(chunks):
        b0 = (n_ % 2) * 2
        proj_fm(C, W, X["wfm"][(30 if kind == "q" else 36) + idx], b0=b0)
        wcol = O_QKW + (0 if kind == "q" else 1)
        for half in range(2):
            b = b0 + half
            ts = slice(half * 512, (half + 1) * 512)
            s = rr(C, "sq", 2)
            sq = C.sq[s]
            q_ = qn[half]
            P.op("act", lambda e, sq=sq, b=b: e.activation(out=sq[:, :], in_=C.psum[b][:, :], func=AF.Square),
                 reads=["ps%d" % b], writes=["sq%d" % s])
            P.op("pe", lambda e, sq=sq, half=half: e.matmul(C.psum[4 + half][:, :], lhsT=bones, rhs=sq[:, :], start=True, stop=True),
                 reads=["sq%d" % s, "cC"], writes=["ps%d" % (4 + half)])
            P.op("dve", lambda e, half=half: e.tensor_scalar(out=C.rstd[:, :], in0=C.psum[4 + half][:, :], scalar1=1.0 / 64, scalar2=EPS,
                                                             op0=ALU.mult, op1=ALU.add),
                 reads=["ps%d" % (4 + half)], writes=["rstd"])
            P.op("act", lambda e: e.activation(out=C.rstd[:, :], in_=C.rstd[:, :], func=AF.Sqrt), reads=["rstd"], writes=["rstd"])
            P.op("dve", lambda e: e.reciprocal(out=C.rstd[:, :], in_=C.rstd[:, :]), reads=["rstd"], writes=["rstd"])
            P.op("dve", lambda e, q_=q_, b=b, wcol=wcol: e.scalar_tensor_tensor(
                out=q_[:, :], in0=C.psum[b][:, :], scalar=cst[:, wcol:wcol + 1], in1=C.rstd[:, :], op0=ALU.mult, op1=ALU.mult),
                reads=["ps%d" % b, "rstd", "cst"], writes=[("qn", half)])
            P.op("pe", lambda e, q_=q_, half=half: e.matmul(C.psum[6 + half][:, :], lhsT=Rm, rhs=q_[:, :], start=True, stop=True),
                 reads=[("qn", half), "cC"], writes=["ps%d" % (6 + half)])
            s2 = rr(C, "sq", 2)
            t2 = C.sq[s2]
            P.op("dve", lambda e, t2=t2, half=half, ts=ts: e.tensor_tensor(out=t2[:, :], in0=C.psum[6 + half][:, :], in1=cs[:, 1, ts], op=ALU.mult),
                 reads=["ps%d" % (6 + half), "cs"], writes=["sq%d" % s2])
            P.op("dve", lambda e, q_=q_, ts=ts: e.tensor_tensor(out=q_[:, :], in0=q_[:, :], in1=cs[:, 0, ts], op=ALU.mult),
                 reads=[("qn", half), "cs"], writes=[("qn", half)])
            dst = A.qC[:, idx, ts] if kind == "q" else A.kCl[:, idx, ts]
            P.op("dve", lambda e, q_=q_, t2=t2, dst=dst: e.tensor_tensor(out=dst, in0=q_[:, :], in1=t2[:, :], op=ALU.add),
                 reads=[("qn", half), "sq%d" % s2], writes=[("qkC", kind, idx, half)])
    if stage == "S1":
        for pair in range(2):
            proj_tm(C, W, X["wfm"][38 + pair], b0=4)
            for bb in range(2):
                pv = C.psum[4 + bb][:, :].rearrange("p (a n) -> p a n", a=4)
                P.op("act", lambda e, bb=bb, pv=pv, pair=pair: e.activation(
                    out=A.vCl[:, 4 * bb:4 * bb + 4, pair * 128:(pair + 1) * 128], in_=pv, func=AF.Copy),
                    reads=["ps%d" % (4 + bb)], writes=[("vCl", pair, bb)])


def emit_attn_scratch(C):
    A = C.A
    arB = C.arB
    o = OFF_SCR
    A.bias = [carve(arB, o + i * 1536, 1536, F32) for i in range(2)]
    o += 3072
    A.tA = [carve(arB, o + i * 1536, 1536, F32) for i in range(2)]
    o += 3072
    A.den = carve(arB, o, 2048, F32)
    A.rden = carve(arB, o + 2048, 2048, F32)
    A.bcs = carve(arB, o + 4096, 2048, F32)
    o += 6144
    A.PT = [carve(arB, o + i * 1024, 1024, BF16) for i in range(4)]
    A.APT = [carve(arB, o + i * 768, 768, BF16) for i in range(10)]
    A.PT7 = [carve(arB, o + i * 1024, 1024, BF16) for i in range(7)]
    assert o + 7680 <= 90112
    A.yA = carve(C.arH, 0, 12288, BF16, "p (c t) -> p c t", c=6)
    A.yC = carve(C.arH, 12288, 12288, BF16, "p (c t) -> p c t", c=6)


def emit_attn_A(C, X):
    P, A = C.P, C.A
    cst = C.cst
    for side in range(2):
        col = 0 if side == 0 else 1152
        P.dma("sp", lambda e, side=side, col=col: e.dma_start(out=A.kA[:, :, col:col + 128], in_=X["kA_halo"][:, :, side, :]),
              writes=[("kAh", side)])
        t = 0 if side == 0 else 9
        P.dma("sp", lambda e, side=side, t=t: e.dma_start(out=A.vA[:, t, :], in_=X["vA_halo"][:, side, :]),
              reads=["vA"], writes=[("vAh", side)])
        P.op("pool", lambda e, side=side, t=t: e.tensor_scalar(
            out=A.vA[:, t, :], in0=A.vA[:, t, :], scalar1=cst[:, O_FLAGS + side:O_FLAGS + side + 1], scalar2=None, op0=ALU.mult),
            reads=[("vAh", side), "vA", "cst"], writes=[("vAt", side)])
    vkeys = ["vA", ("vAt", 0), ("vAt", 1)] + [("vAx", p_, b_, x_) for p_ in range(2) for b_ in range(2) for x_ in range(2)]
    for g in range(4):
        pair = g // 2
        o0, M, r, r0 = vx_cols(g)
        rows = slice(r0, r0 + 64)
        for r3 in range(3):
            h = 3 * g + r3
            ci = pair * 3 + r3
            bs = rr(C, "biasA", 2)
            P.dma("sp", lambda e, bs=bs, h=h: e.dma_start(out=A.bias[bs][:, :], in_=X["biasT"][h]), writes=[("biasA", bs)])
            for kt in range(10):
                n_lo, n_hi = max(kt - 2, 0), min(kt, 7)
                lo = (n_lo - (kt - 2)) * 128
                ncol = (n_hi - n_lo + 1) * 128
                sb_ = rr(C, "psA", 3)
                pss = C.psum[sb_][:, 0:ncol]
                P.op("pe", lambda e, pss=pss, kt=kt, n_lo=n_lo, ncol=ncol, ci=ci, pair=pair, rows=rows: e.matmul(
                    pss, lhsT=A.kA[rows, pair, kt * 128:(kt + 1) * 128], rhs=A.qA[rows, ci, n_lo * 128:n_lo * 128 + ncol],
                    start=True, stop=True),
                    reads=[("kA", pair), ("kAh", 0), ("kAh", 1), ("qA", ci)], writes=["ps%d" % sb_])
                tb = rr(C, "tA", 2)
                tA = A.tA[tb][:, 0:ncol]
                P.op("dve", lambda e, tA=tA, pss=pss, bs=bs, lo=lo, ncol=ncol: e.scalar_tensor_tensor(
                    out=tA, in0=pss, scalar=0.125, in1=A.bias[bs][:, lo:lo + ncol], op0=ALU.mult, op1=ALU.add),
                    reads=["ps%d" % sb_, ("biasA", bs)], writes=[("tA", tb)])
                PT = A.APT[kt][:, 0:ncol]
                P.op("act", lambda e, PT=PT, tA=tA: e.activation(out=PT, in_=tA, func=AF.Exp), reads=[("tA", tb)], writes=[("APT", kt)])
            for n in range(8):
                ob = 4 + n // 4
                for kt in (n, n + 1, n + 2):
                    n_lo = max(kt - 2, 0)
                    P.op("pe", lambda e, ob=ob, n=n, kt=kt, n_lo=n_lo, o0=o0, M=M: e.matmul(
                        C.psum[ob][0:M, (n % 4) * 128:(n % 4 + 1) * 128], lhsT=A.vA[:, kt, o0:o0 + M],
                        rhs=A.APT[kt][:, (n - n_lo) * 128:(n - n_lo + 1) * 128], start=(kt == n), stop=(kt == n + 2)),
                        reads=vkeys + [("APT", kt)], writes=["ps%d" % ob])
            for half in range(2):
                emit_norm_out(C, A, C.psum[4 + half], g, A.yA[rows, ci, half * 512:(half + 1) * 512], h,
                              ["ps%d" % (4 + half)], ("yA", ci, g % 2, half))


def emit_attn_C(C, X):
    P, A = C.P, C.A
    arB = C.arB
    Kall = carve(arB, OFF_OV, 16384, BF16)
    Vx = carve(arB, OFF_OV + 16384, 64 * VXW * 2, BF16, "p (a n) -> p a n", a=64)
    LA = 2
    qz = [[carve(arB, OFF_KCL + (par * 3 + r3) * 1024, 1024, BF16) for r3 in range(3)] for par in range(2)]
    for r3 in range(3):
        P.op("pool", lambda e, r3=r3: e.memset(qz[0][r3][64:128, :], 0.0), writes=[("qz", 0, r3)])
        P.op("pool", lambda e, r3=r3: e.memset(qz[1][r3][0:64, :], 0.0), writes=[("qz", 1, r3)])
    for pair in range(2):
        for q4 in range(4):
            P.dma("sp", lambda e, pair=pair, q4=q4: e.dma_start(
                out=Kall[:, q4 * 2048:(q4 + 1) * 2048], in_=X["kC_all"][:, pair, q4 * 2048:(q4 + 1) * 2048]),
                writes=[("Kall", q4)])
        for q4 in range(4):
            P.dma("sp", lambda e, pair=pair, q4=q4: e.dma_start(
                out=Vx[:, q4 * 16:(q4 + 1) * 16, :], in_=X["vC_all"][:, pair, q4 * 16:(q4 + 1) * 16, :]),
                writes=[("Vxd", 0, q4), ("Vxd", 1, q4)])
        kkeys = [("Kall", q4) for q4 in range(4)]
        vkeys = [("Vxd", gh, q4) for gh in range(2) for q4 in range(4)]
        for gh in range(2):
            g = 2 * pair + gh
            o0, M, r, r0 = vx_cols(g)
            o0 -= pair * VXW
            rows = slice(r0, r0 + 64)
            for qh in range(2):
                qs = slice(qh * 512, (qh + 1) * 512)
                pend = []
                for r3 in range(3):
                    ci = pair * 3 + r3
                    P.op("pool", lambda e, r3=r3, ci=ci, rows=rows, qs=qs, gh=gh: e.tensor_copy(out=qz[gh][r3][rows, :], in_=A.qC[rows, ci, qs]),
                         reads=[("qkC", "q", ci, qh)], writes=[("qz", gh, r3)])

                def qk3(kt):
                    ent = []
                    for r3 in range(3):
                        ci = pair * 3 + r3
                        sb_ = rr(C, "psC", 4)
                        P.op("pe", lambda e, sb_=sb_, kt=kt, r3=r3, gh=gh: e.matmul(
                            C.psum[sb_][:, :], lhsT=Kall[:, kt * 128:(kt + 1) * 128], rhs=qz[gh][r3][:, :], start=True, stop=True),
                            reads=kkeys + [("qz", gh, r3)], writes=["ps%d" % sb_])
                        pb_ = rr(C, "PT7", 7)
                        P.op("act", lambda e, sb_=sb_, pb_=pb_: e.activation(out=A.PT7[pb_][:, :], in_=C.psum[sb_][:, :], func=AF.Exp, scale=0.125),
                             reads=["ps%d" % sb_], writes=[("PT7", pb_)])
                        ent.append(pb_)
                    pend.append((kt, ent))

                def pv3():
                    kt, ent = pend.pop(0)
                    for r3 in range(3):
                        P.op("pe", lambda e, kt=kt, pb_=ent[r3], ob=4 + r3, M=M, o0=o0: e.matmul(
                            C.psum[ob][0:M, :], lhsT=Vx[:, kt, o0:o0 + M], rhs=A.PT7[pb_][:, :], start=(kt == 0), stop=(kt == 63)),
                            reads=vkeys + [("PT7", x) for x in (ent if r3 == 0 else [ent[r3]])], writes=["ps%d" % (4 + r3)])

                for kt in range(64):
                    qk3(kt)
                    if kt >= 1:
                        pv3()
                while pend:
                    pv3()
                for r3 in range(3):
                    ci = pair * 3 + r3
                    emit_norm_out(C, A, C.psum[4 + r3], g, A.yC[rows, ci, qs], None, ["ps%d" % (4 + r3)], ("yC", ci, gh, qh), bcb=7)


def emit_wout(C, X, W):
    P, A = C.P, C.A
    yT = carve(C.arB, 16384, 32768, F32, "p (c t) -> p c t", c=DC)
    ych = [(A.yA[:, k, :], None) for k in range(6)] + [(C.yB[:, k, :], None) for k in range(4)] + \
          [(A.yC[:, k, :], None) for k in range(6)]
    nw = C.nw[:, 3, :]
    for half in range(2):
        ts = slice(half * 512, (half + 1) * 512)
        for m in range(DC):
            b = m % 2
            t, key = W.load(X["wout"][m])
            for k in range(DC):
                P.op("pe", lambda e, t=t, k=k, b=b, ts=ts: e.matmul(
                    C.psum[b][:, :], lhsT=t[:, k, :], rhs=ych[k][0][:, ts], start=(k == 0), stop=(k == DC - 1)),
                    reads=[key], writes=["ps%d" % b])
            P.op("act", lambda e, m=m, b=b: e.activation(out=yT[:, m, :], in_=C.psum[b][:, :], func=AF.Copy),
                 reads=["ps%d" % b], writes=[("yT", m)])
        emit_rms_stats(C, lambda c: yT[:, c, :], lambda c: [("yT", c)], half, 7)
        for c in range(DC):
            P.op("dve", lambda e, c=c: e.scalar_tensor_tensor(
                out=yT[:, c, :], in0=yT[:, c, :], scalar=nw[:, c:c + 1], in1=C.rstd[:, :], op0=ALU.mult, op1=ALU.mult),
                reads=[("yT", c), "rstd", "cst"], writes=[("yT", c)])
            P.op("pool", lambda e, c=c, ts=ts: e.tensor_tensor(out=C.xT[:, c, ts], in0=yT[:, c, :], in1=C.xT[:, c, ts], op=ALU.add),
                 reads=[("yT", c), ("xT", c, half)], writes=[("xT", c, half)])


def _din(nc, name, shape, dt=F32):
    return nc.dram_tensor(name, list(shape), dt, kind="ExternalInput").ap()


def _dout(nc, name, shape, dt=F32):
    return nc.dram_tensor(name, list(shape), dt, kind="ExternalOutput").ap()


def build_stage(stage):
    nc = bass.Bass("TRN2", target_bir_lowering=False)
    X = {}
    X["xin"] = _din(nc, "xin", [D, NT])
    X["wgu"] = _din(nc, "wgu", [FC, 128, 2, DC, 128])
    X["wd"] = _din(nc, "wd", [DC, 128, FC, 128])
    X["wfm"] = _din(nc, "wfm", [NCH_IN, 128, DC, 128])
    X["cst"] = _din(nc, "cst", [128, NCST])
    X["cB"] = _din(nc, "cB", [128, NCB])
    X["cC"] = _din(nc, "cC", [128, NCC])
    X["cs"] = _din(nc, "cs", [128, 2, NT])
    X["cmq"] = _din(nc, "cmq", [128, 8, 128], BF16)
    X["xout"] = _dout(nc, "xout", [D, NT])
    if stage == "S1":
        X["bst_out"] = _dout(nc, "bst_out", [128, 8, 129])
        X["kA_out"] = _dout(nc, "kA_out", [128, 2, NT], BF16)
        X["vA_out"] = _dout(nc, "vA_out", [128, 8, 256], BF16)
        X["kC_out"] = _dout(nc, "kC_out", [128, 2, NT], BF16)
        X["vC_out"] = _dout(nc, "vC_out", [128, 8, 256], BF16)
    else:
        X["wout"] = _din(nc, "wout", [DC, 128, DC, 128])
        X["bst_all"] = _din(nc, "bst_all", [4, 128, NCORES, 2, 129])
        X["kC_all"] = _din(nc, "kC_all", [128, 2, SEQ], BF16)
        X["vC_all"] = _din(nc, "vC_all", [128, 2, SEQ // 128, VXW], BF16)
        X["kA_halo"] = _din(nc, "kA_halo", [128, 2, 2, 128], BF16)
        X["vA_halo"] = _din(nc, "vA_halo", [128, 2, 2 * VXW], BF16)
        X["biasT"] = _din(nc, "biasT", [12, 128, 384])
    with contextlib.ExitStack() as stack:
        P = Prog(nc)
        C = alloc_common(nc, stack, P)
        F = alloc_ffn(C, stack)
        emit_consts(C, X)
        for c in range(DC):
            P.dma("sp", lambda e, c=c: e.dma_start(out=C.xT[:, c, :], in_=X["xin"][c * 128:(c + 1) * 128, :]),
                  writes=[("xT", c, 0), ("xT", c, 1)])
        W = WStream(C)
        if stage == "S1":
            emit_ffn(C, X["wgu"], X["wd"], C.nw[:, 0, :], C.nw[:, 1, :], "cst", F)
            P.barrier()
            emit_prenorm(C, C.nw[:, 2, :], "cst")
            bst_sb = emit_B(C, X, W, final=False)
            P.dma("sp", lambda e: e.dma_start(out=X["bst_out"], in_=bst_sb[:, :, :]),
                  reads=[("bst", k) for k in range(8)] + [("bstd", k) for k in range(8)])
            P.barrier()
            emit_proj_attn(C, X, W, "S1")
            A = C.A
            P.dma("sp", lambda e: e.dma_start(out=X["kA_out"], in_=A.kAl[:, :, :]), reads=[("kA", 0), ("kA", 1)])
            P.dma("sp", lambda e: e.dma_start(out=X["vA_out"], in_=A.vAl[:, :, :]),
                  reads=[("vAl", p_, b_) for p_ in range(2) for b_ in range(2)])
            P.dma("sp", lambda e: e.dma_start(out=X["kC_out"], in_=A.kCl[:, :, :]),
                  reads=[("qkC", "k", i_, h_) for i_ in range(2) for h_ in range(2)])
            P.dma("sp", lambda e: e.dma_start(out=X["vC_out"], in_=A.vCl[:, :, :]),
                  reads=[("vCl", p_, b_) for p_ in range(2) for b_ in range(2)])
        else:
            import os
            stop = os.environ.get("KSTOP", "")
            phases = ["B", "proj", "A", "C", "dbg", "wout", "ffn"]
            upto = phases.index(stop) if stop in phases else len(phases) - 1
            skip = os.environ.get("KSKIP", "").split(",")
            emit_prenorm(C, C.nw[:, 2, :], "cst")
            if "B" not in skip:
                emit_B(C, X, W, final=True)
            P.barrier()
            if upto >= 1:
                emit_proj_attn(C, X, W, "S2")
                P.barrier()
                emit_attn_scratch(C)
            if upto >= 2 and "A" not in skip:
                emit_attn_A(C, X)
                P.barrier()
            if upto >= 3 and "C" not in skip:
                emit_attn_C(C, X)
                P.barrier()
            if upto >= 4 and os.environ.get("KDEBUG"):
                X["dbgH"] = _dout(nc, "dbgH", [128, 8192])
                X["dbgB"] = _dout(nc, "dbgB", [128, 2048])
                P.dma("sp", lambda e: e.dma_start(out=X["dbgH"], in_=C.arH[:, :]))
                P.dma("sp", lambda e: e.dma_start(out=X["dbgB"], in_=C.arB[:, 0:2048]))
                P.barrier()
            if upto >= 5:
                emit_wout(C, X, W)
                P.barrier()
            if upto >= 6:
                emit_ffn(C, X["wgu"], X["wd"], C.nw[:, 4, :], C.nw[:, 5, :], "cst", F)
        for c in range(DC):
            P.dma("sp", lambda e, c=c: e.dma_start(out=X["xout"][c * 128:(c + 1) * 128, :], in_=C.xT[:, c, :]),
                  reads=[("xT", c, 0), ("xT", c, 1)])
        P.emit(stack)
        print("stage", stage, "stats", P.stats)
    return nc


import ml_dtypes

OFFS = dict(aq=0, ak=768, av=1024, bq=1280, bzf=1792, bzb=2304, bi=2816, bg=3328, cq=3840, ck=4608, cv=4864)


def _qcols(base, ci):
    pair, r3 = ci // 3, ci % 3
    lo = base + (3 * (2 * pair) + r3) * 64
    hi = base + (3 * (2 * pair + 1) + r3) * 64
    return list(range(lo, lo + 64)) + list(range(hi, hi + 64))


def in_col_index():
    cols = []
    for h in range(4):
        for nm in ("bi", "bq", "bzf", "bzb", "bg"):
            cols += list(range(OFFS[nm] + h * 128, OFFS[nm] + (h + 1) * 128))
    for pre in ("a", "c"):
        for ci in range(6):
            cols += _qcols(OFFS[pre + "q"], ci)
        for pair in range(2):
            cols += list(range(OFFS[pre + "k"] + pair * 128, OFFS[pre + "k"] + (pair + 1) * 128))
        for pair in range(2):
            cols += list(range(OFFS[pre + "v"] + pair * 128, OFFS[pre + "v"] + (pair + 1) * 128))
    return np.asarray(cols)


def out_row_index():
    rows = []
    for ci in range(6):
        rows += _qcols(0, ci)
    rows += list(range(768, 1280))
    for ci in range(6):
        rows += _qcols(1280, ci)
    return np.asarray(rows)


def lay_wfm(w_in):
    w = w_in[:, in_col_index()]
    a = w.reshape(DC, 128, NCH_IN, 128)
    return np.ascontiguousarray(a.transpose(2, 1, 0, 3))


def lay_wout(w_out):
    w = w_out[out_row_index(), :]
    a = w.reshape(DC, 128, DC, 128)
    return np.ascontiguousarray(a.transpose(2, 1, 0, 3))


def t5_bucket_np(rel):
    nb, max_exact = 16, 8
    n = np.abs(rel)
    nf = np.maximum(n, 1).astype(np.float32)
    val = np.log(nf / np.float32(max_exact)) / np.float32(np.log(128 / 8)) * np.float32(nb - max_exact)
    large = max_exact + val.astype(np.int32)
    large = np.minimum(large, nb - 1)
    return np.where(rel > 0, nb, 0) + np.where(n < max_exact, n, large)


def make_biasT(rel_bias):
    kj = np.arange(128)[:, None, None]
    dd = np.arange(3)[None, :, None]
    qi = np.arange(128)[None, None, :]
    rel = (1 - dd) * 128 + kj - qi
    valid = np.abs(rel) <= 128
    bk = t5_bucket_np(rel)
    tab = rel_bias.astype(np.float32)[bk]
    tab = np.where(valid[..., None], tab, np.float32(-30000.0))
    return np.ascontiguousarray(tab.transpose(3, 0, 1, 2).reshape(12, 128, 384)).astype(np.float32)


def make_consts():
    s = np.arange(128)[:, None]
    t = np.arange(128)[None, :]
    same = (s // 16) == (t // 16)
    cB = np.zeros((128, NCB), np.float32)
    cB[:, 0:128] = (same & (s <= t))
    cB[:, 128:256] = (same & (s >= t))
    cB[:, 256:384] = np.eye(128)
    cB[:, 384:392] = (np.arange(128)[:, None] // 16) == np.arange(8)[None, :]
    global CMQ
    CMQ = np.ascontiguousarray(np.broadcast_to(((np.arange(128)[None, :] // 16) == np.arange(8)[:, None]).astype(np.float32)[None], (128, 8, 128))).astype(ml_dtypes.bfloat16)
    cC = np.zeros((128, NCC), np.float32)
    for i in range(64):
        cC[2 * i + 1, 2 * i] = -1.0
        cC[2 * i, 2 * i + 1] = 1.0
    cC[:, 128:256] = (s // 64) == (t // 64)
    half = 32
    inv = (1.0 / (np.float32(10000.0) ** (np.arange(0, half, 2, dtype=np.float32) / np.float32(half)))).astype(np.float32)
    pos = np.arange(SEQ)
    row_ang = (pos // 64).astype(np.float32)[:, None] * inv[None, :]
    col_ang = (pos % 64).astype(np.float32)[:, None] * inv[None, :]
    ang = np.concatenate([row_ang, col_ang], axis=1).astype(np.float32)
    idx = (np.arange(128) % 64) // 2
    cs = np.stack([np.cos(ang)[:, idx].T, np.sin(ang)[:, idx].T], axis=1).astype(np.float32)
    cs_cores = [np.ascontiguousarray(cs[:, :, c * NT:(c + 1) * NT]) for c in range(NCORES)]
    return cB, cC, cs_cores


def make_cst(l, core, norm_w, hgrn_lb, qk_norm_w, hgrn_norm_w, sink_logits):
    cst = np.zeros((128, NCST), np.float32)
    cst[:, O_NW:O_NW + 96] = norm_w[l].reshape(6, DC, 128).transpose(2, 0, 1).reshape(128, 96)
    cst[:, O_LBR:O_LBR + 32] = hgrn_lb.reshape(4, 2, 4, 128).transpose(3, 0, 1, 2).reshape(128, 32)
    for l2 in range(4):
        cst[:, O_LMASK + l2] = 1.0 if (1 <= l2 <= l) else 0.0
    cst[:, O_QKW] = np.tile(qk_norm_w[l, 0], 2)
    cst[:, O_QKW + 1] = np.tile(qk_norm_w[l, 1], 2)
    cst[:, O_GNW] = hgrn_norm_w[l]
    cst[:, O_SINK:O_SINK + 12] = sink_logits[l][None, :]
    for c2 in range(NCORES):
        mf = 1.0 if c2 < core else 0.0
        mb = 1.0 if c2 > core else 0.0
        cst[:, O_CMASK + c2] = mf
        cst[:, O_CMASK + 8 + c2] = 1.0 - mf
        cst[:, O_CMASK + 16 + c2] = mb
        cst[:, O_CMASK + 24 + c2] = 1.0 - mb
    cst[:, O_FLAGS] = 1.0 if core > 0 else 0.0
    cst[:, O_FLAGS + 1] = 1.0 if core < NCORES - 1 else 0.0
    return cst


def v_ext(v):
    out = np.zeros(v.shape[:-1] + (2 * VXW,), v.dtype)
    for pair in range(2):
        b = pair * VXW
        out[..., b:b + 64] = v[..., (2 * pair) * 64:(2 * pair + 1) * 64]
        out[..., b + 64] = 1
        out[..., b + 65] = 1
        out[..., b + 128:b + 192] = v[..., (2 * pair + 1) * 64:(2 * pair + 2) * 64]
    return out


def exchange_layout(r1):
    b = np.stack([r["bst_out"] for r in r1], axis=0)
    b = b.reshape(NCORES, 128, 4, 2, 129).transpose(2, 1, 0, 3, 4)
    kC_all = np.ascontiguousarray(np.concatenate([r["kC_out"] for r in r1], axis=2))
    v = np.concatenate([r["vC_out"] for r in r1], axis=1)
    ve = v_ext(v).reshape(128, SEQ // 128, 2, VXW).transpose(0, 2, 1, 3)
    return np.ascontiguousarray(b), kC_all, np.ascontiguousarray(ve)


def _run(name, in_maps):
    import os
    if name not in _PROGS:
        _PROGS[name] = build_stage(name)
    res = run_bass_kernel_spmd(_PROGS[name], in_maps, core_ids=list(range(NCORES)), trace=bool(os.environ.get("KTRACE")))
    if os.environ.get("KTRACE"):
        print("exec_time_ns", name, res.exec_time_ns)
    return res.results


def run_layer(l, xT, inp, consts, debug=None):
    cB, cC, cs_cores = consts
    wfm = lay_wfm(inp["w_in"][l])
    wout = lay_wout(inp["w_out"][l])
    gu1, d1 = lay_gu(inp["ffn1_gate"][l], inp["ffn1_up"][l]), lay_d(inp["ffn1_down"][l])
    csts = [make_cst(l, c, inp["norm_w"], inp["hgrn_lb"], inp["qk_norm_w"], inp["hgrn_norm_w"], inp["sink_logits"])
            for c in range(NCORES)]
    in1 = [dict(xin=xT[c], wgu=gu1, wd=d1, wfm=wfm, cst=csts[c], cB=cB, cC=cC, cs=cs_cores[c], cmq=CMQ) for c in range(NCORES)]
    r1 = _run("S1", in1)
    del gu1, d1, in1
    bst_all, kC_all, vC_all = exchange_layout(r1)
    zk = np.zeros((128, 2, 128), r1[0]["kA_out"].dtype)
    zv = np.zeros((128, 256), r1[0]["vA_out"].dtype)
    gu2, d2 = lay_gu(inp["ffn2_gate"][l], inp["ffn2_up"][l]), lay_d(inp["ffn2_down"][l])
    biasT = make_biasT(inp["rel_bias"])
    in2 = []
    for c in range(NCORES):
        kp = r1[c - 1]["kA_out"][:, :, NT - 128:] if c > 0 else zk
        kn = r1[c + 1]["kA_out"][:, :, :128] if c < NCORES - 1 else zk
        vp = r1[c - 1]["vA_out"][:, 7, :] if c > 0 else zv
        vn = r1[c + 1]["vA_out"][:, 0, :] if c < NCORES - 1 else zv
        in2.append(dict(xin=r1[c]["xout"], wgu=gu2, wd=d2, wfm=wfm, wout=wout, cst=csts[c], cB=cB, cC=cC, cs=cs_cores[c], cmq=CMQ,
                        bst_all=bst_all, kC_all=kC_all, vC_all=vC_all,
                        kA_halo=np.ascontiguousarray(np.stack([kp, kn], axis=2)),
                        vA_halo=np.ascontiguousarray(np.stack([v_ext(vp), v_ext(vn)], axis=1)), biasT=biasT))
    if debug is not None:
        debug["r1"] = r1
    r2 = _run("S2", in2)
    if debug is not None:
        debug["r2"] = r2
    return [r["xout"] for r in r2]


def kernel(**inputs):
    inp = {k: np.asarray(v) for k, v in inputs.items()}
    x = inp["x"][0]
    xT = [np.ascontiguousarray(x[c * NT:(c + 1) * NT].T) for c in range(NCORES)]
    consts = make_consts()
    for l in range(DEPTH):
        xT = run_layer(l, xT, inp, consts)
    y = np.concatenate([o.T for o in xT], axis=0)
    return np.ascontiguousarray(y[None]).astype(np.float32)
```
